# Optimizing a Trainium2 kernel written in Bass

```python
import math
import jax, jax.numpy as jnp
from jax import lax
import numpy as np

D_MODEL = 1024
BATCH = 32
SEQ = 2048
DEPTH = 2
DEC_BATCH = 32
DEC_SEQ = 32
PAST_LEN = 4096

CHUNK = 64
N_META = 16
N_EVEN = (DEPTH + 1) // 2
N_ODD = DEPTH // 2
HD_A = 64
W_A = D_MODEL // 2
H_A = W_A // HD_A
R_W = 64
R_A = 64
R_G = 128
A_COLS = 3 * W_A + R_W + R_A + R_G
HD_B = 64
W_B = D_MODEL // 2
H_B = W_B // HD_B
B_COLS = 3 * W_B + H_B
N_IN_E = A_COLS + B_COLS
Q_BLOCK = 128
FORGET_BIAS_INIT = 2.0
GC = 16
G_C = D_MODEL // GC
P_C = 64
D_FF = ((8 * D_MODEL // 3 + 127) // 128) * 128
CONV_W = 3
EPS = 1e-6
LN_EPS = 64e-5

kernel_name = 'hybrid_rwkv7_fox_s5_streaming_step'


def rms_norm(x, g):
    xf = x.astype(jnp.float32)
    return xf * lax.rsqrt(jnp.mean(xf * xf, axis=-1, keepdims=True) + EPS) * g.astype(jnp.float32)


def wkv7_scan(r, decay, k, v, kk, a, s0):
    def step(s, inp):
        r_t, w_t, k_t, v_t, kk_t, a_t = inp
        sa = jnp.einsum('bhij,bhj->bhi', s, kk_t)
        s = (s * w_t[:, :, None, :] - sa[..., None] * (kk_t * a_t)[:, :, None, :]
             + v_t[..., None] * k_t[:, :, None, :])
        return s, jnp.einsum('bhij,bhj->bhi', s, r_t)
    xs = tuple(jnp.moveaxis(t.astype(jnp.float32), 1, 0) for t in (r, decay, k, v, kk, a))
    s, ys = lax.scan(step, s0.astype(jnp.float32), xs)
    return s, jnp.moveaxis(ys, 0, 1)


def rwkv7_mix(u, shift_prev, wkv0, mu, w0, w2, a0, a2, g2, k_k, k_a, r_k, ln_w, ln_b):
    b, l = u.shape[:2]
    u_prev = jnp.concatenate([shift_prev[:, None].astype(u.dtype), u[:, :-1]], axis=1)
    um = u + (u_prev - u) * mu
    r, k, v, wd, ad, gd = jnp.split(
        um, [W_A, 2 * W_A, 3 * W_A, 3 * W_A + R_W, 3 * W_A + R_W + R_A], axis=-1)
    w_log = -jax.nn.softplus(-(w0 + jnp.tanh(wd) @ w2)) - 0.5
    decay = jnp.exp(-jnp.exp(w_log.astype(jnp.float32)))
    a = jax.nn.sigmoid(a0 + ad @ a2)
    g = jax.nn.sigmoid(gd) @ g2
    heads = lambda t: t.reshape(b, l, H_A, HD_A)
    kk = heads(k * k_k).astype(jnp.float32)
    kk = kk * lax.rsqrt(jnp.sum(kk * kk, axis=-1, keepdims=True) + 1e-12)
    k = k * (1.0 + (a - 1.0) * k_a)
    r, k, v, decay, a = heads(r), heads(k), heads(v), heads(decay), heads(a)
    wkv, y = wkv7_scan(r, decay, k, v, kk, a, wkv0)
    mean = jnp.mean(y, axis=-1, keepdims=True)
    var = jnp.mean(jnp.square(y - mean), axis=-1, keepdims=True)
    y = (y - mean) * lax.rsqrt(var + LN_EPS)
    bonus = jnp.sum(r * k * r_k, axis=-1, keepdims=True) * v
    y = (y.reshape(b, l, W_A) * ln_w + ln_b + bonus.reshape(b, l, W_A)) * g
    return y, u[:, -1], wkv


def fox_attention(q, k, v, logf, past):
    l = q.shape[1]
    if past is None:
        k_all, v_all, lf_all, p = k, v, logf, 0
    else:
        pk, pv, plf = past
        p = pk.shape[1]
        k_all = jnp.concatenate([pk, k], axis=1)
        v_all = jnp.concatenate([pv, v], axis=1)
        lf_all = jnp.concatenate([plf, logf], axis=1)
    c = jnp.transpose(jnp.cumsum(lf_all.astype(jnp.float32), axis=1), (0, 2, 1))
    scale = HD_B ** -0.5
    outs = []
    for qs in range(0, l, Q_BLOCK):
        qe = min(l, qs + Q_BLOCK)
        ke = p + qe
        s = jnp.einsum('bqhd,bkhd->bhqk', q[:, qs:qe], k_all[:, :ke]).astype(jnp.float32) * scale
        s = s + c[:, :, p + qs:p + qe, None] - c[:, :, None, :ke]
        causal = jnp.arange(ke)[None, :] <= (p + jnp.arange(qs, qe))[:, None]
        s = jnp.where(causal, s, -jnp.inf)
        probs = jax.nn.softmax(s, axis=-1).astype(v_all.dtype)
        outs.append(jnp.einsum('bhqk,bkhd->bqhd', probs, v_all[:, :ke]))
    return jnp.concatenate(outs, axis=1)


def fox_mix(u, past, f_bias, q_gain, k_gain):
    b, l = u.shape[:2]
    q, k, v, f = jnp.split(u, [W_B, 2 * W_B, 3 * W_B], axis=-1)
    q = rms_norm(q.reshape(b, l, H_B, HD_B), q_gain)
    k = rms_norm(k.reshape(b, l, H_B, HD_B), k_gain)
    v = v.reshape(b, l, H_B, HD_B)
    logf = jax.nn.log_sigmoid((f + f_bias).astype(jnp.float32))
    y = fox_attention(q, k, v, logf, past)
    return y.reshape(b, l, W_B), k, v, logf


def complex_affine_combine(e1, e2):
    a1r, a1i, b1r, b1i = e1
    a2r, a2i, b2r, b2i = e2
    return (a1r * a2r - a1i * a2i, a1r * a2i + a1i * a2r,
            a2r * b1r - a2i * b1i + b2r, a2r * b1i + a2i * b1r + b2i)


def s5_mix(u, s0_re, s0_im, lam_re, lam_im, log_dt, b_re, b_im, c_re, c_im, d_skip):
    b, l = u.shape[:2]
    ug = u.astype(jnp.float32).reshape(b, l, G_C, GC)
    lam_re = lam_re.astype(jnp.float32)
    lam_im = lam_im.astype(jnp.float32)
    dt = jnp.exp(log_dt.astype(jnp.float32))[:, None]
    mag = jnp.exp(lam_re * dt)
    ab_re = mag * jnp.cos(lam_im * dt)
    ab_im = mag * jnp.sin(lam_im * dt)
    den = lam_re * lam_re + lam_im * lam_im
    f_re = ((ab_re - 1.0) * lam_re + ab_im * lam_im) / den
    f_im = (ab_im * lam_re - (ab_re - 1.0) * lam_im) / den
    bb_re = f_re[..., None] * b_re - f_im[..., None] * b_im
    bb_im = f_re[..., None] * b_im + f_im[..., None] * b_re
    bu_re = jnp.einsum('blgc,gpc->blgp', ug, bb_re)
    bu_im = jnp.einsum('blgc,gpc->blgp', ug, bb_im)
    s0_re = s0_re.astype(jnp.float32)
    s0_im = s0_im.astype(jnp.float32)
    bu_re = bu_re.at[:, 0].add(ab_re * s0_re - ab_im * s0_im)
    bu_im = bu_im.at[:, 0].add(ab_re * s0_im + ab_im * s0_re)
    a_re = jnp.broadcast_to(ab_re, (1, l, G_C, P_C))
    a_im = jnp.broadcast_to(ab_im, (1, l, G_C, P_C))
    _, _, s_re, s_im = lax.associative_scan(complex_affine_combine, (a_re, a_im, bu_re, bu_im), axis=1)
    y = (jnp.einsum('blgp,gcp->blgc', s_re, c_re) - jnp.einsum('blgp,gcp->blgc', s_im, c_im)
         + d_skip * ug)
    return y.reshape(b, l, D_MODEL), s_re[:, -1], s_im[:, -1]


def conv_ffn(h, buf, w_up, conv_w, conv_b, w_down):
    up = h @ w_up
    l = up.shape[1]
    hp = jnp.concatenate([buf.astype(up.dtype), up], axis=1)
    z = conv_b + sum(conv_w[i] * hp[:, i:i + l] for i in range(CONV_W))
    val, gate = jnp.split(z, 2, axis=-1)
    return (jax.nn.silu(gate) * val) @ w_down, hp[:, -(CONV_W - 1):]


def trunk(x, init, prm):
    a_shift, a_wkv, b_k, b_v, b_lf, c_re, c_im, f_conv = [], [], [], [], [], [], [], []
    for layer in range(DEPTH):
        j = layer // 2
        h = rms_norm(x, prm['norm_mix'][layer])
        if layer % 2 == 0:
            proj = h @ prm['w_in_e'][j]
            y_a, sh, wkv = rwkv7_mix(
                proj[..., :A_COLS], init['a_shift'][j], init['a_wkv'][j], prm['a_mu'][j],
                prm['a_w0'][j], prm['a_w2'][j], prm['a_a0'][j], prm['a_a2'][j], prm['a_g2'][j],
                prm['a_kk'][j], prm['a_ka'][j], prm['a_rk'][j], prm['a_ln_w'][j], prm['a_ln_b'][j])
            y_b, k, v, lf = fox_mix(proj[..., A_COLS:], init['b_past'][j], prm['b_fbias'][j],
                                    prm['b_qnorm'][j], prm['b_knorm'][j])
            mix = jnp.concatenate([y_a, y_b], axis=-1) @ prm['w_out_e'][j]
            a_shift.append(sh)
            a_wkv.append(wkv)
            b_k.append(k)
            b_v.append(v)
            b_lf.append(lf)
        else:
            y_c, sre, sim = s5_mix(
                h, init['c_re'][j], init['c_im'][j], prm['c_lam_re'][j], prm['c_lam_im'][j],
                prm['c_log_dt'][j], prm['c_b_re'][j], prm['c_b_im'][j], prm['c_c_re'][j],
                prm['c_c_im'][j], prm['c_d'][j])
            ga, gb = jnp.split(jax.nn.gelu(y_c) @ prm['w_glu'][j], 2, axis=-1)
            mix = ga * jax.nn.sigmoid(gb)
            c_re.append(sre)
            c_im.append(sim)
        x = x + mix.astype(x.dtype)
        f_out, buf = conv_ffn(rms_norm(x, prm['norm_ffn'][layer]), init['ffn_conv'][layer],
                              prm['w_up'][layer], prm['conv_w'][layer], prm['conv_b'][layer],
                              prm['w_down'][layer])
        x = x + f_out.astype(x.dtype)
        f_conv.append(buf)
    st = lambda t: jnp.stack(t, axis=0)
    return x, (st(a_shift), st(a_wkv), st(b_k), st(b_v), st(b_lf), st(c_re), st(c_im), st(f_conv))


def setup_inputs(seed: int = 0) -> dict:
    key = jax.random.key(seed)
    keys = list(jax.random.split(key, 48))

    def nrm(shape, scale):
        return jax.random.normal(keys.pop(), shape, jnp.float32) * scale

    def uni(shape, lo, hi):
        return jax.random.uniform(keys.pop(), shape, jnp.float32, lo, hi)

    return dict(
        x_prompt=nrm((BATCH, SEQ, D_MODEL), 1.0),
        x_sample=nrm((DEC_BATCH, DEC_SEQ, D_MODEL), 1.0),
        state_a_shift=nrm((N_EVEN, DEC_BATCH, A_COLS), 1.0),
        state_a_wkv=nrm((N_EVEN, DEC_BATCH, H_A, HD_A, HD_A), 1.0),
        cache_b_k=nrm((N_EVEN, DEC_BATCH, PAST_LEN, H_B, HD_B), 1.0),
        cache_b_v=nrm((N_EVEN, DEC_BATCH, PAST_LEN, H_B, HD_B), 1.0),
        cache_b_logf=jax.nn.log_sigmoid(nrm((N_EVEN, DEC_BATCH, PAST_LEN, H_B), 1.0) + FORGET_BIAS_INIT),
        state_c_re=nrm((N_ODD, DEC_BATCH, G_C, P_C), 1.0),
        state_c_im=nrm((N_ODD, DEC_BATCH, G_C, P_C), 1.0),
        state_ffn_conv=nrm((DEPTH, DEC_BATCH, CONV_W - 1, 2 * D_FF), 1.0),
        meta=nrm((N_META, D_MODEL), 1.0),
        norm_mix=1.0 + nrm((DEPTH, D_MODEL), 0.02),
        norm_ffn=1.0 + nrm((DEPTH, D_MODEL), 0.02),
        w_in_e=nrm((N_EVEN, D_MODEL, N_IN_E), D_MODEL ** -0.5),
        a_mu=uni((N_EVEN, A_COLS), 0.0, 1.0),
        a_w0=nrm((N_EVEN, W_A), 0.5) - 1.0,
        a_w2=nrm((N_EVEN, R_W, W_A), 0.1 * R_W ** -0.5),
        a_a0=nrm((N_EVEN, W_A), 0.5),
        a_a2=nrm((N_EVEN, R_A, W_A), 0.1 * R_A ** -0.5),
        a_g2=nrm((N_EVEN, R_G, W_A), R_G ** -0.5),
        a_kk=1.0 + nrm((N_EVEN, W_A), 0.1),
        a_ka=1.0 + nrm((N_EVEN, W_A), 0.1),
        a_rk=nrm((N_EVEN, H_A, HD_A), 0.1),
        a_ln_w=1.0 + nrm((N_EVEN, W_A), 0.02),
        a_ln_b=nrm((N_EVEN, W_A), 0.02),
        b_fbias=FORGET_BIAS_INIT + nrm((N_EVEN, H_B), 0.1),
        b_qnorm=1.0 + nrm((N_EVEN, HD_B), 0.02),
        b_knorm=1.0 + nrm((N_EVEN, HD_B), 0.02),
        w_out_e=nrm((N_EVEN, W_A + W_B, D_MODEL), (W_A + W_B) ** -0.5),
        c_lam_re=-0.5 + nrm((N_ODD, G_C, P_C), 0.01),
        c_lam_im=jnp.broadcast_to(jnp.pi * jnp.arange(P_C, dtype=jnp.float32), (N_ODD, G_C, P_C)),
        c_log_dt=uni((N_ODD, G_C), math.log(1e-3), math.log(1e-1)),
        c_b_re=nrm((N_ODD, G_C, P_C, GC), (2 * GC) ** -0.5),
        c_b_im=nrm((N_ODD, G_C, P_C, GC), (2 * GC) ** -0.5),
        c_c_re=nrm((N_ODD, G_C, GC, P_C), (2 * P_C) ** -0.5),
        c_c_im=nrm((N_ODD, G_C, GC, P_C), (2 * P_C) ** -0.5),
        c_d=nrm((N_ODD, G_C, GC), 0.5),
        w_glu=nrm((N_ODD, D_MODEL, 2 * D_MODEL), D_MODEL ** -0.5),
        w_up=nrm((DEPTH, D_MODEL, 2 * D_FF), D_MODEL ** -0.5),
        conv_w=nrm((DEPTH, CONV_W, 2 * D_FF), CONV_W ** -0.5),
        conv_b=nrm((DEPTH, 2 * D_FF), 0.02),
        w_down=nrm((DEPTH, D_FF, D_MODEL), D_FF ** -0.5),
    )


def reference(x_prompt, x_sample, state_a_shift, state_a_wkv, cache_b_k, cache_b_v, cache_b_logf,
              state_c_re, state_c_im, state_ffn_conv, meta, norm_mix, norm_ffn, w_in_e, a_mu, a_w0,
              a_w2, a_a0, a_a2, a_g2, a_kk, a_ka, a_rk, a_ln_w, a_ln_b, b_fbias, b_qnorm, b_knorm,
              w_out_e, c_lam_re, c_lam_im, c_log_dt, c_b_re, c_b_im, c_c_re, c_c_im, c_d, w_glu,
              w_up, conv_w, conv_b, w_down):
    prm = dict(norm_mix=norm_mix, norm_ffn=norm_ffn, w_in_e=w_in_e, a_mu=a_mu, a_w0=a_w0,
               a_w2=a_w2, a_a0=a_a0, a_a2=a_a2, a_g2=a_g2, a_kk=a_kk, a_ka=a_ka, a_rk=a_rk,
               a_ln_w=a_ln_w, a_ln_b=a_ln_b, b_fbias=b_fbias, b_qnorm=b_qnorm, b_knorm=b_knorm,
               w_out_e=w_out_e, c_lam_re=c_lam_re, c_lam_im=c_lam_im, c_log_dt=c_log_dt,
               c_b_re=c_b_re, c_b_im=c_b_im, c_c_re=c_c_re, c_c_im=c_c_im, c_d=c_d, w_glu=w_glu,
               w_up=w_up, conv_w=conv_w, conv_b=conv_b, w_down=w_down)
    b = x_prompt.shape[0]
    meta_rows = jnp.broadcast_to(meta.astype(x_prompt.dtype)[None], (b, N_META, D_MODEL))
    xp = jnp.concatenate([meta_rows, x_prompt], axis=1)
    init_p = dict(
        a_shift=jnp.zeros((N_EVEN, b, A_COLS), jnp.float32),
        a_wkv=jnp.zeros((N_EVEN, b, H_A, HD_A, HD_A), jnp.float32),
        b_past=[None] * N_EVEN,
        c_re=jnp.zeros((N_ODD, b, G_C, P_C), jnp.float32),
        c_im=jnp.zeros((N_ODD, b, G_C, P_C), jnp.float32),
        ffn_conv=jnp.zeros((DEPTH, b, CONV_W - 1, 2 * D_FF), jnp.float32))
    yp, (p_a_shift, p_a_wkv, p_b_k, p_b_v, p_b_logf, p_c_re, p_c_im, p_ffn_conv) = trunk(xp, init_p, prm)
    init_s = dict(
        a_shift=state_a_shift,
        a_wkv=state_a_wkv,
        b_past=[(cache_b_k[j], cache_b_v[j], cache_b_logf[j]) for j in range(N_EVEN)],
        c_re=state_c_re,
        c_im=state_c_im,
        ffn_conv=state_ffn_conv)
    ys, (s_a_shift, s_a_wkv, s_b_k, s_b_v, s_b_logf, s_c_re, s_c_im, s_ffn_conv) = trunk(x_sample, init_s, prm)
    return (yp[:, N_META:], ys, p_a_shift, p_a_wkv, p_b_k, p_b_v, p_b_logf, p_c_re, p_c_im, p_ffn_conv,
            s_a_shift, s_a_wkv, s_b_k, s_b_v, s_b_logf, s_c_re, s_c_im, s_ffn_conv)
```

```python
import numpy as np
import concourse.bass as bass
import concourse.mybir as mybir
from concourse.bass_utils import run_bass_kernel_spmd

F32 = mybir.dt.float32
BF16 = mybir.dt.bfloat16
I32 = mybir.dt.int32
AF = mybir.ActivationFunctionType
ALU = mybir.AluOpType
AX = mybir.AxisListType
PI = 3.141592653589793

D = 1024
NIN = 3336
ACOLS = 1792
DFF = 2816
NCORES = 8
import os as _os
SAME_ENGINE_SYNC = _os.environ.get('KSAME', '1') == '1'
DEFER_PE = _os.environ.get('KDEFER', '0') == '1'


class T:
    def __init__(self, ap):
        self.ap = ap
        self.w = None
        self.r = {}


class Rot:
    def __init__(self, ts):
        self.ts = ts
        self.i = 0

    def next(self):
        t = self.ts[self.i % len(self.ts)]
        self.i += 1
        return t


class Sched:
    EPOCH = 30000
    DEPOCH = 1800

    def __init__(self, nc):
        self.nc = nc
        self.streams = {e: [] for e in ("pe", "act", "dve", "pool", "sp")}
        self.cnt = {e: 0 for e in self.streams}
        self.sems = {}
        self.waited = {e: {} for e in self.streams}
        self.dq = {"sp": list(range(0, 20)), "pool": list(range(20, 28)), "act": list(range(28, 32))}
        self.drr = {q: 0 for q in self.dq}
        self.dcnt = [0] * 32
        self.nops = 0
        self.pend = {}

    def _wait(self, eng, tok):
        key, val = tok
        if self.waited[eng].get(key, 0) >= val:
            return
        self.waited[eng][key] = val
        self.streams[eng].append(("w", key, val))

    def _deps(self, eng, reads, writes):
        toks = {}
        for t in reads:
            if t.w is not None:
                k, v = t.w
                toks[k] = max(toks.get(k, 0), v)
        for t in writes:
            if t.w is not None:
                k, v = t.w
                toks[k] = max(toks.get(k, 0), v)
            for k, v in t.r.items():
                toks[k] = max(toks.get(k, 0), v)
        for k, v in toks.items():
            if eng == "pe" and k[0] == "pe":
                continue
            if not SAME_ENGINE_SYNC and k[0] == eng and eng in ("act", "dve"):
                continue
            self._wait(eng, (k, v))

    def _upd(self, tok, reads, writes):
        k, v = tok
        for t in reads:
            t.r[k] = max(t.r.get(k, 0), v)
        for t in writes:
            t.w = tok
            t.r = {}

    def op(self, eng, fn, reads=(), writes=(), defer=False):
        self._deps(eng, reads, writes)
        n = self.cnt[eng]
        self.cnt[eng] = n + 1
        key = (eng, n // self.EPOCH)
        val = n % self.EPOCH + 1
        if defer and (n % self.EPOCH) != self.EPOCH - 1:
            self.pend[eng] = self.pend.get(eng, 0) + 1
            self.streams[eng].append(("o", fn, key, 0))
        else:
            self.streams[eng].append(("o", fn, key, 1 + self.pend.get(eng, 0)))
            self.pend[eng] = 0
        self._upd((key, val), reads, writes)
        self.nops += 1

    def dma(self, q, out_ap, in_ap, reads=(), writes=(), **kw):
        self._deps(q, reads, writes)
        lst = self.dq[q]
        k = lst[self.drr[q] % len(lst)]
        self.drr[q] += 1
        c = self.dcnt[k]
        key = ("d", k, c // self.DEPOCH)
        cc = c % self.DEPOCH
        if cc > 0:
            self._wait(q, (key, 16 * cc))
        elif c > 0:
            self._wait(q, (("d", k, c // self.DEPOCH - 1), 16 * self.DEPOCH))
        self.dcnt[k] = c + 1
        self.streams[q].append(("d", out_ap, in_ap, key, kw))
        self._upd((key, 16 * (cc + 1)), reads, writes)
        self.nops += 1

    def emit(self):
        nc = self.nc
        for k in range(32):
            c = self.dcnt[k]
            if c > 0:
                e = (c - 1) // self.DEPOCH
                self._wait("sp", (("d", k, e), 16 * ((c - 1) % self.DEPOCH + 1)))
        keys = set()
        for e, st in self.streams.items():
            for it in st:
                if it[0] == "w":
                    keys.add(it[1])
                elif it[0] == "o":
                    keys.add(it[2])
                else:
                    keys.add(it[3])
        for k in sorted(keys, key=str):
            self.sems[k] = nc.alloc_semaphore("s_" + "_".join(str(x) for x in k))
        sems = self.sems

        def run(eng, name):
            for it in self.streams[name]:
                if it[0] == "w":
                    eng.wait_ge(sems[it[1]], it[2])
                elif it[0] == "o":
                    if it[3] > 0:
                        it[1](eng).then_inc(sems[it[2]], it[3])
                    else:
                        it[1](eng)
                else:
                    eng.dma_start(out=it[1], in_=it[2], **it[4]).then_inc(sems[it[3]], 16)

        with nc.Block() as block:
            @block.tensor
            def _(e):
                run(e, "pe")

            @block.scalar
            def _(e):
                run(e, "act")

            @block.vector
            def _(e):
                run(e, "dve")

            @block.gpsimd
            def _(e):
                run(e, "pool")

            @block.sync
            def _(e):
                run(e, "sp")


def build(NSP, LP, NSS, LS, PAST):
    nc = bass.Bass("TRN2", target_bir_lowering=False)
    S = Sched(nc)
    XP = LP - 16
    assert PAST % 128 == 0

    def din(name, shape):
        return T(nc.dram_tensor(name, list(shape), F32, kind="ExternalInput").ap())

    def dout(name, shape):
        return T(nc.dram_tensor(name, list(shape), F32, kind="ExternalOutput").ap())

    def dscr(name, shape, dt):
        return T(nc.dram_tensor(name, list(shape), dt, kind="Internal").ap())

    cnt = [0]

    def sb(shape, dt=F32, name=None):
        cnt[0] += 1
        return T(nc.alloc_sbuf_tensor("%s_%d" % (name or "t", cnt[0]), list(shape), dt)[:])

    I = {}
    I["x_prompt"] = din("x_prompt", [NSP, XP, D])
    I["x_sample"] = din("x_sample", [NSS, LS, D])
    I["state_a_shift"] = din("state_a_shift", [NSS, ACOLS])
    I["state_a_wkv"] = din("state_a_wkv", [NSS, 8, 64, 64])
    I["cache_b_k"] = din("cache_b_k", [NSS, PAST, 512])
    I["cache_b_v"] = din("cache_b_v", [NSS, PAST, 512])
    I["cache_b_logf"] = din("cache_b_logf", [NSS, PAST, 8])
    I["state_c_re"] = din("state_c_re", [NSS, 64, 64])
    I["state_c_im"] = din("state_c_im", [NSS, 64, 64])
    I["state_ffn_conv"] = din("state_ffn_conv", [2, NSS, 2, 2 * DFF])
    wshapes = dict(meta=[16, D], norm_mix=[2, D], norm_ffn=[2, D], w_in_e=[D, NIN], a_mu=[ACOLS], a_w0=[512],
                   a_w2=[64, 512], a_a0=[512], a_a2=[64, 512], a_g2=[128, 512], a_kk=[512], a_ka=[512],
                   a_rk=[512], a_ln_w=[512], a_ln_b=[512], b_fbias=[8], b_qnorm=[64], b_knorm=[64],
                   w_out_e=[D, D], c_lam_re=[64, 64], c_lam_im=[64, 64], c_log_dt=[64],
                   c_b_re=[64, 64, 16], c_b_im=[64, 64, 16], c_c_re=[64, 16, 64], c_c_im=[64, 16, 64],
                   c_d=[1024], w_glu=[D, 2 * D], w_up=[2, D, 2 * DFF], conv_w=[2, 3, 2 * DFF],
                   conv_b=[2, 2 * DFF], w_down=[2, DFF, D])
    for k_, v_ in wshapes.items():
        I[k_] = din(k_, v_)
    O = {}
    for g, ns, lrows, lx in (("p", NSP, LP, XP), ("s", NSS, LS, LS)):
        O[g + "_y"] = dout(g + "_y", [ns, lx, D])
        O[g + "_a_shift"] = dout(g + "_a_shift", [ns, ACOLS])
        O[g + "_a_wkv"] = dout(g + "_a_wkv", [ns, 8, 64, 64])
        O[g + "_b_k"] = dout(g + "_b_k", [ns, lrows, 512])
        O[g + "_b_v"] = dout(g + "_b_v", [ns, lrows, 512])
        O[g + "_b_logf"] = dout(g + "_b_logf", [ns, lrows, 8])
        O[g + "_c_re"] = dout(g + "_c_re", [ns, 64, 64])
        O[g + "_c_im"] = dout(g + "_c_im", [ns, 64, 64])
        O[g + "_ffn_conv"] = dout(g + "_ffn_conv", [2, ns, 2, 2 * DFF])
    W = {}
    W["in"] = dscr("w_in_b", [7, 128, 4096], BF16)
    W["out"] = dscr("w_out_b", [2, 128, 4096], BF16)
    W["up0"] = dscr("w_up0_b", [11, 128, 4096], BF16)
    W["up1"] = dscr("w_up1_b", [11, 128, 4096], BF16)
    W["dn0"] = dscr("w_dn0_b", [8, 128, 2816], BF16)
    W["dn1"] = dscr("w_dn1_b", [8, 128, 2816], BF16)
    W["glu"] = dscr("w_glu_b", [4, 128, 4096], BF16)

    PS = Rot([T(nc.alloc_psum_tensor("ps%d" % i, [128, 1024], F32)[:]) for i in range(3)])
    PSACC = [T(nc.alloc_psum_tensor("psacc%d" % i, [128, 512], F32)[:]) for i in range(2)]
    SLOTS = Rot([sb([128, 4096], BF16, "slot") for _ in range(3)])
    STG = Rot([sb([128, 2048], F32, "stg") for _ in range(2)])

    def rot(n, shape, dt=F32, name="r"):
        return Rot([sb(shape, dt, name) for _ in range(n)])

    def mm(ot, o, lt, l, rt, r, start=True, stop=True):
        S.op("pe", lambda e: e.matmul(o, l, r, start=start, stop=stop), reads=[lt, rt], writes=[ot],
             defer=(not stop) and DEFER_PE)

    def tr(ot, o, it, i, idt, idap):
        S.op("pe", lambda e: e.transpose(o, i, idap), reads=[it, idt], writes=[ot])

    def act(ot, o, it, i, func, scale=None, bias=None, extra=()):
        kw = {}
        if scale is not None:
            kw["scale"] = scale
        if bias is not None:
            kw["bias"] = bias
        S.op("act", lambda e: e.activation(o, i, func, **kw), reads=[it] + list(extra), writes=[ot])

    def tt(eng, ot, o, at, a, bt, b, op):
        S.op(eng, lambda e: e.tensor_tensor(o, a, b, op), reads=[at, bt], writes=[ot])

    def ts(eng, ot, o, at, a, s1, s2, op0, op1=None, extra=()):
        if op1 is None:
            S.op(eng, lambda e: e.tensor_scalar(o, a, s1, None, op0), reads=[at] + list(extra), writes=[ot])
        else:
            S.op(eng, lambda e: e.tensor_scalar(o, a, s1, s2, op0, op1), reads=[at] + list(extra), writes=[ot])

    def stt(ot, o, at, a, sc, bt, b, op0, op1, extra=()):
        S.op("dve", lambda e: e.scalar_tensor_tensor(o, a, sc, b, op0, op1), reads=[at, bt] + list(extra),
             writes=[ot])

    def cp(eng, ot, o, it, i):
        if eng == "act":
            S.op("act", lambda e: e.copy(o, i), reads=[it], writes=[ot])
        else:
            S.op(eng, lambda e: e.tensor_copy(o, i), reads=[it], writes=[ot])

    def memset(eng, t, ap, val):
        S.op(eng, lambda e: e.memset(ap, val), writes=[t])

    def recip(ot, o, it, i):
        S.op("dve", lambda e: e.reciprocal(o, i), reads=[it], writes=[ot])

    CONST = sb([128, 6, 128], F32, "const")
    CB = sb([128, 3, 128], BF16, "constb")
    ii = sb([128, 3, 128], I32, "iota")
    fi = sb([128, 3, 128], F32, "iotaf")
    S.op("pool", lambda e: e.iota(ii.ap[:, 0, :], [[1, 128]], base=0, channel_multiplier=-1), writes=[ii])
    S.op("pool", lambda e: e.iota(ii.ap[:, 1, :], [[0, 128]], base=0, channel_multiplier=1), writes=[ii])
    S.op("pool", lambda e: e.iota(ii.ap[:, 2, :], [[1, 128]], base=0, channel_multiplier=0), writes=[ii])
    cp("dve", fi, fi.ap, ii, ii.ap)
    c_ = CONST.ap
    ts("dve", CONST, c_[:, 0, :], fi, fi.ap[:, 0, :], 0.0, None, ALU.is_equal)
    ts("dve", CONST, c_[:, 1, :], fi, fi.ap[:, 0, :], 0.0, None, ALU.is_gt)
    ts("dve", CONST, c_[:, 2, :], fi, fi.ap[:, 0, :], 0.0, None, ALU.is_ge)
    ts("dve", CONST, c_[:, 3, :], fi, fi.ap[:, 0, :], 0.0, None, ALU.is_lt)
    memset("dve", CONST, c_[:, 4, :], 1.0)
    ts("dve", fi, fi.ap[:, 1, :], fi, fi.ap[:, 1, :], 64.0, None, ALU.is_ge)
    ts("dve", fi, fi.ap[:, 2, :], fi, fi.ap[:, 2, :], 64.0, None, ALU.is_ge)
    tt("dve", CONST, c_[:, 5, :], fi, fi.ap[:, 1, :], fi, fi.ap[:, 2, :], ALU.is_equal)
    cp("dve", CB, CB.ap[:, 0, :], CONST, c_[:, 0, :])
    cp("dve", CB, CB.ap[:, 1, :], CONST, c_[:, 4, :])
    cp("dve", CB, CB.ap[:, 2, :], CONST, c_[:, 5, :])
    IDF = c_[:, 0, :]
    M_SU = c_[:, 1, :]
    M_U = c_[:, 2, :]
    M_SL = c_[:, 3, :]
    ONESF = c_[:, 4, :]
    IDB = CB.ap[:, 0, :]
    ONESB = CB.ap[:, 1, :]
    BDB = CB.ap[:, 2, :]

    PRM = sb([128, 512], F32, "prm")
    pcol = [0]

    def pload(vec_ap, ntile):
        c0 = pcol[0]
        pcol[0] += ntile
        S.dma("sp", PRM.ap[:, c0:c0 + ntile], vec_ap.rearrange("(j p) -> p j", p=128), writes=[PRM],
              allow_slow_non_contiguous=True)
        return c0

    def pc(c0, j):
        return PRM.ap[:, c0 + j:c0 + j + 1]

    def pnew(n):
        c0 = pcol[0]
        pcol[0] += n
        return c0

    g_mix = [pload(I["norm_mix"].ap[l], 8) for l in range(2)]
    g_ffn = [pload(I["norm_ffn"].ap[l], 8) for l in range(2)]
    p_mu = pload(I["a_mu"].ap, 14)
    p_w0 = pload(I["a_w0"].ap, 4)
    p_a0 = pload(I["a_a0"].ap, 4)
    p_kk = pload(I["a_kk"].ap, 4)
    p_ka = pload(I["a_ka"].ap, 4)
    p_rk = pload(I["a_rk"].ap, 4)
    p_lnw = pload(I["a_ln_w"].ap, 4)
    p_lnb = pload(I["a_ln_b"].ap, 4)
    p_cw = [[pload(I["conv_w"].ap[l, i], 44) for i in range(3)] for l in range(2)]
    p_cb = [pload(I["conv_b"].ap[l], 44) for l in range(2)]
    p_dsk = pload(I["c_d"].ap, 8)
    p_omm = pnew(14)
    ts("dve", PRM, PRM.ap[:, p_omm:p_omm + 14], PRM, PRM.ap[:, p_mu:p_mu + 14], -1.0, 1.0, ALU.mult, ALU.add)
    p_omka = pnew(4)
    ts("dve", PRM, PRM.ap[:, p_omka:p_omka + 4], PRM, PRM.ap[:, p_ka:p_ka + 4], -1.0, 1.0, ALU.mult, ALU.add)
    p_gq = pnew(1)
    p_gk = pnew(1)
    for hh in range(2):
        S.dma("sp", PRM.ap[hh * 64:(hh + 1) * 64, p_gq:p_gq + 1], I["b_qnorm"].ap.rearrange("(p o) -> p o", o=1),
              writes=[PRM], allow_slow_non_contiguous=True)
        S.dma("sp", PRM.ap[hh * 64:(hh + 1) * 64, p_gk:p_gk + 1], I["b_knorm"].ap.rearrange("(p o) -> p o", o=1),
              writes=[PRM], allow_slow_non_contiguous=True)
    ts("dve", PRM, PRM.ap[:, p_gq:p_gq + 1], PRM, PRM.ap[:, p_gq:p_gq + 1], 0.125, None, ALU.mult)
    FB = sb([128, 8], F32, "fbias")
    S.dma("sp", FB.ap, I["b_fbias"].ap.partition_broadcast(128), writes=[FB], allow_slow_non_contiguous=True)

    ceng = Rot(["act", "dve", "pool"])

    def convert(src_t, src_ap, dst_t, dst_ap, A, B):
        st = STG.next()
        sl = SLOTS.next()
        sv = st.ap[:, 0:A * B].rearrange("p (a b) -> p a b", a=A)
        bv = sl.ap[:, 0:A * B].rearrange("p (a b) -> p a b", a=A)
        S.dma("sp", sv, src_ap, reads=[src_t], writes=[st])
        cp(ceng.next(), sl, bv, st, sv)
        S.dma("pool", dst_ap, bv, reads=[sl], writes=[dst_t])

    def conv_mat(src_t, src2d, dst_t, npieces, ncols):
        v = src2d.rearrange("(k p) c -> p k c", p=128)
        for q in range(npieces):
            for h in range(2):
                c0 = q * 512 + h * 256
                cw = min(256, ncols - c0)
                if cw <= 0:
                    continue
                dv = dst_t.ap[q].rearrange("p (k c) -> p k c", k=8)[:, :, h * 256:h * 256 + cw]
                convert(src_t, v[:, :, c0:c0 + cw], dst_t, dv, 8, cw)

    conv_mat(I["w_in_e"], I["w_in_e"].ap, W["in"], 7, NIN)
    conv_mat(I["w_out_e"], I["w_out_e"].ap, W["out"], 2, D)
    conv_mat(I["w_glu"], I["w_glu"].ap, W["glu"], 4, 2 * D)
    for l in range(2):
        v = I["w_up"].ap[l].rearrange("(k p) c -> p k c", p=128)
        for q in range(11):
            for h in range(2):
                c0 = h * DFF + q * 256
                dv = W["up%d" % l].ap[q].rearrange("p (k c) -> p k c", k=8)[:, :, h * 256:(h + 1) * 256]
                convert(I["w_up"], v[:, :, c0:c0 + 256], W["up%d" % l], dv, 8, 256)
        v = I["w_down"].ap[l].rearrange("(k p) c -> p k c", p=128)
        for j in range(8):
            for h in range(2):
                dv = W["dn%d" % l].ap[j].rearrange("p (k c) -> p k c", k=22)[:, h * 11:(h + 1) * 11, :]
                convert(I["w_down"], v[:, h * 11:(h + 1) * 11, j * 128:(j + 1) * 128], W["dn%d" % l], dv, 11, 128)

    LR = sb([128, 512], BF16, "lr")
    G2 = sb([128, 512], BF16, "g2")
    st = STG.next()
    S.dma("sp", st.ap[0:64, 0:512], I["a_w2"].ap, writes=[st])
    S.dma("sp", st.ap[64:128, 0:512], I["a_a2"].ap, writes=[st])
    cp("dve", LR, LR.ap, st, st.ap[:, 0:512])
    st = STG.next()
    S.dma("sp", st.ap[:, 0:512], I["a_g2"].ap, writes=[st])
    cp("dve", G2, G2.ap, st, st.ap[:, 0:512])

    S5P = sb([128, 16, 32], F32, "s5p")
    s5 = S5P.ap
    LRE, LIM, LDT, DT_, MAG, TH, FRE, FIM, FIR, FII, T0, T1, T2, T3 = range(14)
    S.dma("sp", s5[:, LRE, :], I["c_lam_re"].ap.rearrange("(m gg) p -> (gg p) m", gg=2), writes=[S5P],
          allow_slow_non_contiguous=True)
    S.dma("sp", s5[:, LIM, :], I["c_lam_im"].ap.rearrange("(m gg) p -> (gg p) m", gg=2), writes=[S5P],
          allow_slow_non_contiguous=True)
    for gg in range(2):
        S.dma("sp", s5[gg * 64:(gg + 1) * 64, LDT, :],
              I["c_log_dt"].ap.rearrange("(m gg) -> gg m", gg=2)[gg].partition_broadcast(64), writes=[S5P],
              allow_slow_non_contiguous=True)
    act(S5P, s5[:, DT_, :], S5P, s5[:, LDT, :], AF.Exp)
    tt("dve", S5P, s5[:, T0, :], S5P, s5[:, LRE, :], S5P, s5[:, DT_, :], ALU.mult)
    act(S5P, s5[:, MAG, :], S5P, s5[:, T0, :], AF.Exp)
    tt("dve", S5P, s5[:, TH, :], S5P, s5[:, LIM, :], S5P, s5[:, DT_, :], ALU.mult)

    RRI = sb([128, 1024], I32, "rri")
    RRK = sb([128, 2048], F32, "rrk")

    def range_reduce(t, ap, n):
        if len(ap.shape) == 3 and n > 1024:
            a_ = ap.shape[1]
            step = 1024 // ap.shape[2]
            for a0 in range(0, a_, step):
                range_reduce(t, ap[:, a0:a0 + step, :], 1024)
            return
        ki = RRI.ap[:, 0:n]
        kf = RRK.ap[:, 0:n]
        if len(ap.shape) == 3:
            ki = ki.rearrange("p (a b) -> p a b", a=ap.shape[1])
            kf = kf.rearrange("p (a b) -> p a b", a=ap.shape[1])
        ts("dve", RRK, kf, t, ap, 1.0 / (2 * PI), 0.5, ALU.mult, ALU.add)
        cp("dve", RRI, ki, RRK, kf)
        cp("dve", RRK, kf, RRI, ki)
        stt(t, ap, RRK, kf, -6.28125, t, ap, ALU.mult, ALU.add)
        stt(t, ap, RRK, kf, -(2 * PI - 6.28125), t, ap, ALU.mult, ALU.add)
        ts("dve", RRK, kf, t, ap, PI, -2 * PI, ALU.is_gt, ALU.mult)
        tt("dve", t, ap, t, ap, RRK, kf, ALU.add)
        ts("dve", RRK, kf, t, ap, -PI, 2 * PI, ALU.is_lt, ALU.mult)
        tt("dve", t, ap, t, ap, RRK, kf, ALU.add)
        ts("dve", t, ap, t, ap, PI, -PI, ALU.min, ALU.max)

    ts("dve", S5P, s5[:, T1, :], S5P, s5[:, TH, :], 1.0, None, ALU.mult)
    range_reduce(S5P, s5[:, T1, :], 32)
    act(S5P, s5[:, T2, :], S5P, s5[:, T1, :], AF.Sin)
    ts("dve", S5P, s5[:, T1, :], S5P, s5[:, TH, :], PI / 2, None, ALU.add)
    range_reduce(S5P, s5[:, T1, :], 32)
    act(S5P, s5[:, T3, :], S5P, s5[:, T1, :], AF.Sin)
    tt("dve", S5P, s5[:, T2, :], S5P, s5[:, T2, :], S5P, s5[:, MAG, :], ALU.mult)
    tt("dve", S5P, s5[:, T3, :], S5P, s5[:, T3, :], S5P, s5[:, MAG, :], ALU.mult)
    ts("dve", S5P, s5[:, T3, :], S5P, s5[:, T3, :], -1.0, None, ALU.add)
    tt("dve", S5P, s5[:, T0, :], S5P, s5[:, LRE, :], S5P, s5[:, LRE, :], ALU.mult)
    tt("dve", S5P, s5[:, T1, :], S5P, s5[:, LIM, :], S5P, s5[:, LIM, :], ALU.mult)
    tt("dve", S5P, s5[:, T0, :], S5P, s5[:, T0, :], S5P, s5[:, T1, :], ALU.add)
    recip(S5P, s5[:, T0, :], S5P, s5[:, T0, :])
    tt("dve", S5P, s5[:, FRE, :], S5P, s5[:, T3, :], S5P, s5[:, LRE, :], ALU.mult)
    tt("dve", S5P, s5[:, T1, :], S5P, s5[:, T2, :], S5P, s5[:, LIM, :], ALU.mult)
    tt("dve", S5P, s5[:, FRE, :], S5P, s5[:, FRE, :], S5P, s5[:, T1, :], ALU.add)
    tt("dve", S5P, s5[:, FRE, :], S5P, s5[:, FRE, :], S5P, s5[:, T0, :], ALU.mult)
    tt("dve", S5P, s5[:, FIM, :], S5P, s5[:, T2, :], S5P, s5[:, LRE, :], ALU.mult)
    tt("dve", S5P, s5[:, T1, :], S5P, s5[:, T3, :], S5P, s5[:, LIM, :], ALU.mult)
    tt("dve", S5P, s5[:, FIM, :], S5P, s5[:, FIM, :], S5P, s5[:, T1, :], ALU.subtract)
    tt("dve", S5P, s5[:, FIM, :], S5P, s5[:, FIM, :], S5P, s5[:, T0, :], ALU.mult)
    tt("dve", S5P, s5[:, T0, :], S5P, s5[:, FRE, :], S5P, s5[:, FRE, :], ALU.mult)
    tt("dve", S5P, s5[:, T1, :], S5P, s5[:, FIM, :], S5P, s5[:, FIM, :], ALU.mult)
    tt("dve", S5P, s5[:, T0, :], S5P, s5[:, T0, :], S5P, s5[:, T1, :], ALU.add)
    recip(S5P, s5[:, T0, :], S5P, s5[:, T0, :])
    tt("dve", S5P, s5[:, FIR, :], S5P, s5[:, FRE, :], S5P, s5[:, T0, :], ALU.mult)
    tt("dve", S5P, s5[:, FII, :], S5P, s5[:, FIM, :], S5P, s5[:, T0, :], ALU.mult)
    ts("dve", S5P, s5[:, FII, :], S5P, s5[:, FII, :], -1.0, None, ALU.mult)
    TCOS = sb([128, 32, 64], F32, "tcos")
    TSIN = sb([128, 32, 64], F32, "tsin")
    cp("dve", fi, fi.ap[:, 2, :], ii, ii.ap[:, 2, :])
    ts("dve", fi, fi.ap[:, 2, :], fi, fi.ap[:, 2, :], 1.0, None, ALU.add)
    for m in range(32):
        ts("dve", TSIN, TSIN.ap[:, m, :], fi, fi.ap[:, 2, 0:64], s5[:, TH, m:m + 1], None, ALU.mult, extra=[S5P])
    ts("dve", TCOS, TCOS.ap, TSIN, TSIN.ap, PI / 2, None, ALU.add)
    range_reduce(TSIN, TSIN.ap, 2048)
    range_reduce(TCOS, TCOS.ap, 2048)
    act(TSIN, TSIN.ap, TSIN, TSIN.ap, AF.Sin)
    act(TCOS, TCOS.ap, TCOS, TCOS.ap, AF.Sin)
    BPR = sb([128, 32, 128], BF16, "bpr")
    BPI = sb([128, 32, 128], BF16, "bpi")
    CPR = sb([128, 32, 128], BF16, "cpr")
    CPI = sb([128, 32, 128], BF16, "cpi")
    for half in range(2):
        for src, dst in ((I["c_b_re"], BPR), (I["c_b_im"], BPI)):
            st = STG.next()
            memset("pool", st, st.ap, 0.0)
            sv = st.ap.rearrange("r (m4 mm s) -> r mm m4 s", mm=4, s=128)
            bv = src.ap.rearrange("(m4 mm gg) p c -> mm gg c m4 p", mm=4, gg=2)
            for mm_ in range(4):
                for gg in range(2):
                    r0 = mm_ * 32 + gg * 16
                    for m4l in range(4):
                        S.dma("sp", sv[r0:r0 + 16, mm_, m4l, gg * 64:(gg + 1) * 64],
                              bv[mm_, gg][:, 4 * half + m4l, :], reads=[src], writes=[st],
                              allow_slow_non_contiguous=True)
            cp("act", dst, dst.ap[:, 16 * half:16 * half + 16, :].rearrange("p a b -> p (a b)"), st, st.ap)
        stc = []
        for src in (I["c_c_re"], I["c_c_im"]):
            st = STG.next()
            memset("pool", st, st.ap, 0.0)
            sv = st.ap.rearrange("r (m4 mm s) -> r mm m4 s", mm=4, s=128)
            cv = src.ap.rearrange("(m4 mm gg) c p -> mm gg p m4 c", mm=4, gg=2)
            for mm_ in range(4):
                for gg in range(2):
                    c0 = mm_ * 32 + gg * 16
                    for m4l in range(4):
                        S.dma("sp", sv[gg * 64:(gg + 1) * 64, mm_, m4l, c0:c0 + 16],
                              cv[mm_, gg][:, 4 * half + m4l, :], reads=[src], writes=[st],
                              allow_slow_non_contiguous=True)
            stc.append(st)
        for qt in range(2):
            m0 = 16 * half + 8 * qt
            cre = stc[0].ap[:, 1024 * qt:1024 * qt + 1024].rearrange("p (m s) -> p m s", s=128)
            cim = stc[1].ap[:, 1024 * qt:1024 * qt + 1024].rearrange("p (m s) -> p m s", s=128)
            fre = s5[:, FRE, m0:m0 + 8].unsqueeze(2).to_broadcast([128, 8, 128])
            fim = s5[:, FIM, m0:m0 + 8].unsqueeze(2).to_broadcast([128, 8, 128])
            ta = RRK.ap[:, 0:1024].rearrange("p (m s) -> p m s", s=128)
            tb = RRK.ap[:, 1024:2048].rearrange("p (m s) -> p m s", s=128)
            tt("dve", RRK, ta, stc[0], cre, S5P, fre, ALU.mult)
            tt("dve", RRK, tb, stc[1], cim, S5P, fim, ALU.mult)
            tt("dve", CPR, CPR.ap[:, m0:m0 + 8, :], RRK, ta, RRK, tb, ALU.subtract)
            tt("dve", RRK, ta, stc[0], cre, S5P, fim, ALU.mult)
            tt("dve", RRK, tb, stc[1], cim, S5P, fre, ALU.mult)
            tt("dve", RRK, ta, RRK, ta, RRK, tb, ALU.add)
            ts("dve", CPI, CPI.ap[:, m0:m0 + 8, :], RRK, ta, -1.0, None, ALU.mult)

    XT = sb([128, 8, 128], F32, "xT")
    HT = sb([128, 8, 128], BF16, "hT")
    XIN = rot(1, [128, D], F32, "xin")
    U = sb([128, 14, 129], F32, "U")
    UM = sb([128, 14, 128], F32, "UM")
    QKR = sb([128, 8, 128], F32, "qkraw")
    VB = rot(1, [128, 512], F32, "vbtok")
    LFT = sb([128, 8], F32, "lftok")
    HS = sb([128, 4, 64], F32, "HS")
    HSB = sb([128, 4, 64], BF16, "HSb")
    CAR = sb([128, 2, 32], F32, "s5car")
    CC = [sb([128, 44, 2], F32, "convcar") for _ in range(2)]
    NBMAX = max(PAST // 128 + 1, (LP + 127) // 128)
    assert NBMAX * 8 <= 384
    SUF = T(fi.ap.rearrange("p a b -> p (a b)")[:, 0:NBMAX * 8].rearrange("p (b h) -> p b h", h=8))
    TOT = T(ii.ap.bitcast(F32).rearrange("p a b -> p (a b)")[:, 0:NBMAX * 8].rearrange("p (b h) -> p b h", h=8))
    S.op("pool", lambda e: e.memset(SUF.ap, 0.0), writes=[fi, SUF])
    S.op("pool", lambda e: e.memset(TOT.ap, 0.0), writes=[ii, TOT])
    YCAT = sb([128, 8, 128], BF16, "ycat")
    ACTB = sb([128, 22, 128], BF16, "actb")
    SREr = rot(2, [128, 4, 128], BF16, "sre")
    SIMr = rot(2, [128, 4, 128], BF16, "sim")
    tmpF = Rot([T(STG.ts[i // 4].ap[:, 512 * (i % 4):512 * (i % 4 + 1)]) for i in range(6)])
    tmpB = Rot([T(STG.ts[1].ap[:, 1024 + 256 * i:1024 + 256 * (i + 1)].bitcast(BF16)) for i in range(4)])
    for t_ in tmpF.ts + tmpB.ts:
        S.op("pool", lambda e, t_=t_: e.memset(t_.ap[:, 0:1], 0.0), writes=STG.ts + [t_])

    def load_piece(name, q, size=4096):
        sl = SLOTS.next()
        if _os.environ.get("KNOW") == "1" and name.startswith(("up", "dn")):
            return sl
        if name == "in" and q == 6:
            S.dma("sp", sl.ap.rearrange("p (k c) -> p k c", k=8)[:, :, 0:264],
                  W[name].ap[q].rearrange("p (k c) -> p k c", k=8)[:, :, 0:264], reads=[W[name]], writes=[sl])
        else:
            S.dma("sp", sl.ap[:, 0:size], W[name].ap[q], reads=[W[name]], writes=[sl])
        return sl

    def rmsnorm(n, gcol):
        sq = tmpB.next()
        ps = PS.next()
        for half in range(2):
            sqv = sq.ap[:, 0:4 * n].rearrange("p (k t) -> p k t", k=4)
            act(sq, sqv, XT, XT.ap[:, 4 * half:4 * half + 4, 0:n], AF.Square)
            for k in range(4):
                mm(ps, ps.ap[:, 0:n], CB, ONESB, sq, sqv[:, k, :], start=(half == 0 and k == 0),
                   stop=(half == 1 and k == 3))
            if half == 0:
                sq = tmpB.next()
        r = tmpF.next()
        act(r, r.ap[:, 0:n], ps, ps.ap[:, 0:n], AF.Sqrt, scale=1.0 / D, bias=EPSC.ap[:, 0:1], extra=[EPSC])
        recip(r, r.ap[:, 0:n], r, r.ap[:, 0:n])
        for k in range(8):
            stt(HT, HT.ap[:, k, 0:n], XT, XT.ap[:, k, 0:n], pc(gcol, k), r, r.ap[:, 0:n], ALU.mult, ALU.mult,
                extra=[PRM])

    EPSC = sb([128, 4], F32, "epsc")
    memset("dve", EPSC, EPSC.ap[:, 0:1], 1e-6)
    memset("dve", EPSC, EPSC.ap[:, 1:2], 1e-12)
    memset("dve", EPSC, EPSC.ap[:, 2:3], 64e-5)
    memset("dve", EPSC, EPSC.ap[:, 3:4], 1.0)

    def ffn(n, l):
        rmsnorm(n, g_ffn[l])
        cw0, cw1, cw2 = p_cw[l]
        cb = p_cb[l]
        car = CC[l]
        for q in range(11):
            sl = load_piece("up%d" % l, q)
            wv = sl.ap.rearrange("p (k c) -> p k c", k=8)
            for loc in range(2):
                jj = 2 * q + loc
                zs = []
                for part in range(2):
                    tile_ = jj + 22 * part
                    ps = PS.next()
                    c0 = part * 256 + loc * 128
                    for k in range(8):
                        mm(ps, ps.ap[:, 0:n], sl, wv[:, k, c0:c0 + 128], HT, HT.ap[:, k, 0:n], start=(k == 0),
                           stop=(k == 7))
                    up = tmpF.next()
                    cp("pool", up, up.ap[:, 0:2], car, car.ap[:, tile_, :])
                    cp("act", up, up.ap[:, 2:2 + n], ps, ps.ap[:, 0:n])
                    z = tmpF.next()
                    ts("dve", z, z.ap[:, 0:n], up, up.ap[:, 2:2 + n], pc(cw2, tile_), pc(cb, tile_), ALU.mult,
                       ALU.add, extra=[PRM])
                    stt(z, z.ap[:, 0:n], up, up.ap[:, 1:1 + n], pc(cw1, tile_), z, z.ap[:, 0:n], ALU.mult, ALU.add,
                        extra=[PRM])
                    stt(z, z.ap[:, 0:n], up, up.ap[:, 0:n], pc(cw0, tile_), z, z.ap[:, 0:n], ALU.mult, ALU.add,
                        extra=[PRM])
                    cp("pool", car, car.ap[:, tile_, :], up, up.ap[:, n:n + 2])
                    zs.append(z)
                sg = tmpF.next()
                act(sg, sg.ap[:, 0:n], zs[1], zs[1].ap[:, 0:n], AF.Silu)
                tt("dve", ACTB, ACTB.ap[:, jj, 0:n], zs[0], zs[0].ap[:, 0:n], sg, sg.ap[:, 0:n], ALU.mult)
        for j in range(8):
            sl = load_piece("dn%d" % l, j, 2816)
            wv = sl.ap[:, 0:2816].rearrange("p (k c) -> p k c", k=22)
            ps = PS.next()
            for k in range(22):
                mm(ps, ps.ap[:, 0:n], sl, wv[:, k, :], ACTB, ACTB.ap[:, k, 0:n], start=(k == 0), stop=(k == 21))
            tt("dve", XT, XT.ap[:, j, 0:n], XT, XT.ap[:, j, 0:n], ps, ps.ap[:, 0:n], ALU.add)

    S5I = sb([128, 2, 8, 4], F32, "s5init")
    S5SETS = []

    def s5_chain(n, k, e):
        nhalf = (n + 63) // 64
        nh = n // nhalf
        t1, t2, t3, t4 = S5SETS[e]
        pp = PS.next()
        for mi in range(4):
            m = 4 * k + mi
            mm(pp, pp.ap[:, mi * 128:mi * 128 + n], BPR, BPR.ap[:, m, :], HT, HT.ap[:, k, 0:n])
            mm(pp, pp.ap[:, 512 + mi * 128:512 + mi * 128 + n], BPI, BPI.ap[:, m, :], HT, HT.ap[:, k, 0:n])
        yield

        def v4(ap512):
            return ap512.rearrange("p (m t) -> p m t", m=4)[:, :, 0:n].rearrange("p m (h t) -> p m h t", h=nhalf)

        prv = v4(pp.ap[:, 0:512])
        piv = v4(pp.ap[:, 512:1024])
        cs_ = TCOS.ap[:, 4 * k:4 * k + 4, 0:nh].unsqueeze(2).to_broadcast([128, 4, nhalf, nh])
        sn_ = TSIN.ap[:, 4 * k:4 * k + 4, 0:nh].unsqueeze(2).to_broadcast([128, 4, nhalf, nh])
        x1, x2, x3, x4 = [v4(t.ap) for t in (t1, t2, t3, t4)]
        tt("dve", t1, x1, pp, prv, TCOS, cs_, ALU.mult)
        yield
        tt("dve", t2, x2, pp, piv, TSIN, sn_, ALU.mult)
        yield
        tt("dve", t1, x1, t1, x1, t2, x2, ALU.add)
        yield
        tt("dve", t3, x3, pp, piv, TCOS, cs_, ALU.mult)
        yield
        tt("dve", t4, x4, pp, prv, TSIN, sn_, ALU.mult)
        yield
        tt("dve", t3, x3, t3, x3, t4, x4, ALU.subtract)
        yield
        si = S5I.ap[:, e]
        for hf in range(nhalf):
            if hf == 1:
                c63 = TCOS.ap[:, 4 * k:4 * k + 4, nh - 1]
                s63 = TSIN.ap[:, 4 * k:4 * k + 4, nh - 1]
                wr = x2[:, :, 0, nh - 1]
                wi = x4[:, :, 0, nh - 1]
                tt("dve", S5I, si[:, 0, :], t2, wr, TCOS, c63, ALU.mult)
                yield
                tt("dve", S5I, si[:, 1, :], t4, wi, TSIN, s63, ALU.mult)
                yield
                tt("dve", S5I, si[:, 2, :], t4, wi, TCOS, c63, ALU.mult)
                yield
                tt("dve", S5I, si[:, 3, :], t2, wr, TSIN, s63, ALU.mult)
                yield
                tt("dve", S5I, si[:, 4, :], S5I, si[:, 0, :], S5I, si[:, 1, :], ALU.subtract)
                yield
                tt("dve", S5I, si[:, 5, :], S5I, si[:, 2, :], S5I, si[:, 3, :], ALU.add)
                yield
            for mi in range(4):
                m = 4 * k + mi
                lamb = s5[:, MAG, m:m + 1].to_broadcast([128, nh])
                if hf == 0:
                    i_re, i_im, i_t = CAR.ap[:, 0, m:m + 1], CAR.ap[:, 1, m:m + 1], CAR
                else:
                    i_re, i_im, i_t = si[:, 4, mi:mi + 1], si[:, 5, mi:mi + 1], S5I
                S.op("dve", lambda e_, o=x2[:, mi, hf, :], d=x1[:, mi, hf, :], lamb=lamb, ini=i_re:
                     e_.tensor_tensor_scan(o, lamb, d, ini, ALU.mult, ALU.add), reads=[S5P, t1, i_t], writes=[t2])
                yield
                S.op("dve", lambda e_, o=x4[:, mi, hf, :], d=x3[:, mi, hf, :], lamb=lamb, ini=i_im:
                     e_.tensor_tensor_scan(o, lamb, d, ini, ALU.mult, ALU.add), reads=[S5P, t3, i_t], writes=[t4])
                yield
        tt("dve", t1, x1, t2, x2, TCOS, cs_, ALU.mult)
        yield
        tt("dve", t3, x3, t4, x4, TSIN, sn_, ALU.mult)
        yield
        tt("dve", t1, x1, t1, x1, t3, x3, ALU.subtract)
        yield
        tt("dve", t3, x3, t4, x4, TCOS, cs_, ALU.mult)
        yield
        tt("dve", t2, x2, t2, x2, TSIN, sn_, ALU.mult)
        yield
        tt("dve", t3, x3, t3, x3, t2, x2, ALU.add)
        yield
        SRE, SIM = SREr.ts[e], SIMr.ts[e]
        cp("act", SRE, v4(SRE.ap.rearrange("p m t -> p (m t)")), t1, x1)
        cp("act", SIM, v4(SIM.ap.rearrange("p m t -> p (m t)")), t3, x3)
        cp("act", CAR, CAR.ap[:, 0, 4 * k:4 * k + 4], t1, x1[:, :, nhalf - 1, nh - 1])
        cp("act", CAR, CAR.ap[:, 1, 4 * k:4 * k + 4], t3, x3[:, :, nhalf - 1, nh - 1])
        yield
        ps = PS.next()
        for mi in range(4):
            m = 4 * k + mi
            mm(ps, ps.ap[:, 0:n], CPR, CPR.ap[:, m, :], SRE, SRE.ap[:, mi, 0:n], start=(mi == 0), stop=False)
            mm(ps, ps.ap[:, 0:n], CPI, CPI.ap[:, m, :], SIM, SIM.ap[:, mi, 0:n], start=False, stop=(mi == 3))
        yield
        y, g1 = t2, t4
        stt(y, y.ap[:, 0:n], HT, HT.ap[:, k, 0:n], pc(p_dsk, k), ps, ps.ap[:, 0:n], ALU.mult, ALU.add, extra=[PRM])
        yield
        act(g1, g1.ap[:, 0:n], y, y.ap[:, 0:n], AF.Square)
        yield
        ts("dve", g1, g1.ap[:, 0:n], g1, g1.ap[:, 0:n], 0.044715, 1.0, ALU.mult, ALU.add)
        yield
        tt("dve", g1, g1.ap[:, 0:n], g1, g1.ap[:, 0:n], y, y.ap[:, 0:n], ALU.mult)
        yield
        act(g1, g1.ap[:, 0:n], g1, g1.ap[:, 0:n], AF.Sigmoid, scale=1.5957691216057308)
        yield
        tt("dve", YCAT, YCAT.ap[:, k, 0:n], g1, g1.ap[:, 0:n], y, y.ap[:, 0:n], ALU.mult)
        yield

    def s5_layer(n):
        rmsnorm(n, g_mix[1])
        if not S5SETS:
            S5SETS.append(tmpF.ts[0:4])
            x0 = T(RWC.ts[0].ap[:, 0:4, :].rearrange("p a b -> p (a b)"))
            x1_ = T(RWC.ts[0].ap[:, 4:8, :].rearrange("p a b -> p (a b)"))
            S5SETS.append([tmpF.ts[4], tmpF.ts[5], x0, x1_])
        for kp in range(0, 8, 2):
            gens = [s5_chain(n, kp, 0), s5_chain(n, kp + 1, 1)]
            if _os.environ.get("KS5SEQ", "0") == "1":
                for g_ in gens:
                    for _ in g_:
                        pass
                continue
            while gens:
                for g_ in list(gens):
                    try:
                        next(g_)
                    except StopIteration:
                        gens.remove(g_)
        sls = {}
        for q in (0, 2, 1, 3):
            sls[q] = load_piece("glu", q)
            if q in (2, 3):
                qa = q - 2
                wa = sls[qa].ap.rearrange("p (k c) -> p k c", k=8)
                wb = sls[q].ap.rearrange("p (k c) -> p k c", k=8)
                for loc in range(4):
                    j = 4 * qa + loc
                    pa = PS.next()
                    pb = PS.next()
                    for k in range(8):
                        mm(pa, pa.ap[:, 0:n], sls[qa], wa[:, k, loc * 128:(loc + 1) * 128], YCAT, YCAT.ap[:, k, 0:n],
                           start=(k == 0), stop=(k == 7))
                    for k in range(8):
                        mm(pb, pb.ap[:, 0:n], sls[q], wb[:, k, loc * 128:(loc + 1) * 128], YCAT, YCAT.ap[:, k, 0:n],
                           start=(k == 0), stop=(k == 7))
                    sg = tmpF.next()
                    act(sg, sg.ap[:, 0:n], pb, pb.ap[:, 0:n], AF.Sigmoid)
                    tt("dve", sg, sg.ap[:, 0:n], sg, sg.ap[:, 0:n], pa, pa.ap[:, 0:n], ALU.mult)
                    tt("dve", XT, XT.ap[:, j, 0:n], XT, XT.ap[:, j, 0:n], sg, sg.ap[:, 0:n], ALU.add)

    RWC = rot(2, [128, 10, 128], F32, "rwc")
    SIGW, AA, KK, KKN, KMOD, BBv, CS, DD, DINV, DPREV = range(10)
    GGB = sb([128, 2, 4, 128], F32, "ggb")
    DLC = sb([128, 4], F32, "dlc")
    KR = sb([128, 4, 2, 128], BF16, "kr")
    BT = sb([128, 4, 128], BF16, "bt")
    KT_ = sb([128, 4, 128], BF16, "kt")
    BH = sb([128, 4, 128], BF16, "bh")
    KH = sb([128, 4, 128], BF16, "kh")
    VBF = sb([128, 4, 128], BF16, "vbf")
    TOK = sb([128, 3, 512], BF16, "tok")
    YTOK = sb([128, 8, 64], F32, "ytok")
    TWB = sb([128, 128], BF16, "twb")
    SGB = sb([128, 128], BF16, "sgb")
    RMASK = sb([128, 128], F32, "rmask")
    memset("dve", RMASK, RMASK.ap, 1.0)
    memset("dve", RMASK, RMASK.ap[:, 0:1], 0.0)
    MATS = rot(2, [128, 5, 2, 128], BF16, "mats")
    LV = rot(3, [128, 2, 2, 128], BF16, "lv")
    PP = rot(3, [128, 2, 128], BF16, "pp")
    ZU = rot(2, [128, 2, 2, 64], BF16, "zu")

    def rwkv(n, s_in_proj_done=None):
        for j in range(14):
            t0 = tmpF.next()
            ts("dve", t0, t0.ap[:, 0:n], U, U.ap[:, j, 1:n + 1], pc(p_omm, j), None, ALU.mult, extra=[PRM])
            stt(UM, UM.ap[:, j, 0:n], U, U.ap[:, j, 0:n], pc(p_mu, j), t0, t0.ap[:, 0:n], ALU.mult, ALU.add,
                extra=[PRM])
            if j % 4 == 3:
                yield
        tw = TWB
        act(tw, tw.ap[0:64, 0:n], UM, UM.ap[0:64, 12, 0:n], AF.Tanh)
        cp("act", tw, tw.ap[64:128, 0:n], UM, UM.ap[64:128, 12, 0:n])
        sg = SGB
        act(sg, sg.ap[:, 0:n], UM, UM.ap[:, 13, 0:n], AF.Sigmoid)
        for c in range(4):
            rwc = RWC.next()
            ps = PS.next()
            mm(ps, ps.ap[:, 0:n], LR, LR.ap[0:64, c * 128:(c + 1) * 128], tw, tw.ap[0:64, 0:n])
            act(rwc, rwc.ap[:, SIGW, 0:n], ps, ps.ap[:, 0:n], AF.Sigmoid, bias=pc(p_w0, c), extra=[PRM])
            ps = PS.next()
            mm(ps, ps.ap[:, 0:n], LR, LR.ap[64:128, c * 128:(c + 1) * 128], tw, tw.ap[64:128, 0:n])
            act(rwc, rwc.ap[:, AA, 0:n], ps, ps.ap[:, 0:n], AF.Sigmoid, bias=pc(p_a0, c), extra=[PRM])
            ps = PS.next()
            mm(ps, ps.ap[:, 0:n], G2, G2.ap[:, c * 128:(c + 1) * 128], sg, sg.ap[:, 0:n])
            cp("act", GGB, GGB.ap[:, 0, c, 0:n], ps, ps.ap[:, 0:n])
            yield
            ts("dve", rwc, rwc.ap[:, KK, 0:n], UM, UM.ap[:, 4 + c, 0:n], pc(p_kk, c), None, ALU.mult, extra=[PRM])
            sq = tmpB.next()
            act(sq, sq.ap[:, 0:n], rwc, rwc.ap[:, KK, 0:n], AF.Square)
            ps = PS.next()
            mm(ps, ps.ap[:, 0:n], CB, BDB, sq, sq.ap[:, 0:n])
            r_ = tmpF.next()
            act(r_, r_.ap[:, 0:n], ps, ps.ap[:, 0:n], AF.Sqrt, bias=EPSC.ap[:, 1:2], extra=[EPSC])
            recip(r_, r_.ap[:, 0:n], r_, r_.ap[:, 0:n])
            tt("dve", rwc, rwc.ap[:, KKN, 0:n], rwc, rwc.ap[:, KK, 0:n], r_, r_.ap[:, 0:n], ALU.mult)
            t1 = tmpF.next()
            ts("dve", t1, t1.ap[:, 0:n], rwc, rwc.ap[:, AA, 0:n], pc(p_ka, c), pc(p_omka, c), ALU.mult, ALU.add,
               extra=[PRM])
            tt("dve", rwc, rwc.ap[:, KMOD, 0:n], UM, UM.ap[:, 4 + c, 0:n], t1, t1.ap[:, 0:n], ALU.mult)
            tt("dve", rwc, rwc.ap[:, BBv, 0:n], rwc, rwc.ap[:, KKN, 0:n], rwc, rwc.ap[:, AA, 0:n], ALU.mult)
            ts("dve", rwc, rwc.ap[:, SIGW, 0:n], rwc, rwc.ap[:, SIGW, 0:n], -0.6065306597126334, None, ALU.mult)
            S.op("dve", lambda e, rwc=rwc: e.tensor_tensor_scan(rwc.ap[:, CS, 0:n], RMASK.ap[:, 0:n],
                                                               rwc.ap[:, SIGW, 0:n], 0.0, ALU.mult, ALU.add),
                 reads=[RMASK, rwc], writes=[rwc])
            act(rwc, rwc.ap[:, DD, 0:n], rwc, rwc.ap[:, CS, 0:n], AF.Exp)
            cp("pool", DLC, DLC.ap[:, c:c + 1], rwc, rwc.ap[:, DD, n - 1:n])
            act(rwc, rwc.ap[:, DINV, 0:n], rwc, rwc.ap[:, CS, 0:n], AF.Exp, scale=-1.0)
            t2 = tmpF.next()
            tt("dve", t2, t2.ap[:, 0:n], rwc, rwc.ap[:, CS, 0:n], rwc, rwc.ap[:, SIGW, 0:n], ALU.subtract)
            act(rwc, rwc.ap[:, DPREV, 0:n], t2, t2.ap[:, 0:n], AF.Exp)
            tt("dve", KR, KR.ap[:, c, 1, 0:n], UM, UM.ap[:, c, 0:n], rwc, rwc.ap[:, DD, 0:n], ALU.mult)
            tt("dve", KR, KR.ap[:, c, 0, 0:n], rwc, rwc.ap[:, KKN, 0:n], rwc, rwc.ap[:, DPREV, 0:n], ALU.mult)
            tt("dve", BT, BT.ap[:, c, 0:n], rwc, rwc.ap[:, BBv, 0:n], rwc, rwc.ap[:, DINV, 0:n], ALU.mult)
            tt("dve", KT_, KT_.ap[:, c, 0:n], rwc, rwc.ap[:, KMOD, 0:n], rwc, rwc.ap[:, DINV, 0:n], ALU.mult)
            ts("dve", BH, BH.ap[:, c, 0:n], BT, BT.ap[:, c, 0:n], rwc.ap[:, DD, n - 1:n], None, ALU.mult, extra=[rwc])
            ts("dve", KH, KH.ap[:, c, 0:n], KT_, KT_.ap[:, c, 0:n], rwc.ap[:, DD, n - 1:n], None, ALU.mult,
               extra=[rwc])
            cp("act", VBF, VBF.ap[:, c, 0:n], UM, UM.ap[:, 8 + c, 0:n])
            pr_ = tmpB.next()
            stt(pr_, pr_.ap[:, 0:n], UM, UM.ap[:, c, 0:n], pc(p_rk, c), rwc, rwc.ap[:, KMOD, 0:n], ALU.mult, ALU.mult,
                extra=[PRM])
            ps = PS.next()
            mm(ps, ps.ap[:, 0:n], CB, BDB, pr_, pr_.ap[:, 0:n])
            tt("dve", GGB, GGB.ap[:, 1, c, 0:n], ps, ps.ap[:, 0:n], UM, UM.ap[:, 8 + c, 0:n], ALU.mult)
            yield
        dbg("rwkv_a")
        for qi, src in enumerate((VBF, BH, KH)):
            ps = PS.next()
            pb_ = ps.ap.bitcast(BF16)
            for c in range(4):
                tr(ps, pb_[0:n, c * 128:(c + 1) * 128], src, src.ap[:, c, 0:n], CB, IDB)
            cp("act", TOK, TOK.ap[0:n, qi, :], ps, pb_[0:n, 0:512])
            yield
        dbg("rwkv_b")
        nlev = max(1, int(np.ceil(np.log2(n))))
        for c in range(4):
            mt = MATS.next()
            ps1 = PS.next()
            ps2 = PS.next()
            ps3 = PS.next()
            v1 = ps1.ap[0:n].rearrange("p (h r) -> p h r", h=2)[:, :, 0:2 * n].rearrange("p h (x t) -> p h x t", x=2)
            v2 = ps2.ap[0:n].rearrange("p (h r) -> p h r", h=2)[:, :, 0:2 * n].rearrange("p h (x t) -> p h x t", x=2)
            v3 = ps3.ap[0:n].rearrange("p (h r) -> p h r", h=2)[:, :, 0:n]
            for h2 in range(2):
                pb = 64 * h2
                krv = KR.ap[pb:pb + 64, c, :, 0:n]
                mm(ps1, v1[:, h2], BT, BT.ap[pb:pb + 64, c, 0:n], KR, krv)
                mm(ps2, v2[:, h2], KT_, KT_.ap[pb:pb + 64, c, 0:n], KR, krv)
                mm(ps3, v3[:, h2], KR, KR.ap[pb:pb + 64, c, 0, 0:n], BT, BT.ap[pb:pb + 64, c, 0:n])
            msu = M_SU[0:n, 0:n].unsqueeze(1).to_broadcast([n, 2, n])
            mu_ = M_U[0:n, 0:n].unsqueeze(1).to_broadcast([n, 2, n])
            msl = M_SL[0:n, 0:n].unsqueeze(1).to_broadcast([n, 2, n])
            m_ = mt.ap
            stt(mt, m_[0:n, 0, :, 0:n], ps1, v1[:, :, 0, :], -1.0, CONST, msu, ALU.mult, ALU.mult)
            tt("dve", mt, m_[0:n, 2, :, 0:n], ps1, v1[:, :, 1, :], CONST, mu_, ALU.mult)
            tt("dve", mt, m_[0:n, 3, :, 0:n], ps2, v2[:, :, 0, :], CONST, msu, ALU.mult)
            tt("dve", mt, m_[0:n, 4, :, 0:n], ps2, v2[:, :, 1, :], CONST, mu_, ALU.mult)
            stt(mt, m_[0:n, 1, :, 0:n], ps3, v3, -1.0, CONST, msl, ALU.mult, ALU.mult)
            dbg("rwkv_c")
            yield
            P = PP.next()
            idb = IDF[0:n, 0:n].unsqueeze(1).to_broadcast([n, 2, n])
            tt("dve", P, P.ap[0:n, :, 0:n], mt, m_[0:n, 0, :, 0:n], CONST, idb, ALU.add)
            Mt, Ma, Nt, Na = mt, m_[0:n, 0, :, 0:n], mt, m_[0:n, 1, :, 0:n]
            for lev in range(1, nlev):
                lv = LV.next()
                pA = PS.next()
                pB = PS.next()
                vA = pA.ap[0:n, 0:2 * n].rearrange("p (h t) -> p h t", h=2)
                vB = pB.ap[0:n, 0:2 * n].rearrange("p (h t) -> p h t", h=2)
                for h2 in range(2):
                    mm(pA, vA[:, h2], Nt, Na[:, h2], Mt, Ma[:, h2])
                    mm(pB, vB[:, h2], Mt, Ma[:, h2], Nt, Na[:, h2])
                cp("act", lv, lv.ap[0:n, 0, :, 0:n], pA, vA)
                cp("act", lv, lv.ap[0:n, 1, :, 0:n], pB, vB)
                Mt, Ma, Nt, Na = lv, lv.ap[0:n, 0, :, 0:n], lv, lv.ap[0:n, 1, :, 0:n]
                pC = PS.next()
                vC = pC.ap[0:n, 0:2 * n].rearrange("p (h t) -> p h t", h=2)
                for h2 in range(2):
                    mm(pC, vC[:, h2], Nt, Na[:, h2], P, P.ap[0:n, h2, 0:n])
                Pn = PP.next()
                tt("dve", Pn, Pn.ap[0:n, :, 0:n], pC, vC, P, P.ap[0:n, :, 0:n], ALU.add)
                P = Pn
                yield
            dbg("rwkv_d")
            zu = ZU.next()
            pz = PS.next()
            vz = pz.ap[0:n].rearrange("p (h r) -> p h r", h=2)[:, :, 0:64]
            for h2 in range(2):
                pb = 64 * h2
                h = 2 * c + h2
                mm(pz, vz[:, h2], KR, KR.ap[pb:pb + 64, c, 0, 0:n], HSB, HSB.ap[pb:pb + 64, c, :], start=True,
                   stop=False)
                mm(pz, vz[:, h2], mt, m_[0:n, 3, h2, 0:n], TOK, TOK.ap[0:n, 0, h * 64:(h + 1) * 64], start=False,
                   stop=True)
            S.op("act", lambda e, zu=zu, vz=vz: e.mul(zu.ap[0:n, 0], vz, -1.0), reads=[pz], writes=[zu])
            pu = PS.next()
            vu = pu.ap[0:n, 0:128].rearrange("p (h v) -> p h v", h=2)
            for h2 in range(2):
                mm(pu, vu[:, h2], P, P.ap[0:n, h2, 0:n], zu, zu.ap[0:n, 0, h2])
            cp("act", zu, zu.ap[0:n, 1], pu, vu)
            yield
            py = PS.next()
            vy = py.ap[0:n].rearrange("p (h r) -> p h r", h=2)[:, :, 0:64]
            for h2 in range(2):
                pb = 64 * h2
                h = 2 * c + h2
                mm(py, vy[:, h2], KR, KR.ap[pb:pb + 64, c, 1, 0:n], HSB, HSB.ap[pb:pb + 64, c, :], start=True,
                   stop=False)
                mm(py, vy[:, h2], mt, m_[0:n, 2, h2, 0:n], zu, zu.ap[0:n, 1, h2], start=False, stop=False)
                mm(py, vy[:, h2], mt, m_[0:n, 4, h2, 0:n], TOK, TOK.ap[0:n, 0, h * 64:(h + 1) * 64], start=False,
                   stop=True)
            cp("act", YTOK, YTOK.ap[0:n, 2 * c:2 * c + 2, :], py, vy)
            ph = PS.next()
            vh = ph.ap[:, 0:128].rearrange("p (h v) -> p h v", h=2)
            for h2 in range(2):
                h = 2 * c + h2
                mm(ph, vh[:, h2], TOK, TOK.ap[0:n, 1, c * 128:(c + 1) * 128], zu, zu.ap[0:n, 1, h2], start=True,
                   stop=False)
                mm(ph, vh[:, h2], TOK, TOK.ap[0:n, 2, c * 128:(c + 1) * 128], TOK,
                   TOK.ap[0:n, 0, h * 64:(h + 1) * 64], start=False, stop=True)
            for h2 in range(2):
                pb = 64 * h2
                stt(HS, HS.ap[pb:pb + 64, c, :], HS, HS.ap[pb:pb + 64, c, :], DLC.ap[pb:pb + 64, c:c + 1], ph,
                    vh[pb:pb + 64, h2], ALU.mult, ALU.add, extra=[DLC])
            yield
        cp("act", HSB, HSB.ap, HS, HS.ap)
        dbg("rwkv_e")
        st_ = tmpF.next()
        sq = tmpF.next()
        S.op("dve", lambda e: e.tensor_reduce(st_.ap[0:n, 0:8], YTOK.ap[0:n], AX.X, ALU.add), reads=[YTOK],
             writes=[st_])
        sqv = sq.ap[0:n, 0:512].rearrange("p (h v) -> p h v", h=8)
        act(sq, sqv, YTOK, YTOK.ap[0:n], AF.Square)
        S.op("dve", lambda e: e.tensor_reduce(st_.ap[0:n, 8:16], sqv, AX.X, ALU.add), reads=[sq], writes=[st_])
        ts("dve", st_, st_.ap[0:n, 0:16], st_, st_.ap[0:n, 0:16], 1.0 / 64, None, ALU.mult)
        tt("dve", st_, st_.ap[0:n, 16:24], st_, st_.ap[0:n, 0:8], st_, st_.ap[0:n, 0:8], ALU.mult)
        tt("dve", st_, st_.ap[0:n, 16:24], st_, st_.ap[0:n, 8:16], st_, st_.ap[0:n, 16:24], ALU.subtract)
        act(st_, st_.ap[0:n, 16:24], st_, st_.ap[0:n, 16:24], AF.Sqrt, bias=EPSC.ap[0:n, 2:3], extra=[EPSC])
        recip(st_, st_.ap[0:n, 16:24], st_, st_.ap[0:n, 16:24])
        yn = tmpF.next()
        ynv = yn.ap[0:n, 0:512].rearrange("p (h v) -> p h v", h=8)
        tt("dve", yn, ynv, YTOK, YTOK.ap[0:n], st_, st_.ap[0:n, 0:8].unsqueeze(2).to_broadcast([n, 8, 64]),
           ALU.subtract)
        tt("dve", yn, ynv, yn, ynv, st_, st_.ap[0:n, 16:24].unsqueeze(2).to_broadcast([n, 8, 64]), ALU.mult)
        ps = PS.next()
        for c in range(4):
            tr(ps, ps.ap[:, c * 128:c * 128 + n], yn, yn.ap[0:n, c * 128:(c + 1) * 128], CONST, IDF[0:n, 0:n])
        for c in range(4):
            t3 = tmpF.next()
            ts("dve", t3, t3.ap[:, 0:n], ps, ps.ap[:, c * 128:c * 128 + n], pc(p_lnw, c), pc(p_lnb, c), ALU.mult,
               ALU.add, extra=[PRM])
            tt("dve", t3, t3.ap[:, 0:n], t3, t3.ap[:, 0:n], GGB, GGB.ap[:, 1, c, 0:n], ALU.add)
            tt("dve", YCAT, YCAT.ap[:, c, 0:n], t3, t3.ap[:, 0:n], GGB, GGB.ap[:, 0, c, 0:n], ALU.mult)

    KBLK = Rot([T(RRK.ap[:, 0:512]), T(RRK.ap[:, 512:1024])])
    VBLK = Rot([T(RRK.ap[:, 1024:1536]), T(RRK.ap[:, 1536:2048])])
    for t_ in KBLK.ts + VBLK.ts:
        S.op("pool", lambda e, t_=t_: e.memset(t_.ap[:, 0:1], 0.0), writes=[RRK, t_])
    KBB = rot(2, [128, 512], BF16, "kbb")
    VP = rot(2, [128, 8, 65], BF16, "vp")
    for t_ in VP.ts:
        memset("dve", t_, t_.ap[:, :, 64:65], 1.0)
    KTB = rot(2, [128, 4, 128], BF16, "ktb")
    QN = sb([128, 4, 128], BF16, "qn")
    KNF = sb([128, 4, 128], F32, "knf")
    KNB = sb([128, 4, 128], BF16, "knb")
    BIAS = rot(3, [128, 8], F32, "bias")
    ACC = sb([128, 8], F32, "lfacc")
    ET = rot(4, [128, 128], BF16, "et")
    MUB = sb([128, 128], BF16, "mub")
    cp("dve", MUB, MUB.ap, CONST, M_U)
    LFO = rot(2, [128, 8], F32, "lfo")
    RRF = RRI.ap.bitcast(F32)
    KOUT = Rot([T(RRF[:, 0:512])])
    XBD = T(RRF[:, 512:640])
    WKT = T(RRF[:, 640:896].rearrange("p (c j) -> p c j", c=4))
    for t_ in (KOUT.ts[0], XBD, WKT):
        S.op("pool", lambda e, t_=t_: e.memset(t_.ap, 0.0), writes=[RRI, t_])

    def attend(n, nk, kt_t, kt_ap, vp, bi, diag, first, last):
        for h in range(8):
            c, pb = h // 2, 64 * (h % 2)
            ps = PS.next()
            mm(ps, ps.ap[0:nk, 0:n], kt_t, kt_ap[pb:pb + 64, c, 0:nk], QN, QN.ap[pb:pb + 64, c, 0:n])
            et = ET.next()
            act(et, et.ap[0:nk, 0:n], ps, ps.ap[0:nk, 0:n], AF.Exp, bias=bi.ap[0:nk, h:h + 1], extra=[bi])
            if diag:
                tt("pool", et, et.ap[0:nk, 0:n], et, et.ap[0:nk, 0:n], MUB, MUB.ap[0:nk, 0:n], ALU.mult)
            acc = PSACC[h // 4]
            av = acc.ap[0:n, 0:260].rearrange("p (h d) -> p h d", h=4)
            st_flag = first and (h % 4 == 0)
            S.op("pe", lambda e, av=av, et=et, vp=vp, h=h, st_flag=st_flag: e.matmul(
                av[:, h % 4, :], et.ap[0:nk, 0:n], vp.ap[0:nk, h, :], start=st_flag, stop=last,
                skip_group_check=True), reads=[et, vp], writes=[acc])
            if h % 2 == 1:
                yield

    def fox(n, blk, kv_src, out_k, out_v, out_lf, row0):
        for j in range(8):
            sq = tmpB.next()
            act(sq, sq.ap[:, 0:n], QKR, QKR.ap[:, j, 0:n], AF.Square)
            ps = PS.next()
            mm(ps, ps.ap[:, 0:n], CB, BDB, sq, sq.ap[:, 0:n])
            r_ = tmpF.next()
            act(r_, r_.ap[:, 0:n], ps, ps.ap[:, 0:n], AF.Sqrt, scale=1.0 / 64, bias=EPSC.ap[:, 0:1], extra=[EPSC])
            recip(r_, r_.ap[:, 0:n], r_, r_.ap[:, 0:n])
            if j < 4:
                stt(QN, QN.ap[:, j, 0:n], QKR, QKR.ap[:, j, 0:n], pc(p_gq, 0), r_, r_.ap[:, 0:n], ALU.mult, ALU.mult,
                    extra=[PRM])
            else:
                stt(KNF, KNF.ap[:, j - 4, 0:n], QKR, QKR.ap[:, j, 0:n], pc(p_gk, 0), r_, r_.ap[:, 0:n], ALU.mult,
                    ALU.mult, extra=[PRM])
            yield
        cp("act", KNB, KNB.ap[:, :, 0:n], KNF, KNF.ap[:, :, 0:n])
        ps = PS.next()
        for c in range(4):
            tr(ps, ps.ap[0:n, c * 128:(c + 1) * 128], KNF, KNF.ap[:, c, 0:n], CONST, IDF)
        ko = KOUT.next()
        cp("act", ko, ko.ap[0:n], ps, ps.ap[0:n, 0:512])
        S.dma("pool", out_k.ap[row0:row0 + n], ko.ap[0:n], reads=[ko], writes=[out_k])
        vcur = VB.ts[(VB.i - 1) % len(VB.ts)]
        S.dma("pool", out_v.ap[row0:row0 + n], vcur.ap[0:n], reads=[vcur], writes=[out_v])
        yield
        lf = tmpF.next()
        tt("dve", lf, lf.ap[0:n, 0:8], LFT, LFT.ap[0:n, :], FB, FB.ap[0:n, :], ALU.add)
        act(lf, lf.ap[0:n, 0:8], lf, lf.ap[0:n, 0:8], AF.Exp, scale=-1.0)
        act(lf, lf.ap[0:n, 0:8], lf, lf.ap[0:n, 0:8], AF.Ln, bias=EPSC.ap[0:n, 3:4], extra=[EPSC])
        lo = LFO.next()
        ts("dve", lo, lo.ap[0:n], lf, lf.ap[0:n, 0:8], -1.0, None, ALU.mult)
        S.dma("pool", out_lf.ap[row0:row0 + n], lo.ap[0:n], reads=[lo], writes=[out_lf])
        ps = PS.next()
        mm(ps, ps.ap[0:n, 0:8], CONST, M_SL[0:n, 0:n], lo, lo.ap[0:n])
        mm(ps, ps.ap[:, 8:16], CONST, ONESF[0:n, :], lo, lo.ap[0:n])
        cp("dve", SUF, SUF.ap[0:n, blk, :], ps, ps.ap[0:n, 0:8])
        cp("dve", TOT, TOT.ap[:, blk, :], ps, ps.ap[:, 8:16])
        cp("dve", ACC, ACC.ap, ps, ps.ap[:, 8:16])
        yield
        vp = VP.next()
        cp("dve", vp, vp.ap[0:n, :, 0:64], vcur, vcur.ap[0:n].rearrange("p (h d) -> p h d", h=8))
        bi = BIAS.next()
        cp("dve", bi, bi.ap[0:n], SUF, SUF.ap[0:n, blk, :])
        yield from attend(n, n, KNB, KNB.ap, vp, bi, True, True, blk == 0)
        for b in range(blk - 1, -1, -1):
            k_t, k_ap, v_t, v_ap = kv_src(b)
            kb = KBLK.next()
            vb = VBLK.next()
            S.dma("sp", kb.ap, k_ap, reads=[k_t], writes=[kb])
            S.dma("sp", vb.ap, v_ap, reads=[v_t], writes=[vb])
            kbb = KBB.next()
            cp("dve", kbb, kbb.ap, kb, kb.ap)
            vp = VP.next()
            cp("dve", vp, vp.ap[:, :, 0:64], vb, vb.ap.rearrange("p (h d) -> p h d", h=8))
            ps = PS.next()
            pb_ = ps.ap.bitcast(BF16)
            for c in range(4):
                tr(ps, pb_[:, c * 128:(c + 1) * 128], kbb, kbb.ap[:, c * 128:(c + 1) * 128], CB, IDB)
            ktb = KTB.next()
            cp("act", ktb, ktb.ap.rearrange("p c t -> p (c t)"), ps, pb_[:, 0:512])
            bi = BIAS.next()
            tt("dve", bi, bi.ap, SUF, SUF.ap[:, b, :], ACC, ACC.ap, ALU.add)
            tt("dve", ACC, ACC.ap, ACC, ACC.ap, TOT, TOT.ap[:, b, :], ALU.add)
            yield
            yield from attend(n, 128, ktb, ktb.ap, vp, bi, False, False, b == 0)
        yb = tmpF.next()
        ybv = yb.ap[0:n, 0:512].rearrange("p (h d) -> p h d", h=8)
        for half in range(2):
            acc = PSACC[half]
            av = acc.ap[0:n, 0:260].rearrange("p (h d) -> p h d", h=4)
            rc = tmpF.next()
            recip(rc, rc.ap[0:n, 0:4], acc, av[:, :, 64])
            tt("dve", yb, ybv[:, 4 * half:4 * half + 4, :], acc, av[:, :, 0:64], rc,
               rc.ap[0:n, 0:4].unsqueeze(2).to_broadcast([n, 4, 64]), ALU.mult)
        ps = PS.next()
        for c in range(4):
            tr(ps, ps.ap[:, c * 128:c * 128 + n], yb, yb.ap[0:n, c * 128:(c + 1) * 128], CONST, IDF[0:n, 0:n])
        cp("act", YCAT, YCAT.ap[:, 4:8, 0:n], ps, ps.ap[:, 0:512].rearrange("p (c t) -> p c t", c=4)[:, :, 0:n])

    def in_proj(n):
        rmsnorm(n, g_mix[0])
        sls = {}
        for q in range(7):
            sl = load_piece("in", q)
            sls[q] = sl
            wv = sl.ap.rearrange("p (k c) -> p k c", k=8)
            for loc in range(4):
                tile_ = 4 * q + loc
                if tile_ >= 22:
                    continue
                ps = PS.next()
                for k in range(8):
                    mm(ps, ps.ap[:, 0:n], sl, wv[:, k, loc * 128:(loc + 1) * 128], HT, HT.ap[:, k, 0:n],
                       start=(k == 0), stop=(k == 7))
                if tile_ < 14:
                    cp("act", U, U.ap[:, tile_, 1:n + 1], ps, ps.ap[:, 0:n])
                else:
                    cp("act", QKR, QKR.ap[:, tile_ - 14, 0:n], ps, ps.ap[:, 0:n])
        w5 = sls[5].ap.rearrange("p (k c) -> p k c", k=8)
        w6 = sls[6].ap.rearrange("p (k c) -> p k c", k=8)
        ps = PS.next()
        for k in range(8):
            mm(ps, ps.ap[0:n, 0:256], HT, HT.ap[:, k, 0:n], sls[5], w5[:, k, 256:512], start=(k == 0), stop=(k == 7))
        for k in range(8):
            mm(ps, ps.ap[0:n, 256:512], HT, HT.ap[:, k, 0:n], sls[6], w6[:, k, 0:256], start=(k == 0), stop=(k == 7))
        vb = VB.next()
        cp("act", vb, vb.ap[0:n], ps, ps.ap[0:n, 0:512])
        ps = PS.next()
        for k in range(8):
            mm(ps, ps.ap[0:n, 0:8], HT, HT.ap[:, k, 0:n], sls[6], w6[:, k, 256:264], start=(k == 0), stop=(k == 7))
        cp("act", LFT, LFT.ap[0:n], ps, ps.ap[0:n, 0:8])

    def out_proj(n):
        for q in range(2):
            sl = load_piece("out", q)
            wv = sl.ap.rearrange("p (k c) -> p k c", k=8)
            for loc in range(4):
                j = 4 * q + loc
                ps = PS.next()
                for k in range(8):
                    mm(ps, ps.ap[:, 0:n], sl, wv[:, k, loc * 128:(loc + 1) * 128], YCAT, YCAT.ap[:, k, 0:n],
                       start=(k == 0), stop=(k == 7))
                tt("dve", XT, XT.ap[:, j, 0:n], XT, XT.ap[:, j, 0:n], ps, ps.ap[:, 0:n], ALU.add)

    def load_x(n, pieces):
        xin = XIN.next()
        for (r0, nr, st_, sap) in pieces:
            S.dma("sp", xin.ap[r0:r0 + nr], sap, reads=[st_], writes=[xin])
        for half in range(2):
            ps = PS.next()
            for kk_ in range(4):
                k = 4 * half + kk_
                tr(ps, ps.ap[:, kk_ * 128:kk_ * 128 + n], xin, xin.ap[0:n, k * 128:(k + 1) * 128], CONST,
                   IDF[0:n, 0:n])
            cp("act", XT, XT.ap[:, 4 * half:4 * half + 4, 0:n],
               ps, ps.ap[:, 0:512].rearrange("p (c t) -> p c t", c=4)[:, :, 0:n])

    def store_y(n, dst_t, dst_ap, skip):
        xo = XIN.next()
        for half in range(2):
            ps = PS.next()
            for kk_ in range(4):
                k = 4 * half + kk_
                tr(ps, ps.ap[0:n, kk_ * 128:(kk_ + 1) * 128], XT, XT.ap[:, k, 0:n], CONST, IDF)
            cp("act", xo, xo.ap[0:n, 512 * half:512 * half + 512], ps, ps.ap[0:n, 0:512])
        S.dma("pool", dst_ap, xo.ap[skip:n], reads=[xo], writes=[dst_t])


    def bd_transpose(dst_t, dst_ap3, src_t, src_ap3):
        for c in range(4):
            for h2 in range(2):
                pb = 64 * h2
                cp("dve", XBD, XBD.ap[pb:pb + 64, pb:pb + 64], src_t, src_ap3[pb:pb + 64, c, :])
            ps = PS.next()
            tr(ps, ps.ap[:, 0:128], XBD, XBD.ap, CONST, IDF)
            for h2 in range(2):
                pb = 64 * h2
                cp("dve", dst_t, dst_ap3[pb:pb + 64, c, :], ps, ps.ap[pb:pb + 64, pb:pb + 64])

    def run_stream(g, s, L, sample):
        if not sample:
            memset("dve", U, U.ap[:, :, 0:1], 0.0)
            memset("dve", HS, HS.ap, 0.0)
            memset("dve", CAR, CAR.ap, 0.0)
            for l in range(2):
                memset("dve", CC[l], CC[l].ap, 0.0)
            nb_past = 0
        else:
            S.dma("sp", U.ap[:, :, 0], I["state_a_shift"].ap[s].rearrange("(j p) -> p j", p=128),
                  reads=[I["state_a_shift"]], writes=[U], allow_slow_non_contiguous=True)
            S.dma("sp", WKT.ap, I["state_a_wkv"].ap[s].rearrange("(c h2) v j -> (h2 v) c j", h2=2),
                  reads=[I["state_a_wkv"]], writes=[WKT])
            bd_transpose(HS, HS.ap, WKT, WKT.ap)
            t0 = tmpF.next()
            S.dma("sp", t0.ap[:, 0:32], I["state_c_re"].ap[s].rearrange("(m gg) p -> (gg p) m", gg=2),
                  reads=[I["state_c_re"]], writes=[t0], allow_slow_non_contiguous=True)
            S.dma("sp", t0.ap[:, 32:64], I["state_c_im"].ap[s].rearrange("(m gg) p -> (gg p) m", gg=2),
                  reads=[I["state_c_im"]], writes=[t0], allow_slow_non_contiguous=True)
            tt("dve", t0, t0.ap[:, 64:96], t0, t0.ap[:, 0:32], S5P, s5[:, FIR, :], ALU.mult)
            tt("dve", t0, t0.ap[:, 96:128], t0, t0.ap[:, 32:64], S5P, s5[:, FII, :], ALU.mult)
            tt("dve", CAR, CAR.ap[:, 0, :], t0, t0.ap[:, 64:96], t0, t0.ap[:, 96:128], ALU.subtract)
            tt("dve", t0, t0.ap[:, 64:96], t0, t0.ap[:, 0:32], S5P, s5[:, FII, :], ALU.mult)
            tt("dve", t0, t0.ap[:, 96:128], t0, t0.ap[:, 32:64], S5P, s5[:, FIR, :], ALU.mult)
            tt("dve", CAR, CAR.ap[:, 1, :], t0, t0.ap[:, 64:96], t0, t0.ap[:, 96:128], ALU.add)
            for l in range(2):
                for r_ in range(2):
                    S.dma("sp", CC[l].ap[:, :, r_], I["state_ffn_conv"].ap[l, s, r_].rearrange("(j p) -> p j", p=128),
                          reads=[I["state_ffn_conv"]], writes=[CC[l]], allow_slow_non_contiguous=True)
            nb_past = PAST // 128
            lfp = tmpF.next()
            lfv = lfp.ap[:, 0:nb_past * 8].rearrange("p (b h) -> p b h", h=8)
            S.dma("sp", lfv, I["cache_b_logf"].ap[s].rearrange("(b p) h -> p b h", p=128),
                  reads=[I["cache_b_logf"]], writes=[lfp])
            for b0 in range(0, nb_past, 32):
                nb = min(32, nb_past - b0)
                ps = PS.next()
                ps2 = PS.next()
                mm(ps, ps.ap[:, 0:nb * 8], CONST, M_SL, lfp, lfp.ap[:, b0 * 8:(b0 + nb) * 8])
                mm(ps2, ps2.ap[:, 0:nb * 8], CONST, ONESF, lfp, lfp.ap[:, b0 * 8:(b0 + nb) * 8])
                cp("dve", SUF, SUF.ap[:, b0:b0 + nb, :], ps, ps.ap[:, 0:nb * 8].rearrange("p (b h) -> p b h", h=8))
                cp("dve", TOT, TOT.ap[:, b0:b0 + nb, :], ps2, ps2.ap[:, 0:nb * 8].rearrange("p (b h) -> p b h", h=8))
        cp("act", HSB, HSB.ap, HS, HS.ap)
        npass = (L + 127) // 128
        n_last = L - 128 * (npass - 1)
        for p in range(npass):
            n = min(128, L - 128 * p)
            t0_ = 128 * p
            if sample:
                load_x(n, [(0, n, I["x_sample"], I["x_sample"].ap[s, t0_:t0_ + n])])
            elif p == 0:
                xin = XIN.next()
                S.dma("sp", xin.ap[0:16], I["meta"].ap, reads=[I["meta"]], writes=[xin])
                if n > 16:
                    S.dma("sp", xin.ap[16:n], I["x_prompt"].ap[s, 0:n - 16], reads=[I["x_prompt"]], writes=[xin])
                for half in range(2):
                    ps = PS.next()
                    for kk_ in range(4):
                        k = 4 * half + kk_
                        tr(ps, ps.ap[:, kk_ * 128:kk_ * 128 + n], xin, xin.ap[0:n, k * 128:(k + 1) * 128], CONST,
                           IDF[0:n, 0:n])
                    cp("act", XT, XT.ap[:, 4 * half:4 * half + 4, 0:n],
                       ps, ps.ap[:, 0:512].rearrange("p (c t) -> p c t", c=4)[:, :, 0:n])
            else:
                load_x(n, [(0, n, I["x_prompt"], I["x_prompt"].ap[s, t0_ - 16:t0_ - 16 + n])])
            if _os.environ.get("KCNT") == "1" and p == 0 and s == 0:
                def _snap(tag, _last=[None]):
                    cur = dict(S.cnt)
                    nw = {e: sum(1 for it in S.streams[e] if it[0] == "w") for e in S.streams}
                    if _last[0] is not None:
                        print("CNT", tag, {e: cur[e] - _last[0][0][e] for e in cur}, "waits", {e: nw[e] - _last[0][1][e] for e in nw})
                    _last[0] = (cur, nw)
                build._snap = _snap
            else:
                build._snap = lambda tag: None
            build._snap("start")
            in_proj(n)
            build._snap("in_proj")
            dbg("inproj")
            blk = nb_past + p
            if sample:
                def kv_src(b):
                    return (I["cache_b_k"], I["cache_b_k"].ap[s, 128 * b:128 * b + 128],
                            I["cache_b_v"], I["cache_b_v"].ap[s, 128 * b:128 * b + 128])
            else:
                def kv_src(b):
                    return (O[g + "_b_k"], O[g + "_b_k"].ap[s, 128 * b:128 * b + 128],
                            O[g + "_b_v"], O[g + "_b_v"].ap[s, 128 * b:128 * b + 128])
            gens = [rwkv(n), fox(n, blk, kv_src, T_sub(O[g + "_b_k"], s), T_sub(O[g + "_b_v"], s),
                                 T_sub(O[g + "_b_logf"], s), t0_)]
            if _os.environ.get("KSEQ", "1") == "1":
                for g_ in gens:
                    for _ in g_:
                        pass
            else:
                while gens:
                    for g_ in list(gens):
                        try:
                            next(g_)
                        except StopIteration:
                            gens.remove(g_)
            cp("act", U, U.ap[:, :, 0:1], U, U.ap[:, :, n:n + 1])
            build._snap("rwkv+fox")
            dbg("fox")
            out_proj(n)
            build._snap("out_proj")
            dbg("outproj")
            ffn(n, 0)
            build._snap("ffn0")
            dbg("ffn0")
            s5_layer(n)
            build._snap("s5")
            dbg("s5")
            ffn(n, 1)
            build._snap("ffn1")
            if sample:
                store_y(n, O[g + "_y"], O[g + "_y"].ap[s, t0_:t0_ + n], 0)
            elif p == 0:
                if n > 16:
                    store_y(n, O[g + "_y"], O[g + "_y"].ap[s, 0:n - 16], 16)
            else:
                store_y(n, O[g + "_y"], O[g + "_y"].ap[s, t0_ - 16:t0_ - 16 + n], 0)
        S.dma("sp", O[g + "_a_shift"].ap[s].rearrange("(j p) -> p j", p=128), U.ap[:, :, 0], reads=[U],
              writes=[O[g + "_a_shift"]], allow_slow_non_contiguous=True)
        bd_transpose(WKT, WKT.ap, HS, HS.ap)
        S.dma("sp", O[g + "_a_wkv"].ap[s].rearrange("(c h2) v j -> (h2 v) c j", h2=2), WKT.ap, reads=[WKT],
              writes=[O[g + "_a_wkv"]])
        t0 = tmpF.next()
        tt("dve", t0, t0.ap[:, 64:96], CAR, CAR.ap[:, 0, :], S5P, s5[:, FRE, :], ALU.mult)
        tt("dve", t0, t0.ap[:, 96:128], CAR, CAR.ap[:, 1, :], S5P, s5[:, FIM, :], ALU.mult)
        tt("dve", t0, t0.ap[:, 0:32], t0, t0.ap[:, 64:96], t0, t0.ap[:, 96:128], ALU.subtract)
        tt("dve", t0, t0.ap[:, 64:96], CAR, CAR.ap[:, 0, :], S5P, s5[:, FIM, :], ALU.mult)
        tt("dve", t0, t0.ap[:, 96:128], CAR, CAR.ap[:, 1, :], S5P, s5[:, FRE, :], ALU.mult)
        tt("dve", t0, t0.ap[:, 32:64], t0, t0.ap[:, 64:96], t0, t0.ap[:, 96:128], ALU.add)
        S.dma("sp", O[g + "_c_re"].ap[s].rearrange("(m gg) p -> (gg p) m", gg=2), t0.ap[:, 0:32], reads=[t0],
              writes=[O[g + "_c_re"]], allow_slow_non_contiguous=True)
        S.dma("sp", O[g + "_c_im"].ap[s].rearrange("(m gg) p -> (gg p) m", gg=2), t0.ap[:, 32:64], reads=[t0],
              writes=[O[g + "_c_im"]], allow_slow_non_contiguous=True)
        for l in range(2):
            for r_ in range(2):
                S.dma("sp", O[g + "_ffn_conv"].ap[l, s, r_].rearrange("(j p) -> p j", p=128), CC[l].ap[:, :, r_],
                      reads=[CC[l]], writes=[O[g + "_ffn_conv"]], allow_slow_non_contiguous=True)

    class StopBuild(Exception):
        pass

    def dbg(stage):
        import os
        if os.environ.get("KDBG") != stage:
            return
        base = dict(XT=(XT, XT.ap), U=(U, U.ap), QKR=(QKR, QKR.ap))
        if stage in ("rwkv", "fox"):
            base.update(UM=(UM, UM.ap), GGB=(GGB, GGB.ap), YTOK=(YTOK, YTOK.ap), HS=(HS, HS.ap),
                        YCAT=(YCAT, YCAT.ap[:, 0:4] if stage == "rwkv" else YCAT.ap), KR=(KR, KR.ap), BT=(BT, BT.ap),
                        KT=(KT_, KT_.ap), BH=(BH, BH.ap), KH=(KH, KH.ap), TOK=(TOK, TOK.ap), DLC=(DLC, DLC.ap))
        if stage == "fox":
            base.update(QN=(QN, QN.ap), KNF=(KNF, KNF.ap))
        if stage == "s5":
            base.update(CAR=(CAR, CAR.ap), TCOS=(TCOS, TCOS.ap), TSIN=(TSIN, TSIN.ap), S5P=(S5P, S5P.ap[:, 0:10]),
                        YCAT=(YCAT, YCAT.ap))
        dumps = base
        for nm, (t, tap) in dumps.items():
            shp = list(tap.shape)
            d = T(nc.dram_tensor("dbg_" + nm, shp, tap.dtype, kind="ExternalOutput").ap())
            S.dma("sp", d.ap, tap, reads=[t], writes=[d])
        raise StopBuild()

    class T_sub:
        def __init__(self, parent, s):
            self.p = parent
            self.ap = parent.ap[s]

        w = property(lambda self: self.p.w, lambda self, v: setattr(self.p, "w", v))
        r = property(lambda self: self.p.r, lambda self, v: setattr(self.p, "r", v))

    try:
        for s in range(NSP):
            run_stream("p", s, LP, False)
        for s in range(NSS):
            run_stream("s", s, LS, True)
    except StopBuild:
        pass
    print('SBUF bytes remaining', nc.sbuf_bytes_remaining, 'ops', S.nops, S.cnt)
    S.emit()
    return nc, S


_CACHE = {}

WNAMES = ["meta", "norm_mix", "norm_ffn", "w_in_e", "a_mu", "a_w0", "a_w2", "a_a0", "a_a2", "a_g2", "a_kk", "a_ka",
          "a_rk", "a_ln_w", "a_ln_b", "b_fbias", "b_qnorm", "b_knorm", "w_out_e", "c_lam_re", "c_lam_im", "c_log_dt",
          "c_b_re", "c_b_im", "c_c_re", "c_c_im", "c_d", "w_glu", "w_up", "conv_w", "conv_b", "w_down"]
SQUEEZE = {"w_in_e", "a_mu", "a_w0", "a_w2", "a_a0", "a_a2", "a_g2", "a_kk", "a_ka", "a_rk", "a_ln_w", "a_ln_b",
           "b_fbias", "b_qnorm", "b_knorm", "w_out_e", "c_lam_re", "c_lam_im", "c_log_dt", "c_b_re", "c_b_im",
           "c_c_re", "c_c_im", "c_d", "w_glu"}


def kernel(**inp):
    ncores = inp.pop("_ncores", NCORES)
    B = inp["x_prompt"].shape[0]
    SEQ = inp["x_prompt"].shape[1]
    BS = inp["x_sample"].shape[0]
    LS = inp["x_sample"].shape[1]
    PAST = inp["cache_b_k"].shape[2]
    NSP, NSS = B // ncores, BS // ncores
    LP = SEQ + 16
    key = (NSP, LP, NSS, LS, PAST)
    if key not in _CACHE:
        _CACHE[key] = build(*key)
    nc, S = _CACHE[key]
    f = lambda a: np.ascontiguousarray(np.asarray(a, dtype=np.float32))
    shared = {}
    for k in WNAMES:
        a = f(inp[k])
        if k in SQUEEZE:
            a = a[0]
        if k in ("a_rk", "c_d"):
            a = a.reshape(-1)
        shared[k] = f(a)
    in_maps = []
    for c in range(ncores):
        sp = slice(c * NSP, (c + 1) * NSP)
        ss = slice(c * NSS, (c + 1) * NSS)
        m = dict(shared)
        m["x_prompt"] = f(inp["x_prompt"][sp])
        m["x_sample"] = f(inp["x_sample"][ss])
        m["state_a_shift"] = f(inp["state_a_shift"][0, ss])
        m["state_a_wkv"] = f(inp["state_a_wkv"][0, ss])
        m["cache_b_k"] = f(np.asarray(inp["cache_b_k"])[0, ss].reshape(NSS, PAST, 512))
        m["cache_b_v"] = f(np.asarray(inp["cache_b_v"])[0, ss].reshape(NSS, PAST, 512))
        m["cache_b_logf"] = f(inp["cache_b_logf"][0, ss])
        m["state_c_re"] = f(inp["state_c_re"][0, ss])
        m["state_c_im"] = f(inp["state_c_im"][0, ss])
        m["state_ffn_conv"] = f(np.asarray(inp["state_ffn_conv"])[:, ss])
        in_maps.append(m)
    res = run_bass_kernel_spmd(nc, in_maps, core_ids=list(range(ncores)))
    R = res.results
    cat = lambda name, ax=0: np.concatenate([np.asarray(r[name]) for r in R], axis=ax)
    outs = []
    outs.append(cat("p_y"))
    outs.append(cat("s_y"))
    for g, L in (("p", LP), ("s", LS)):
        n = (NSP if g == "p" else NSS) * ncores
        outs.append(cat(g + "_a_shift")[None])
        outs.append(cat(g + "_a_wkv")[None])
        outs.append(cat(g + "_b_k").reshape(1, n, L, 8, 64))
        outs.append(cat(g + "_b_v").reshape(1, n, L, 8, 64))
        outs.append(cat(g + "_b_logf")[None])
        outs.append(cat(g + "_c_re")[None])
        outs.append(cat(g + "_c_im")[None])
        outs.append(cat(g + "_ffn_conv", ax=1))
    return tuple(np.ascontiguousarray(o.astype(np.float32)) for o in outs)
```

```python
import numpy as np
import concourse.bass as bass
import concourse.mybir as mybir
from concourse.bass_utils import run_bass_kernel_spmd

F32 = mybir.dt.float32
BF16 = mybir.dt.bfloat16
I32 = mybir.dt.int32
AF = mybir.ActivationFunctionType
ALU = mybir.AluOpType
AX = mybir.AxisListType
PI = 3.141592653589793

D = 1024
NIN = 3336
ACOLS = 1792
DFF = 2816
NCORES = 8
import os as _os
SAME_ENGINE_SYNC = _os.environ.get('KSAME', '1') == '1'
DEFER_PE = _os.environ.get('KDEFER', '0') == '1'


class T:
    def __init__(self, ap):
        self.ap = ap
        self.w = None
        self.r = {}


class Rot:
    def __init__(self, ts):
        self.ts = ts
        self.i = 0

    def next(self):
        t = self.ts[self.i % len(self.ts)]
        self.i += 1
        return t


class Sched:
    EPOCH = 30000
    DEPOCH = 1800

    def __init__(self, nc):
        self.nc = nc
        self.streams = {e: [] for e in ("pe", "act", "dve", "pool", "sp")}
        self.cnt = {e: 0 for e in self.streams}
        self.sems = {}
        self.waited = {e: {} for e in self.streams}
        self.dq = {"sp": list(range(0, 20)), "pool": list(range(20, 28)), "act": list(range(28, 32))}
        self.drr = {q: 0 for q in self.dq}
        self.dcnt = [0] * 32
        self.nops = 0
        self.pend = {}

    def _wait(self, eng, tok):
        key, val = tok
        if self.waited[eng].get(key, 0) >= val:
            return
        self.waited[eng][key] = val
        self.streams[eng].append(("w", key, val))

    def _deps(self, eng, reads, writes):
        toks = {}
        for t in reads:
            if t.w is not None:
                k, v = t.w
                toks[k] = max(toks.get(k, 0), v)
        for t in writes:
            if t.w is not None:
                k, v = t.w
                toks[k] = max(toks.get(k, 0), v)
            for k, v in t.r.items():
                toks[k] = max(toks.get(k, 0), v)
        for k, v in toks.items():
            if eng == "pe" and k[0] == "pe":
                continue
            if not SAME_ENGINE_SYNC and k[0] == eng and eng in ("act", "dve"):
                continue
            self._wait(eng, (k, v))

    def _upd(self, tok, reads, writes):
        k, v = tok
        for t in reads:
            t.r[k] = max(t.r.get(k, 0), v)
        for t in writes:
            t.w = tok
            t.r = {}

    def op(self, eng, fn, reads=(), writes=(), defer=False):
        self._deps(eng, reads, writes)
        n = self.cnt[eng]
        self.cnt[eng] = n + 1
        key = (eng, n // self.EPOCH)
        val = n % self.EPOCH + 1
        if defer and (n % self.EPOCH) != self.EPOCH - 1:
            self.pend[eng] = self.pend.get(eng, 0) + 1
            self.streams[eng].append(("o", fn, key, 0))
        else:
            self.streams[eng].append(("o", fn, key, 1 + self.pend.get(eng, 0)))
            self.pend[eng] = 0
        self._upd((key, val), reads, writes)
        self.nops += 1

    def dma(self, q, out_ap, in_ap, reads=(), writes=(), **kw):
        self._deps(q, reads, writes)
        lst = self.dq[q]
        k = lst[self.drr[q] % len(lst)]
        self.drr[q] += 1
        c = self.dcnt[k]
        key = ("d", k, c // self.DEPOCH)
        cc = c % self.DEPOCH
        if cc > 0:
            self._wait(q, (key, 16 * cc))
        elif c > 0:
            self._wait(q, (("d", k, c // self.DEPOCH - 1), 16 * self.DEPOCH))
        self.dcnt[k] = c + 1
        self.streams[q].append(("d", out_ap, in_ap, key, kw))
        self._upd((key, 16 * (cc + 1)), reads, writes)
        self.nops += 1

    def emit(self):
        nc = self.nc
        for k in range(32):
            c = self.dcnt[k]
            if c > 0:
                e = (c - 1) // self.DEPOCH
                self._wait("sp", (("d", k, e), 16 * ((c - 1) % self.DEPOCH + 1)))
        keys = set()
        for e, st in self.streams.items():
            for it in st:
                if it[0] == "w":
                    keys.add(it[1])
                elif it[0] == "o":
                    keys.add(it[2])
                else:
                    keys.add(it[3])
        for k in sorted(keys, key=str):
            self.sems[k] = nc.alloc_semaphore("s_" + "_".join(str(x) for x in k))
        sems = self.sems

        def run(eng, name):
            for it in self.streams[name]:
                if it[0] == "w":
                    eng.wait_ge(sems[it[1]], it[2])
                elif it[0] == "o":
                    if it[3] > 0:
                        it[1](eng).then_inc(sems[it[2]], it[3])
                    else:
                        it[1](eng)
                else:
                    eng.dma_start(out=it[1], in_=it[2], **it[4]).then_inc(sems[it[3]], 16)

        with nc.Block() as block:
            @block.tensor
            def _(e):
                run(e, "pe")

            @block.scalar
            def _(e):
                run(e, "act")

            @block.vector
            def _(e):
                run(e, "dve")

            @block.gpsimd
            def _(e):
                run(e, "pool")

            @block.sync
            def _(e):
                run(e, "sp")


def build(NSP, LP, NSS, LS, PAST):
    nc = bass.Bass("TRN2", target_bir_lowering=False)
    S = Sched(nc)
    XP = LP - 16
    assert PAST % 128 == 0

    def din(name, shape):
        return T(nc.dram_tensor(name, list(shape), F32, kind="ExternalInput").ap())

    def dout(name, shape):
        return T(nc.dram_tensor(name, list(shape), F32, kind="ExternalOutput").ap())

    def dscr(name, shape, dt):
        return T(nc.dram_tensor(name, list(shape), dt, kind="Internal").ap())

    cnt = [0]

    def sb(shape, dt=F32, name=None):
        cnt[0] += 1
        return T(nc.alloc_sbuf_tensor("%s_%d" % (name or "t", cnt[0]), list(shape), dt)[:])

    I = {}
    I["x_prompt"] = din("x_prompt", [NSP, XP, D])
    I["x_sample"] = din("x_sample", [NSS, LS, D])
    I["state_a_shift"] = din("state_a_shift", [NSS, ACOLS])
    I["state_a_wkv"] = din("state_a_wkv", [NSS, 8, 64, 64])
    I["cache_b_k"] = din("cache_b_k", [NSS, PAST, 512])
    I["cache_b_v"] = din("cache_b_v", [NSS, PAST, 512])
    I["cache_b_logf"] = din("cache_b_logf", [NSS, PAST, 8])
    I["state_c_re"] = din("state_c_re", [NSS, 64, 64])
    I["state_c_im"] = din("state_c_im", [NSS, 64, 64])
    I["state_ffn_conv"] = din("state_ffn_conv", [2, NSS, 2, 2 * DFF])
    wshapes = dict(meta=[16, D], norm_mix=[2, D], norm_ffn=[2, D], w_in_e=[D, NIN], a_mu=[ACOLS], a_w0=[512],
                   a_w2=[64, 512], a_a0=[512], a_a2=[64, 512], a_g2=[128, 512], a_kk=[512], a_ka=[512],
                   a_rk=[512], a_ln_w=[512], a_ln_b=[512], b_fbias=[8], b_qnorm=[64], b_knorm=[64],
                   w_out_e=[D, D], c_lam_re=[64, 64], c_lam_im=[64, 64], c_log_dt=[64],
                   c_b_re=[64, 64, 16], c_b_im=[64, 64, 16], c_c_re=[64, 16, 64], c_c_im=[64, 16, 64],
                   c_d=[1024], w_glu=[D, 2 * D], w_up=[2, D, 2 * DFF], conv_w=[2, 3, 2 * DFF],
                   conv_b=[2, 2 * DFF], w_down=[2, DFF, D])
    for k_, v_ in wshapes.items():
        I[k_] = din(k_, v_)
    O = {}
    for g, ns, lrows, lx in (("p", NSP, LP, XP), ("s", NSS, LS, LS)):
        O[g + "_y"] = dout(g + "_y", [ns, lx, D])
        O[g + "_a_shift"] = dout(g + "_a_shift", [ns, ACOLS])
        O[g + "_a_wkv"] = dout(g + "_a_wkv", [ns, 8, 64, 64])
        O[g + "_b_k"] = dout(g + "_b_k", [ns, lrows, 512])
        O[g + "_b_v"] = dout(g + "_b_v", [ns, lrows, 512])
        O[g + "_b_logf"] = dout(g + "_b_logf", [ns, lrows, 8])
        O[g + "_c_re"] = dout(g + "_c_re", [ns, 64, 64])
        O[g + "_c_im"] = dout(g + "_c_im", [ns, 64, 64])
        O[g + "_ffn_conv"] = dout(g + "_ffn_conv", [2, ns, 2, 2 * DFF])
    W = {}
    W["in"] = dscr("w_in_b", [7, 128, 4096], BF16)
    W["out"] = dscr("w_out_b", [2, 128, 4096], BF16)
    W["up0"] = dscr("w_up0_b", [11, 128, 4096], BF16)
    W["up1"] = dscr("w_up1_b", [11, 128, 4096], BF16)
    W["dn0"] = dscr("w_dn0_b", [8, 128, 2816], BF16)
    W["dn1"] = dscr("w_dn1_b", [8, 128, 2816], BF16)
    W["glu"] = dscr("w_glu_b", [4, 128, 4096], BF16)

    PS = Rot([T(nc.alloc_psum_tensor("ps%d" % i, [128, 1024], F32)[:]) for i in range(3)])
    PSACC = [T(nc.alloc_psum_tensor("psacc%d" % i, [128, 512], F32)[:]) for i in range(2)]
    SLOTS = Rot([sb([128, 4096], BF16, "slot") for _ in range(3)])
    STG = Rot([sb([128, 2048], F32, "stg") for _ in range(2)])

    def rot(n, shape, dt=F32, name="r"):
        return Rot([sb(shape, dt, name) for _ in range(n)])

    def mm(ot, o, lt, l, rt, r, start=True, stop=True):
        S.op("pe", lambda e: e.matmul(o, l, r, start=start, stop=stop), reads=[lt, rt], writes=[ot],
             defer=(not stop) and DEFER_PE)

    def tr(ot, o, it, i, idt, idap):
        S.op("pe", lambda e: e.transpose(o, i, idap), reads=[it, idt], writes=[ot])

    def act(ot, o, it, i, func, scale=None, bias=None, extra=()):
        kw = {}
        if scale is not None:
            kw["scale"] = scale
        if bias is not None:
            kw["bias"] = bias
        S.op("act", lambda e: e.activation(o, i, func, **kw), reads=[it] + list(extra), writes=[ot])

    def tt(eng, ot, o, at, a, bt, b, op):
        S.op(eng, lambda e: e.tensor_tensor(o, a, b, op), reads=[at, bt], writes=[ot])

    def ts(eng, ot, o, at, a, s1, s2, op0, op1=None, extra=()):
        if op1 is None:
            S.op(eng, lambda e: e.tensor_scalar(o, a, s1, None, op0), reads=[at] + list(extra), writes=[ot])
        else:
            S.op(eng, lambda e: e.tensor_scalar(o, a, s1, s2, op0, op1), reads=[at] + list(extra), writes=[ot])

    def stt(ot, o, at, a, sc, bt, b, op0, op1, extra=()):
        S.op("dve", lambda e: e.scalar_tensor_tensor(o, a, sc, b, op0, op1), reads=[at, bt] + list(extra),
             writes=[ot])

    def cp(eng, ot, o, it, i):
        if eng == "act":
            S.op("act", lambda e: e.copy(o, i), reads=[it], writes=[ot])
        else:
            S.op(eng, lambda e: e.tensor_copy(o, i), reads=[it], writes=[ot])

    def memset(eng, t, ap, val):
        S.op(eng, lambda e: e.memset(ap, val), writes=[t])

    def recip(ot, o, it, i):
        S.op("dve", lambda e: e.reciprocal(o, i), reads=[it], writes=[ot])

    CONST = sb([128, 6, 128], F32, "const")
    CB = sb([128, 3, 128], BF16, "constb")
    ii = sb([128, 3, 128], I32, "iota")
    fi = sb([128, 3, 128], F32, "iotaf")
    S.op("pool", lambda e: e.iota(ii.ap[:, 0, :], [[1, 128]], base=0, channel_multiplier=-1), writes=[ii])
    S.op("pool", lambda e: e.iota(ii.ap[:, 1, :], [[0, 128]], base=0, channel_multiplier=1), writes=[ii])
    S.op("pool", lambda e: e.iota(ii.ap[:, 2, :], [[1, 128]], base=0, channel_multiplier=0), writes=[ii])
    cp("dve", fi, fi.ap, ii, ii.ap)
    c_ = CONST.ap
    ts("dve", CONST, c_[:, 0, :], fi, fi.ap[:, 0, :], 0.0, None, ALU.is_equal)
    ts("dve", CONST, c_[:, 1, :], fi, fi.ap[:, 0, :], 0.0, None, ALU.is_gt)
    ts("dve", CONST, c_[:, 2, :], fi, fi.ap[:, 0, :], 0.0, None, ALU.is_ge)
    ts("dve", CONST, c_[:, 3, :], fi, fi.ap[:, 0, :], 0.0, None, ALU.is_lt)
    memset("dve", CONST, c_[:, 4, :], 1.0)
    ts("dve", fi, fi.ap[:, 1, :], fi, fi.ap[:, 1, :], 64.0, None, ALU.is_ge)
    ts("dve", fi, fi.ap[:, 2, :], fi, fi.ap[:, 2, :], 64.0, None, ALU.is_ge)
    tt("dve", CONST, c_[:, 5, :], fi, fi.ap[:, 1, :], fi, fi.ap[:, 2, :], ALU.is_equal)
    cp("dve", CB, CB.ap[:, 0, :], CONST, c_[:, 0, :])
    cp("dve", CB, CB.ap[:, 1, :], CONST, c_[:, 4, :])
    cp("dve", CB, CB.ap[:, 2, :], CONST, c_[:, 5, :])
    IDF = c_[:, 0, :]
    M_SU = c_[:, 1, :]
    M_U = c_[:, 2, :]
    M_SL = c_[:, 3, :]
    ONESF = c_[:, 4, :]
    IDB = CB.ap[:, 0, :]
    ONESB = CB.ap[:, 1, :]
    BDB = CB.ap[:, 2, :]

    PRM = sb([128, 512], F32, "prm")
    pcol = [0]

    def pload(vec_ap, ntile):
        c0 = pcol[0]
        pcol[0] += ntile
        S.dma("sp", PRM.ap[:, c0:c0 + ntile], vec_ap.rearrange("(j p) -> p j", p=128), writes=[PRM],
              allow_slow_non_contiguous=True)
        return c0

    def pc(c0, j):
        return PRM.ap[:, c0 + j:c0 + j + 1]

    def pnew(n):
        c0 = pcol[0]
        pcol[0] += n
        return c0

    g_mix = [pload(I["norm_mix"].ap[l], 8) for l in range(2)]
    g_ffn = [pload(I["norm_ffn"].ap[l], 8) for l in range(2)]
    p_mu = pload(I["a_mu"].ap, 14)
    p_w0 = pload(I["a_w0"].ap, 4)
    p_a0 = pload(I["a_a0"].ap, 4)
    p_kk = pload(I["a_kk"].ap, 4)
    p_ka = pload(I["a_ka"].ap, 4)
    p_rk = pload(I["a_rk"].ap, 4)
    p_lnw = pload(I["a_ln_w"].ap, 4)
    p_lnb = pload(I["a_ln_b"].ap, 4)
    p_cw = [[pload(I["conv_w"].ap[l, i], 44) for i in range(3)] for l in range(2)]
    p_cb = [pload(I["conv_b"].ap[l], 44) for l in range(2)]
    p_dsk = pload(I["c_d"].ap, 8)
    p_omm = pnew(14)
    ts("dve", PRM, PRM.ap[:, p_omm:p_omm + 14], PRM, PRM.ap[:, p_mu:p_mu + 14], -1.0, 1.0, ALU.mult, ALU.add)
    p_omka = pnew(4)
    ts("dve", PRM, PRM.ap[:, p_omka:p_omka + 4], PRM, PRM.ap[:, p_ka:p_ka + 4], -1.0, 1.0, ALU.mult, ALU.add)
    p_gq = pnew(1)
    p_gk = pnew(1)
    for hh in range(2):
        S.dma("sp", PRM.ap[hh * 64:(hh + 1) * 64, p_gq:p_gq + 1], I["b_qnorm"].ap.rearrange("(p o) -> p o", o=1),
              writes=[PRM], allow_slow_non_contiguous=True)
        S.dma("sp", PRM.ap[hh * 64:(hh + 1) * 64, p_gk:p_gk + 1], I["b_knorm"].ap.rearrange("(p o) -> p o", o=1),
              writes=[PRM], allow_slow_non_contiguous=True)
    ts("dve", PRM, PRM.ap[:, p_gq:p_gq + 1], PRM, PRM.ap[:, p_gq:p_gq + 1], 0.125, None, ALU.mult)
    FB = sb([128, 8], F32, "fbias")
    S.dma("sp", FB.ap, I["b_fbias"].ap.partition_broadcast(128), writes=[FB], allow_slow_non_contiguous=True)

    ceng = Rot(["act", "dve", "pool"])

    def convert(src_t, src_ap, dst_t, dst_ap, A, B):
        st = STG.next()
        sl = SLOTS.next()
        sv = st.ap[:, 0:A * B].rearrange("p (a b) -> p a b", a=A)
        bv = sl.ap[:, 0:A * B].rearrange("p (a b) -> p a b", a=A)
        S.dma("sp", sv, src_ap, reads=[src_t], writes=[st])
        cp(ceng.next(), sl, bv, st, sv)
        S.dma("pool", dst_ap, bv, reads=[sl], writes=[dst_t])

    def conv_mat(src_t, src2d, dst_t, npieces, ncols):
        v = src2d.rearrange("(k p) c -> p k c", p=128)
        for q in range(npieces):
            for h in range(2):
                c0 = q * 512 + h * 256
                cw = min(256, ncols - c0)
                if cw <= 0:
                    continue
                dv = dst_t.ap[q].rearrange("p (k c) -> p k c", k=8)[:, :, h * 256:h * 256 + cw]
                convert(src_t, v[:, :, c0:c0 + cw], dst_t, dv, 8, cw)

    conv_mat(I["w_in_e"], I["w_in_e"].ap, W["in"], 7, NIN)
    conv_mat(I["w_out_e"], I["w_out_e"].ap, W["out"], 2, D)
    conv_mat(I["w_glu"], I["w_glu"].ap, W["glu"], 4, 2 * D)
    for l in range(2):
        v = I["w_up"].ap[l].rearrange("(k p) c -> p k c", p=128)
        for q in range(11):
            for h in range(2):
                c0 = h * DFF + q * 256
                dv = W["up%d" % l].ap[q].rearrange("p (k c) -> p k c", k=8)[:, :, h * 256:(h + 1) * 256]
                convert(I["w_up"], v[:, :, c0:c0 + 256], W["up%d" % l], dv, 8, 256)
        v = I["w_down"].ap[l].rearrange("(k p) c -> p k c", p=128)
        for j in range(8):
            for h in range(2):
                dv = W["dn%d" % l].ap[j].rearrange("p (k c) -> p k c", k=22)[:, h * 11:(h + 1) * 11, :]
                convert(I["w_down"], v[:, h * 11:(h + 1) * 11, j * 128:(j + 1) * 128], W["dn%d" % l], dv, 11, 128)

    LR = sb([128, 512], BF16, "lr")
    G2 = sb([128, 512], BF16, "g2")
    st = STG.next()
    S.dma("sp", st.ap[0:64, 0:512], I["a_w2"].ap, writes=[st])
    S.dma("sp", st.ap[64:128, 0:512], I["a_a2"].ap, writes=[st])
    cp("dve", LR, LR.ap, st, st.ap[:, 0:512])
    st = STG.next()
    S.dma("sp", st.ap[:, 0:512], I["a_g2"].ap, writes=[st])
    cp("dve", G2, G2.ap, st, st.ap[:, 0:512])

    S5P = sb([128, 16, 32], F32, "s5p")
    s5 = S5P.ap
    LRE, LIM, LDT, DT_, MAG, TH, FRE, FIM, FIR, FII, T0, T1, T2, T3 = range(14)
    S.dma("sp", s5[:, LRE, :], I["c_lam_re"].ap.rearrange("(m gg) p -> (gg p) m", gg=2), writes=[S5P],
          allow_slow_non_contiguous=True)
    S.dma("sp", s5[:, LIM, :], I["c_lam_im"].ap.rearrange("(m gg) p -> (gg p) m", gg=2), writes=[S5P],
          allow_slow_non_contiguous=True)
    for gg in range(2):
        S.dma("sp", s5[gg * 64:(gg + 1) * 64, LDT, :],
              I["c_log_dt"].ap.rearrange("(m gg) -> gg m", gg=2)[gg].partition_broadcast(64), writes=[S5P],
              allow_slow_non_contiguous=True)
    act(S5P, s5[:, DT_, :], S5P, s5[:, LDT, :], AF.Exp)
    tt("dve", S5P, s5[:, T0, :], S5P, s5[:, LRE, :], S5P, s5[:, DT_, :], ALU.mult)
    act(S5P, s5[:, MAG, :], S5P, s5[:, T0, :], AF.Exp)
    tt("dve", S5P, s5[:, TH, :], S5P, s5[:, LIM, :], S5P, s5[:, DT_, :], ALU.mult)

    RRI = sb([128, 1024], I32, "rri")
    RRK = sb([128, 2048], F32, "rrk")

    def range_reduce(t, ap, n):
        if len(ap.shape) == 3 and n > 1024:
            a_ = ap.shape[1]
            step = 1024 // ap.shape[2]
            for a0 in range(0, a_, step):
                range_reduce(t, ap[:, a0:a0 + step, :], 1024)
            return
        ki = RRI.ap[:, 0:n]
        kf = RRK.ap[:, 0:n]
        if len(ap.shape) == 3:
            ki = ki.rearrange("p (a b) -> p a b", a=ap.shape[1])
            kf = kf.rearrange("p (a b) -> p a b", a=ap.shape[1])
        ts("dve", RRK, kf, t, ap, 1.0 / (2 * PI), 0.5, ALU.mult, ALU.add)
        cp("dve", RRI, ki, RRK, kf)
        cp("dve", RRK, kf, RRI, ki)
        stt(t, ap, RRK, kf, -6.28125, t, ap, ALU.mult, ALU.add)
        stt(t, ap, RRK, kf, -(2 * PI - 6.28125), t, ap, ALU.mult, ALU.add)
        ts("dve", RRK, kf, t, ap, PI, -2 * PI, ALU.is_gt, ALU.mult)
        tt("dve", t, ap, t, ap, RRK, kf, ALU.add)
        ts("dve", RRK, kf, t, ap, -PI, 2 * PI, ALU.is_lt, ALU.mult)
        tt("dve", t, ap, t, ap, RRK, kf, ALU.add)
        ts("dve", t, ap, t, ap, PI, -PI, ALU.min, ALU.max)

    ts("dve", S5P, s5[:, T1, :], S5P, s5[:, TH, :], 1.0, None, ALU.mult)
    range_reduce(S5P, s5[:, T1, :], 32)
    act(S5P, s5[:, T2, :], S5P, s5[:, T1, :], AF.Sin)
    ts("dve", S5P, s5[:, T1, :], S5P, s5[:, TH, :], PI / 2, None, ALU.add)
    range_reduce(S5P, s5[:, T1, :], 32)
    act(S5P, s5[:, T3, :], S5P, s5[:, T1, :], AF.Sin)
    tt("dve", S5P, s5[:, T2, :], S5P, s5[:, T2, :], S5P, s5[:, MAG, :], ALU.mult)
    tt("dve", S5P, s5[:, T3, :], S5P, s5[:, T3, :], S5P, s5[:, MAG, :], ALU.mult)
    ts("dve", S5P, s5[:, T3, :], S5P, s5[:, T3, :], -1.0, None, ALU.add)
    tt("dve", S5P, s5[:, T0, :], S5P, s5[:, LRE, :], S5P, s5[:, LRE, :], ALU.mult)
    tt("dve", S5P, s5[:, T1, :], S5P, s5[:, LIM, :], S5P, s5[:, LIM, :], ALU.mult)
    tt("dve", S5P, s5[:, T0, :], S5P, s5[:, T0, :], S5P, s5[:, T1, :], ALU.add)
    recip(S5P, s5[:, T0, :], S5P, s5[:, T0, :])
    tt("dve", S5P, s5[:, FRE, :], S5P, s5[:, T3, :], S5P, s5[:, LRE, :], ALU.mult)
    tt("dve", S5P, s5[:, T1, :], S5P, s5[:, T2, :], S5P, s5[:, LIM, :], ALU.mult)
    tt("dve", S5P, s5[:, FRE, :], S5P, s5[:, FRE, :], S5P, s5[:, T1, :], ALU.add)
    tt("dve", S5P, s5[:, FRE, :], S5P, s5[:, FRE, :], S5P, s5[:, T0, :], ALU.mult)
    tt("dve", S5P, s5[:, FIM, :], S5P, s5[:, T2, :], S5P, s5[:, LRE, :], ALU.mult)
    tt("dve", S5P, s5[:, T1, :], S5P, s5[:, T3, :], S5P, s5[:, LIM, :], ALU.mult)
    tt("dve", S5P, s5[:, FIM, :], S5P, s5[:, FIM, :], S5P, s5[:, T1, :], ALU.subtract)
    tt("dve", S5P, s5[:, FIM, :], S5P, s5[:, FIM, :], S5P, s5[:, T0, :], ALU.mult)
    tt("dve", S5P, s5[:, T0, :], S5P, s5[:, FRE, :], S5P, s5[:, FRE, :], ALU.mult)
    tt("dve", S5P, s5[:, T1, :], S5P, s5[:, FIM, :], S5P, s5[:, FIM, :], ALU.mult)
    tt("dve", S5P, s5[:, T0, :], S5P, s5[:, T0, :], S5P, s5[:, T1, :], ALU.add)
    recip(S5P, s5[:, T0, :], S5P, s5[:, T0, :])
    tt("dve", S5P, s5[:, FIR, :], S5P, s5[:, FRE, :], S5P, s5[:, T0, :], ALU.mult)
    tt("dve", S5P, s5[:, FII, :], S5P, s5[:, FIM, :], S5P, s5[:, T0, :], ALU.mult)
    ts("dve", S5P, s5[:, FII, :], S5P, s5[:, FII, :], -1.0, None, ALU.mult)
    TCOS = sb([128, 32, 64], F32, "tcos")
    TSIN = sb([128, 32, 64], F32, "tsin")
    cp("dve", fi, fi.ap[:, 2, :], ii, ii.ap[:, 2, :])
    ts("dve", fi, fi.ap[:, 2, :], fi, fi.ap[:, 2, :], 1.0, None, ALU.add)
    for m in range(32):
        ts("dve", TSIN, TSIN.ap[:, m, :], fi, fi.ap[:, 2, 0:64], s5[:, TH, m:m + 1], None, ALU.mult, extra=[S5P])
    ts("dve", TCOS, TCOS.ap, TSIN, TSIN.ap, PI / 2, None, ALU.add)
    range_reduce(TSIN, TSIN.ap, 2048)
    range_reduce(TCOS, TCOS.ap, 2048)
    act(TSIN, TSIN.ap, TSIN, TSIN.ap, AF.Sin)
    act(TCOS, TCOS.ap, TCOS, TCOS.ap, AF.Sin)
    BPR = sb([128, 32, 128], BF16, "bpr")
    BPI = sb([128, 32, 128], BF16, "bpi")
    CPR = sb([128, 32, 128], BF16, "cpr")
    CPI = sb([128, 32, 128], BF16, "cpi")
    for half in range(2):
        for src, dst in ((I["c_b_re"], BPR), (I["c_b_im"], BPI)):
            st = STG.next()
            memset("pool", st, st.ap, 0.0)
            sv = st.ap.rearrange("r (m4 mm s) -> r mm m4 s", mm=4, s=128)
            bv = src.ap.rearrange("(m4 mm gg) p c -> mm gg c m4 p", mm=4, gg=2)
            for mm_ in range(4):
                for gg in range(2):
                    r0 = mm_ * 32 + gg * 16
                    for m4l in range(4):
                        S.dma("sp", sv[r0:r0 + 16, mm_, m4l, gg * 64:(gg + 1) * 64],
                              bv[mm_, gg][:, 4 * half + m4l, :], reads=[src], writes=[st],
                              allow_slow_non_contiguous=True)
            cp("act", dst, dst.ap[:, 16 * half:16 * half + 16, :].rearrange("p a b -> p (a b)"), st, st.ap)
        stc = []
        for src in (I["c_c_re"], I["c_c_im"]):
            st = STG.next()
            memset("pool", st, st.ap, 0.0)
            sv = st.ap.rearrange("r (m4 mm s) -> r mm m4 s", mm=4, s=128)
            cv = src.ap.rearrange("(m4 mm gg) c p -> mm gg p m4 c", mm=4, gg=2)
            for mm_ in range(4):
                for gg in range(2):
                    c0 = mm_ * 32 + gg * 16
                    for m4l in range(4):
                        S.dma("sp", sv[gg * 64:(gg + 1) * 64, mm_, m4l, c0:c0 + 16],
                              cv[mm_, gg][:, 4 * half + m4l, :], reads=[src], writes=[st],
                              allow_slow_non_contiguous=True)
            stc.append(st)
        for qt in range(2):
            m0 = 16 * half + 8 * qt
            cre = stc[0].ap[:, 1024 * qt:1024 * qt + 1024].rearrange("p (m s) -> p m s", s=128)
            cim = stc[1].ap[:, 1024 * qt:1024 * qt + 1024].rearrange("p (m s) -> p m s", s=128)
            fre = s5[:, FRE, m0:m0 + 8].unsqueeze(2).to_broadcast([128, 8, 128])
            fim = s5[:, FIM, m0:m0 + 8].unsqueeze(2).to_broadcast([128, 8, 128])
            ta = RRK.ap[:, 0:1024].rearrange("p (m s) -> p m s", s=128)
            tb = RRK.ap[:, 1024:2048].rearrange("p (m s) -> p m s", s=128)
            tt("dve", RRK, ta, stc[0], cre, S5P, fre, ALU.mult)
            tt("dve", RRK, tb, stc[1], cim, S5P, fim, ALU.mult)
            tt("dve", CPR, CPR.ap[:, m0:m0 + 8, :], RRK, ta, RRK, tb, ALU.subtract)
            tt("dve", RRK, ta, stc[0], cre, S5P, fim, ALU.mult)
            tt("dve", RRK, tb, stc[1], cim, S5P, fre, ALU.mult)
            tt("dve", RRK, ta, RRK, ta, RRK, tb, ALU.add)
            ts("dve", CPI, CPI.ap[:, m0:m0 + 8, :], RRK, ta, -1.0, None, ALU.mult)

    XT = sb([128, 8, 128], F32, "xT")
    HT = sb([128, 8, 128], BF16, "hT")
    XIN = rot(1, [128, D], F32, "xin")
    U = sb([128, 14, 129], F32, "U")
    UM = sb([128, 14, 128], F32, "UM")
    QKR = sb([128, 8, 128], F32, "qkraw")
    VB = rot(1, [128, 512], F32, "vbtok")
    LFT = sb([128, 8], F32, "lftok")
    HS = sb([128, 4, 64], F32, "HS")
    HSB = sb([128, 4, 64], BF16, "HSb")
    CAR = sb([128, 2, 32], F32, "s5car")
    CC = [sb([128, 44, 2], F32, "convcar") for _ in range(2)]
    NBMAX = max(PAST // 128 + 1, (LP + 127) // 128)
    assert NBMAX * 8 <= 384
    SUF = T(fi.ap.rearrange("p a b -> p (a b)")[:, 0:NBMAX * 8].rearrange("p (b h) -> p b h", h=8))
    TOT = T(ii.ap.bitcast(F32).rearrange("p a b -> p (a b)")[:, 0:NBMAX * 8].rearrange("p (b h) -> p b h", h=8))
    S.op("pool", lambda e: e.memset(SUF.ap, 0.0), writes=[fi, SUF])
    S.op("pool", lambda e: e.memset(TOT.ap, 0.0), writes=[ii, TOT])
    YCAT = sb([128, 8, 128], BF16, "ycat")
    ACTB = sb([128, 22, 128], BF16, "actb")
    SREr = rot(2, [128, 4, 128], BF16, "sre")
    SIMr = rot(2, [128, 4, 128], BF16, "sim")
    tmpF = Rot([T(STG.ts[i // 4].ap[:, 512 * (i % 4):512 * (i % 4 + 1)]) for i in range(6)])
    tmpB = Rot([T(STG.ts[1].ap[:, 1024 + 256 * i:1024 + 256 * (i + 1)].bitcast(BF16)) for i in range(4)])
    for t_ in tmpF.ts + tmpB.ts:
        S.op("pool", lambda e, t_=t_: e.memset(t_.ap[:, 0:1], 0.0), writes=STG.ts + [t_])

    def load_piece(name, q, size=4096):
        sl = SLOTS.next()
        if _os.environ.get("KNOW") == "1" and name.startswith(("up", "dn")):
            return sl
        if name == "in" and q == 6:
            S.dma("sp", sl.ap.rearrange("p (k c) -> p k c", k=8)[:, :, 0:264],
                  W[name].ap[q].rearrange("p (k c) -> p k c", k=8)[:, :, 0:264], reads=[W[name]], writes=[sl])
        else:
            S.dma("sp", sl.ap[:, 0:size], W[name].ap[q], reads=[W[name]], writes=[sl])
        return sl

    def rmsnorm(n, gcol):
        sq = tmpB.next()
        ps = PS.next()
        for half in range(2):
            sqv = sq.ap[:, 0:4 * n].rearrange("p (k t) -> p k t", k=4)
            act(sq, sqv, XT, XT.ap[:, 4 * half:4 * half + 4, 0:n], AF.Square)
            for k in range(4):
                mm(ps, ps.ap[:, 0:n], CB, ONESB, sq, sqv[:, k, :], start=(half == 0 and k == 0),
                   stop=(half == 1 and k == 3))
            if half == 0:
                sq = tmpB.next()
        r = tmpF.next()
        act(r, r.ap[:, 0:n], ps, ps.ap[:, 0:n], AF.Sqrt, scale=1.0 / D, bias=EPSC.ap[:, 0:1], extra=[EPSC])
        recip(r, r.ap[:, 0:n], r, r.ap[:, 0:n])
        for k in range(8):
            stt(HT, HT.ap[:, k, 0:n], XT, XT.ap[:, k, 0:n], pc(gcol, k), r, r.ap[:, 0:n], ALU.mult, ALU.mult,
                extra=[PRM])

    EPSC = sb([128, 4], F32, "epsc")
    memset("dve", EPSC, EPSC.ap[:, 0:1], 1e-6)
    memset("dve", EPSC, EPSC.ap[:, 1:2], 1e-12)
    memset("dve", EPSC, EPSC.ap[:, 2:3], 64e-5)
    memset("dve", EPSC, EPSC.ap[:, 3:4], 1.0)

    def ffn(n, l):
        rmsnorm(n, g_ffn[l])
        cw0, cw1, cw2 = p_cw[l]
        cb = p_cb[l]
        car = CC[l]
        for q in range(11):
            sl = load_piece("up%d" % l, q)
            wv = sl.ap.rearrange("p (k c) -> p k c", k=8)
            for loc in range(2):
                jj = 2 * q + loc
                zs = []
                for part in range(2):
                    tile_ = jj + 22 * part
                    ps = PS.next()
                    c0 = part * 256 + loc * 128
                    for k in range(8):
                        mm(ps, ps.ap[:, 0:n], sl, wv[:, k, c0:c0 + 128], HT, HT.ap[:, k, 0:n], start=(k == 0),
                           stop=(k == 7))
                    up = tmpF.next()
                    cp("pool", up, up.ap[:, 0:2], car, car.ap[:, tile_, :])
                    cp("act", up, up.ap[:, 2:2 + n], ps, ps.ap[:, 0:n])
                    z = tmpF.next()
                    ts("dve", z, z.ap[:, 0:n], up, up.ap[:, 2:2 + n], pc(cw2, tile_), pc(cb, tile_), ALU.mult,
                       ALU.add, extra=[PRM])
                    stt(z, z.ap[:, 0:n], up, up.ap[:, 1:1 + n], pc(cw1, tile_), z, z.ap[:, 0:n], ALU.mult, ALU.add,
                        extra=[PRM])
                    stt(z, z.ap[:, 0:n], up, up.ap[:, 0:n], pc(cw0, tile_), z, z.ap[:, 0:n], ALU.mult, ALU.add,
                        extra=[PRM])
                    cp("pool", car, car.ap[:, tile_, :], up, up.ap[:, n:n + 2])
                    zs.append(z)
                sg = tmpF.next()
                act(sg, sg.ap[:, 0:n], zs[1], zs[1].ap[:, 0:n], AF.Silu)
                tt("dve", ACTB, ACTB.ap[:, jj, 0:n], zs[0], zs[0].ap[:, 0:n], sg, sg.ap[:, 0:n], ALU.mult)
        for j in range(8):
            sl = load_piece("dn%d" % l, j, 2816)
            wv = sl.ap[:, 0:2816].rearrange("p (k c) -> p k c", k=22)
            ps = PS.next()
            for k in range(22):
                mm(ps, ps.ap[:, 0:n], sl, wv[:, k, :], ACTB, ACTB.ap[:, k, 0:n], start=(k == 0), stop=(k == 21))
            tt("dve", XT, XT.ap[:, j, 0:n], XT, XT.ap[:, j, 0:n], ps, ps.ap[:, 0:n], ALU.add)

    S5I = sb([128, 2, 8, 4], F32, "s5init")
    S5SETS = []

    def s5_chain(n, k, e):
        nhalf = (n + 63) // 64
        nh = n // nhalf
        t1, t2, t3, t4 = S5SETS[e]
        pp = PS.next()
        for mi in range(4):
            m = 4 * k + mi
            mm(pp, pp.ap[:, mi * 128:mi * 128 + n], BPR, BPR.ap[:, m, :], HT, HT.ap[:, k, 0:n])
            mm(pp, pp.ap[:, 512 + mi * 128:512 + mi * 128 + n], BPI, BPI.ap[:, m, :], HT, HT.ap[:, k, 0:n])
        yield

        def v4(ap512):
            return ap512.rearrange("p (m t) -> p m t", m=4)[:, :, 0:n].rearrange("p m (h t) -> p m h t", h=nhalf)

        prv = v4(pp.ap[:, 0:512])
        piv = v4(pp.ap[:, 512:1024])
        cs_ = TCOS.ap[:, 4 * k:4 * k + 4, 0:nh].unsqueeze(2).to_broadcast([128, 4, nhalf, nh])
        sn_ = TSIN.ap[:, 4 * k:4 * k + 4, 0:nh].unsqueeze(2).to_broadcast([128, 4, nhalf, nh])
        x1, x2, x3, x4 = [v4(t.ap) for t in (t1, t2, t3, t4)]
        tt("dve", t1, x1, pp, prv, TCOS, cs_, ALU.mult)
        yield
        tt("dve", t2, x2, pp, piv, TSIN, sn_, ALU.mult)
        yield
        tt("dve", t1, x1, t1, x1, t2, x2, ALU.add)
        yield
        tt("dve", t3, x3, pp, piv, TCOS, cs_, ALU.mult)
        yield
        tt("dve", t4, x4, pp, prv, TSIN, sn_, ALU.mult)
        yield
        tt("dve", t3, x3, t3, x3, t4, x4, ALU.subtract)
        yield
        si = S5I.ap[:, e]
        for hf in range(nhalf):
            if hf == 1:
                c63 = TCOS.ap[:, 4 * k:4 * k + 4, nh - 1]
                s63 = TSIN.ap[:, 4 * k:4 * k + 4, nh - 1]
                wr = x2[:, :, 0, nh - 1]
                wi = x4[:, :, 0, nh - 1]
                tt("dve", S5I, si[:, 0, :], t2, wr, TCOS, c63, ALU.mult)
                yield
                tt("dve", S5I, si[:, 1, :], t4, wi, TSIN, s63, ALU.mult)
                yield
                tt("dve", S5I, si[:, 2, :], t4, wi, TCOS, c63, ALU.mult)
                yield
                tt("dve", S5I, si[:, 3, :], t2, wr, TSIN, s63, ALU.mult)
                yield
                tt("dve", S5I, si[:, 4, :], S5I, si[:, 0, :], S5I, si[:, 1, :], ALU.subtract)
                yield
                tt("dve", S5I, si[:, 5, :], S5I, si[:, 2, :], S5I, si[:, 3, :], ALU.add)
                yield
            for mi in range(4):
                m = 4 * k + mi
                lamb = s5[:, MAG, m:m + 1].to_broadcast([128, nh])
                if hf == 0:
                    i_re, i_im, i_t = CAR.ap[:, 0, m:m + 1], CAR.ap[:, 1, m:m + 1], CAR
                else:
                    i_re, i_im, i_t = si[:, 4, mi:mi + 1], si[:, 5, mi:mi + 1], S5I
                S.op("dve", lambda e_, o=x2[:, mi, hf, :], d=x1[:, mi, hf, :], lamb=lamb, ini=i_re:
                     e_.tensor_tensor_scan(o, lamb, d, ini, ALU.mult, ALU.add), reads=[S5P, t1, i_t], writes=[t2])
                yield
                S.op("dve", lambda e_, o=x4[:, mi, hf, :], d=x3[:, mi, hf, :], lamb=lamb, ini=i_im:
                     e_.tensor_tensor_scan(o, lamb, d, ini, ALU.mult, ALU.add), reads=[S5P, t3, i_t], writes=[t4])
                yield
        tt("dve", t1, x1, t2, x2, TCOS, cs_, ALU.mult)
        yield
        tt("dve", t3, x3, t4, x4, TSIN, sn_, ALU.mult)
        yield
        tt("dve", t1, x1, t1, x1, t3, x3, ALU.subtract)
        yield
        tt("dve", t3, x3, t4, x4, TCOS, cs_, ALU.mult)
        yield
        tt("dve", t2, x2, t2, x2, TSIN, sn_, ALU.mult)
        yield
        tt("dve", t3, x3, t3, x3, t2, x2, ALU.add)
        yield
        SRE, SIM = SREr.ts[e], SIMr.ts[e]
        cp("act", SRE, v4(SRE.ap.rearrange("p m t -> p (m t)")), t1, x1)
        cp("act", SIM, v4(SIM.ap.rearrange("p m t -> p (m t)")), t3, x3)
        cp("act", CAR, CAR.ap[:, 0, 4 * k:4 * k + 4], t1, x1[:, :, nhalf - 1, nh - 1])
        cp("act", CAR, CAR.ap[:, 1, 4 * k:4 * k + 4], t3, x3[:, :, nhalf - 1, nh - 1])
        yield
        ps = PS.next()
        for mi in range(4):
            m = 4 * k + mi
            mm(ps, ps.ap[:, 0:n], CPR, CPR.ap[:, m, :], SRE, SRE.ap[:, mi, 0:n], start=(mi == 0), stop=False)
            mm(ps, ps.ap[:, 0:n], CPI, CPI.ap[:, m, :], SIM, SIM.ap[:, mi, 0:n], start=False, stop=(mi == 3))
        yield
        y, g1 = t2, t4
        stt(y, y.ap[:, 0:n], HT, HT.ap[:, k, 0:n], pc(p_dsk, k), ps, ps.ap[:, 0:n], ALU.mult, ALU.add, extra=[PRM])
        yield
        act(g1, g1.ap[:, 0:n], y, y.ap[:, 0:n], AF.Square)
        yield
        ts("dve", g1, g1.ap[:, 0:n], g1, g1.ap[:, 0:n], 0.044715, 1.0, ALU.mult, ALU.add)
        yield
        tt("dve", g1, g1.ap[:, 0:n], g1, g1.ap[:, 0:n], y, y.ap[:, 0:n], ALU.mult)
        yield
        act(g1, g1.ap[:, 0:n], g1, g1.ap[:, 0:n], AF.Sigmoid, scale=1.5957691216057308)
        yield
        tt("dve", YCAT, YCAT.ap[:, k, 0:n], g1, g1.ap[:, 0:n], y, y.ap[:, 0:n], ALU.mult)
        yield

    def s5_layer(n):
        rmsnorm(n, g_mix[1])
        if not S5SETS:
            S5SETS.append(tmpF.ts[0:4])
            x0 = T(RWC.ts[0].ap[:, 0:4, :].rearrange("p a b -> p (a b)"))
            x1_ = T(RWC.ts[0].ap[:, 4:8, :].rearrange("p a b -> p (a b)"))
            S5SETS.append([tmpF.ts[4], tmpF.ts[5], x0, x1_])
        for kp in range(0, 8, 2):
            gens = [s5_chain(n, kp, 0), s5_chain(n, kp + 1, 1)]
            if _os.environ.get("KS5SEQ", "0") == "1":
                for g_ in gens:
                    for _ in g_:
                        pass
                continue
            while gens:
                for g_ in list(gens):
                    try:
                        next(g_)
                    except StopIteration:
                        gens.remove(g_)
        sls = {}
        for q in (0, 2, 1, 3):
            sls[q] = load_piece("glu", q)
            if q in (2, 3):
                qa = q - 2
                wa = sls[qa].ap.rearrange("p (k c) -> p k c", k=8)
                wb = sls[q].ap.rearrange("p (k c) -> p k c", k=8)
                for loc in range(4):
                    j = 4 * qa + loc
                    pa = PS.next()
                    pb = PS.next()
                    for k in range(8):
                        mm(pa, pa.ap[:, 0:n], sls[qa], wa[:, k, loc * 128:(loc + 1) * 128], YCAT, YCAT.ap[:, k, 0:n],
                           start=(k == 0), stop=(k == 7))
                    for k in range(8):
                        mm(pb, pb.ap[:, 0:n], sls[q], wb[:, k, loc * 128:(loc + 1) * 128], YCAT, YCAT.ap[:, k, 0:n],
                           start=(k == 0), stop=(k == 7))
                    sg = tmpF.next()
                    act(sg, sg.ap[:, 0:n], pb, pb.ap[:, 0:n], AF.Sigmoid)
                    tt("dve", sg, sg.ap[:, 0:n], sg, sg.ap[:, 0:n], pa, pa.ap[:, 0:n], ALU.mult)
                    tt("dve", XT, XT.ap[:, j, 0:n], XT, XT.ap[:, j, 0:n], sg, sg.ap[:, 0:n], ALU.add)

    RWC = rot(2, [128, 10, 128], F32, "rwc")
    SIGW, AA, KK, KKN, KMOD, BBv, CS, DD, DINV, DPREV = range(10)
    GGB = sb([128, 2, 4, 128], F32, "ggb")
    DLC = sb([128, 4], F32, "dlc")
    KR = sb([128, 4, 2, 128], BF16, "kr")
    BT = sb([128, 4, 128], BF16, "bt")
    KT_ = sb([128, 4, 128], BF16, "kt")
    BH = sb([128, 4, 128], BF16, "bh")
    KH = sb([128, 4, 128], BF16, "kh")
    VBF = sb([128, 4, 128], BF16, "vbf")
    TOK = sb([128, 3, 512], BF16, "tok")
    YTOK = sb([128, 8, 64], F32, "ytok")
    TWB = sb([128, 128], BF16, "twb")
    SGB = sb([128, 128], BF16, "sgb")
    RMASK = sb([128, 128], F32, "rmask")
    memset("dve", RMASK, RMASK.ap, 1.0)
    memset("dve", RMASK, RMASK.ap[:, 0:1], 0.0)
    MATS = rot(2, [128, 5, 2, 128], BF16, "mats")
    LV = rot(3, [128, 2, 2, 128], BF16, "lv")
    PP = rot(3, [128, 2, 128], BF16, "pp")
    ZU = rot(2, [128, 2, 2, 64], BF16, "zu")

    def rwkv(n, s_in_proj_done=None):
        for j in range(14):
            t0 = tmpF.next()
            ts("dve", t0, t0.ap[:, 0:n], U, U.ap[:, j, 1:n + 1], pc(p_omm, j), None, ALU.mult, extra=[PRM])
            stt(UM, UM.ap[:, j, 0:n], U, U.ap[:, j, 0:n], pc(p_mu, j), t0, t0.ap[:, 0:n], ALU.mult, ALU.add,
                extra=[PRM])
            if j % 4 == 3:
                yield
        tw = TWB
        act(tw, tw.ap[0:64, 0:n], UM, UM.ap[0:64, 12, 0:n], AF.Tanh)
        cp("act", tw, tw.ap[64:128, 0:n], UM, UM.ap[64:128, 12, 0:n])
        sg = SGB
        act(sg, sg.ap[:, 0:n], UM, UM.ap[:, 13, 0:n], AF.Sigmoid)
        for c in range(4):
            rwc = RWC.next()
            ps = PS.next()
            mm(ps, ps.ap[:, 0:n], LR, LR.ap[0:64, c * 128:(c + 1) * 128], tw, tw.ap[0:64, 0:n])
            act(rwc, rwc.ap[:, SIGW, 0:n], ps, ps.ap[:, 0:n], AF.Sigmoid, bias=pc(p_w0, c), extra=[PRM])
            ps = PS.next()
            mm(ps, ps.ap[:, 0:n], LR, LR.ap[64:128, c * 128:(c + 1) * 128], tw, tw.ap[64:128, 0:n])
            act(rwc, rwc.ap[:, AA, 0:n], ps, ps.ap[:, 0:n], AF.Sigmoid, bias=pc(p_a0, c), extra=[PRM])
            ps = PS.next()
            mm(ps, ps.ap[:, 0:n], G2, G2.ap[:, c * 128:(c + 1) * 128], sg, sg.ap[:, 0:n])
            cp("act", GGB, GGB.ap[:, 0, c, 0:n], ps, ps.ap[:, 0:n])
            yield
            ts("dve", rwc, rwc.ap[:, KK, 0:n], UM, UM.ap[:, 4 + c, 0:n], pc(p_kk, c), None, ALU.mult, extra=[PRM])
            sq = tmpB.next()
            act(sq, sq.ap[:, 0:n], rwc, rwc.ap[:, KK, 0:n], AF.Square)
            ps = PS.next()
            mm(ps, ps.ap[:, 0:n], CB, BDB, sq, sq.ap[:, 0:n])
            r_ = tmpF.next()
            act(r_, r_.ap[:, 0:n], ps, ps.ap[:, 0:n], AF.Sqrt, bias=EPSC.ap[:, 1:2], extra=[EPSC])
            recip(r_, r_.ap[:, 0:n], r_, r_.ap[:, 0:n])
            tt("dve", rwc, rwc.ap[:, KKN, 0:n], rwc, rwc.ap[:, KK, 0:n], r_, r_.ap[:, 0:n], ALU.mult)
            t1 = tmpF.next()
            ts("dve", t1, t1.ap[:, 0:n], rwc, rwc.ap[:, AA, 0:n], pc(p_ka, c), pc(p_omka, c), ALU.mult, ALU.add,
               extra=[PRM])
            tt("dve", rwc, rwc.ap[:, KMOD, 0:n], UM, UM.ap[:, 4 + c, 0:n], t1, t1.ap[:, 0:n], ALU.mult)
            tt("dve", rwc, rwc.ap[:, BBv, 0:n], rwc, rwc.ap[:, KKN, 0:n], rwc, rwc.ap[:, AA, 0:n], ALU.mult)
            S.op("dve", lambda e, rwc=rwc: e.tensor_tensor_scan(rwc.ap[:, CS, 0:n], RMASK.ap[:, 0:n],
                                                               rwc.ap[:, SIGW, 0:n], 0.0, ALU.mult, ALU.add),
                 reads=[RMASK, rwc], writes=[rwc])
            act(rwc, rwc.ap[:, DD, 0:n], rwc, rwc.ap[:, CS, 0:n], AF.Exp, scale=-0.6065306597126334)
            cp("pool", DLC, DLC.ap[:, c:c + 1], rwc, rwc.ap[:, DD, n - 1:n])
            act(rwc, rwc.ap[:, DINV, 0:n], rwc, rwc.ap[:, CS, 0:n], AF.Exp, scale=0.6065306597126334)
            t2 = tmpF.next()
            tt("dve", t2, t2.ap[:, 0:n], rwc, rwc.ap[:, CS, 0:n], rwc, rwc.ap[:, SIGW, 0:n], ALU.subtract)
            act(rwc, rwc.ap[:, DPREV, 0:n], t2, t2.ap[:, 0:n], AF.Exp, scale=-0.6065306597126334)
            tt("dve", KR, KR.ap[:, c, 1, 0:n], UM, UM.ap[:, c, 0:n], rwc, rwc.ap[:, DD, 0:n], ALU.mult)
            tt("dve", KR, KR.ap[:, c, 0, 0:n], rwc, rwc.ap[:, KKN, 0:n], rwc, rwc.ap[:, DPREV, 0:n], ALU.mult)
            tt("dve", BT, BT.ap[:, c, 0:n], rwc, rwc.ap[:, BBv, 0:n], rwc, rwc.ap[:, DINV, 0:n], ALU.mult)
            tt("dve", KT_, KT_.ap[:, c, 0:n], rwc, rwc.ap[:, KMOD, 0:n], rwc, rwc.ap[:, DINV, 0:n], ALU.mult)
            ts("dve", BH, BH.ap[:, c, 0:n], BT, BT.ap[:, c, 0:n], rwc.ap[:, DD, n - 1:n], None, ALU.mult, extra=[rwc])
            ts("dve", KH, KH.ap[:, c, 0:n], KT_, KT_.ap[:, c, 0:n], rwc.ap[:, DD, n - 1:n], None, ALU.mult,
               extra=[rwc])
            cp("act", VBF, VBF.ap[:, c, 0:n], UM, UM.ap[:, 8 + c, 0:n])
            pr_ = tmpB.next()
            stt(pr_, pr_.ap[:, 0:n], UM, UM.ap[:, c, 0:n], pc(p_rk, c), rwc, rwc.ap[:, KMOD, 0:n], ALU.mult, ALU.mult,
                extra=[PRM])
            ps = PS.next()
            mm(ps, ps.ap[:, 0:n], CB, BDB, pr_, pr_.ap[:, 0:n])
            tt("dve", GGB, GGB.ap[:, 1, c, 0:n], ps, ps.ap[:, 0:n], UM, UM.ap[:, 8 + c, 0:n], ALU.mult)
            yield
        dbg("rwkv_a")
        for qi, src in enumerate((VBF, BH, KH)):
            ps = PS.next()
            pb_ = ps.ap.bitcast(BF16)
            for c in range(4):
                tr(ps, pb_[0:n, c * 128:(c + 1) * 128], src, src.ap[:, c, 0:n], CB, IDB)
            cp("act", TOK, TOK.ap[0:n, qi, :], ps, pb_[0:n, 0:512])
            yield
        dbg("rwkv_b")
        nlev = max(1, int(np.ceil(np.log2(n))))
        for c in range(4):
            mt = MATS.next()
            ps1 = PS.next()
            ps2 = PS.next()
            ps3 = PS.next()
            v1 = ps1.ap[0:n].rearrange("p (h r) -> p h r", h=2)[:, :, 0:2 * n].rearrange("p h (x t) -> p h x t", x=2)
            v2 = ps2.ap[0:n].rearrange("p (h r) -> p h r", h=2)[:, :, 0:2 * n].rearrange("p h (x t) -> p h x t", x=2)
            v3 = ps3.ap[0:n].rearrange("p (h r) -> p h r", h=2)[:, :, 0:n]
            for h2 in range(2):
                pb = 64 * h2
                krv = KR.ap[pb:pb + 64, c, :, 0:n]
                mm(ps1, v1[:, h2], BT, BT.ap[pb:pb + 64, c, 0:n], KR, krv)
                mm(ps2, v2[:, h2], KT_, KT_.ap[pb:pb + 64, c, 0:n], KR, krv)
                mm(ps3, v3[:, h2], KR, KR.ap[pb:pb + 64, c, 0, 0:n], BT, BT.ap[pb:pb + 64, c, 0:n])
            msu = M_SU[0:n, 0:n].unsqueeze(1).to_broadcast([n, 2, n])
            mu_ = M_U[0:n, 0:n].unsqueeze(1).to_broadcast([n, 2, n])
            msl = M_SL[0:n, 0:n].unsqueeze(1).to_broadcast([n, 2, n])
            m_ = mt.ap
            stt(mt, m_[0:n, 0, :, 0:n], ps1, v1[:, :, 0, :], -1.0, CONST, msu, ALU.mult, ALU.mult)
            tt("dve", mt, m_[0:n, 2, :, 0:n], ps1, v1[:, :, 1, :], CONST, mu_, ALU.mult)
            tt("dve", mt, m_[0:n, 3, :, 0:n], ps2, v2[:, :, 0, :], CONST, msu, ALU.mult)
            tt("dve", mt, m_[0:n, 4, :, 0:n], ps2, v2[:, :, 1, :], CONST, mu_, ALU.mult)
            stt(mt, m_[0:n, 1, :, 0:n], ps3, v3, -1.0, CONST, msl, ALU.mult, ALU.mult)
            dbg("rwkv_c")
            yield
            P = PP.next()
            idb = IDF[0:n, 0:n].unsqueeze(1).to_broadcast([n, 2, n])
            tt("dve", P, P.ap[0:n, :, 0:n], mt, m_[0:n, 0, :, 0:n], CONST, idb, ALU.add)
            Mt, Ma, Nt, Na = mt, m_[0:n, 0, :, 0:n], mt, m_[0:n, 1, :, 0:n]
            for lev in range(1, nlev):
                lv = LV.next()
                pA = PS.next()
                pB = PS.next()
                vA = pA.ap[0:n, 0:2 * n].rearrange("p (h t) -> p h t", h=2)
                vB = pB.ap[0:n, 0:2 * n].rearrange("p (h t) -> p h t", h=2)
                for h2 in range(2):
                    mm(pA, vA[:, h2], Nt, Na[:, h2], Mt, Ma[:, h2])
                    mm(pB, vB[:, h2], Mt, Ma[:, h2], Nt, Na[:, h2])
                cp("act", lv, lv.ap[0:n, 0, :, 0:n], pA, vA)
                cp("act", lv, lv.ap[0:n, 1, :, 0:n], pB, vB)
                Mt, Ma, Nt, Na = lv, lv.ap[0:n, 0, :, 0:n], lv, lv.ap[0:n, 1, :, 0:n]
                pC = PS.next()
                vC = pC.ap[0:n, 0:2 * n].rearrange("p (h t) -> p h t", h=2)
                for h2 in range(2):
                    mm(pC, vC[:, h2], Nt, Na[:, h2], P, P.ap[0:n, h2, 0:n])
                Pn = PP.next()
                tt("dve", Pn, Pn.ap[0:n, :, 0:n], pC, vC, P, P.ap[0:n, :, 0:n], ALU.add)
                P = Pn
                yield
            dbg("rwkv_d")
            zu = ZU.next()
            pz = PS.next()
            vz = pz.ap[0:n].rearrange("p (h r) -> p h r", h=2)[:, :, 0:64]
            for h2 in range(2):
                pb = 64 * h2
                h = 2 * c + h2
                mm(pz, vz[:, h2], KR, KR.ap[pb:pb + 64, c, 0, 0:n], HSB, HSB.ap[pb:pb + 64, c, :], start=True,
                   stop=False)
                mm(pz, vz[:, h2], mt, m_[0:n, 3, h2, 0:n], TOK, TOK.ap[0:n, 0, h * 64:(h + 1) * 64], start=False,
                   stop=True)
            S.op("act", lambda e, zu=zu, vz=vz: e.mul(zu.ap[0:n, 0], vz, -1.0), reads=[pz], writes=[zu])
            pu = PS.next()
            vu = pu.ap[0:n, 0:128].rearrange("p (h v) -> p h v", h=2)
            for h2 in range(2):
                mm(pu, vu[:, h2], P, P.ap[0:n, h2, 0:n], zu, zu.ap[0:n, 0, h2])
            cp("act", zu, zu.ap[0:n, 1], pu, vu)
            yield
            py = PS.next()
            vy = py.ap[0:n].rearrange("p (h r) -> p h r", h=2)[:, :, 0:64]
            for h2 in range(2):
                pb = 64 * h2
                h = 2 * c + h2
                mm(py, vy[:, h2], KR, KR.ap[pb:pb + 64, c, 1, 0:n], HSB, HSB.ap[pb:pb + 64, c, :], start=True,
                   stop=False)
                mm(py, vy[:, h2], mt, m_[0:n, 2, h2, 0:n], zu, zu.ap[0:n, 1, h2], start=False, stop=False)
                mm(py, vy[:, h2], mt, m_[0:n, 4, h2, 0:n], TOK, TOK.ap[0:n, 0, h * 64:(h + 1) * 64], start=False,
                   stop=True)
            cp("act", YTOK, YTOK.ap[0:n, 2 * c:2 * c + 2, :], py, vy)
            ph = PS.next()
            vh = ph.ap[:, 0:128].rearrange("p (h v) -> p h v", h=2)
            for h2 in range(2):
                h = 2 * c + h2
                mm(ph, vh[:, h2], TOK, TOK.ap[0:n, 1, c * 128:(c + 1) * 128], zu, zu.ap[0:n, 1, h2], start=True,
                   stop=False)
                mm(ph, vh[:, h2], TOK, TOK.ap[0:n, 2, c * 128:(c + 1) * 128], TOK,
                   TOK.ap[0:n, 0, h * 64:(h + 1) * 64], start=False, stop=True)
            for h2 in range(2):
                pb = 64 * h2
                stt(HS, HS.ap[pb:pb + 64, c, :], HS, HS.ap[pb:pb + 64, c, :], DLC.ap[pb:pb + 64, c:c + 1], ph,
                    vh[pb:pb + 64, h2], ALU.mult, ALU.add, extra=[DLC])
            yield
        cp("act", HSB, HSB.ap, HS, HS.ap)
        dbg("rwkv_e")
        st_ = tmpF.next()
        sq = tmpF.next()
        S.op("dve", lambda e: e.tensor_reduce(st_.ap[0:n, 0:8], YTOK.ap[0:n], AX.X, ALU.add), reads=[YTOK],
             writes=[st_])
        sqv = sq.ap[0:n, 0:512].rearrange("p (h v) -> p h v", h=8)
        act(sq, sqv, YTOK, YTOK.ap[0:n], AF.Square)
        S.op("dve", lambda e: e.tensor_reduce(st_.ap[0:n, 8:16], sqv, AX.X, ALU.add), reads=[sq], writes=[st_])
        ts("dve", st_, st_.ap[0:n, 0:16], st_, st_.ap[0:n, 0:16], 1.0 / 64, None, ALU.mult)
        tt("dve", st_, st_.ap[0:n, 16:24], st_, st_.ap[0:n, 0:8], st_, st_.ap[0:n, 0:8], ALU.mult)
        tt("dve", st_, st_.ap[0:n, 16:24], st_, st_.ap[0:n, 8:16], st_, st_.ap[0:n, 16:24], ALU.subtract)
        act(st_, st_.ap[0:n, 16:24], st_, st_.ap[0:n, 16:24], AF.Sqrt, bias=EPSC.ap[0:n, 2:3], extra=[EPSC])
        recip(st_, st_.ap[0:n, 16:24], st_, st_.ap[0:n, 16:24])
        yn = tmpF.next()
        ynv = yn.ap[0:n, 0:512].rearrange("p (h v) -> p h v", h=8)
        tt("dve", yn, ynv, YTOK, YTOK.ap[0:n], st_, st_.ap[0:n, 0:8].unsqueeze(2).to_broadcast([n, 8, 64]),
           ALU.subtract)
        tt("dve", yn, ynv, yn, ynv, st_, st_.ap[0:n, 16:24].unsqueeze(2).to_broadcast([n, 8, 64]), ALU.mult)
        ps = PS.next()
        for c in range(4):
            tr(ps, ps.ap[:, c * 128:c * 128 + n], yn, yn.ap[0:n, c * 128:(c + 1) * 128], CONST, IDF[0:n, 0:n])
        t3 = tmpF.next()
        t3v = t3.ap[:, 0:4 * n].rearrange("p (c t) -> p c t", c=4)
        for c in range(4):
            ts("dve", t3, t3v[:, c, :], ps, ps.ap[:, c * 128:c * 128 + n], pc(p_lnw, c), pc(p_lnb, c), ALU.mult,
               ALU.add, extra=[PRM])
        tt("dve", t3, t3v, t3, t3v, GGB, GGB.ap[:, 1, :, 0:n], ALU.add)
        tt("dve", YCAT, YCAT.ap[:, 0:4, 0:n], t3, t3v, GGB, GGB.ap[:, 0, :, 0:n], ALU.mult)

    KBLK = Rot([T(RRK.ap[:, 0:512]), T(RRK.ap[:, 512:1024])])
    VBLK = Rot([T(RRK.ap[:, 1024:1536]), T(RRK.ap[:, 1536:2048])])
    for t_ in KBLK.ts + VBLK.ts:
        S.op("pool", lambda e, t_=t_: e.memset(t_.ap[:, 0:1], 0.0), writes=[RRK, t_])
    KBB = rot(2, [128, 512], BF16, "kbb")
    VP = rot(2, [128, 8, 65], BF16, "vp")
    for t_ in VP.ts:
        memset("dve", t_, t_.ap[:, :, 64:65], 1.0)
    KTB = rot(2, [128, 4, 128], BF16, "ktb")
    QN = sb([128, 4, 128], BF16, "qn")
    KNF = sb([128, 4, 128], F32, "knf")
    KNB = sb([128, 4, 128], BF16, "knb")
    BIAS = rot(3, [128, 8], F32, "bias")
    ACC = sb([128, 8], F32, "lfacc")
    ET = rot(4, [128, 128], BF16, "et")
    MUB = sb([128, 128], BF16, "mub")
    cp("dve", MUB, MUB.ap, CONST, M_U)
    LFO = rot(2, [128, 8], F32, "lfo")
    RRF = RRI.ap.bitcast(F32)
    KOUT = Rot([T(RRF[:, 0:512])])
    XBD = T(RRF[:, 512:640])
    WKT = T(RRF[:, 640:896].rearrange("p (c j) -> p c j", c=4))
    for t_ in (KOUT.ts[0], XBD, WKT):
        S.op("pool", lambda e, t_=t_: e.memset(t_.ap, 0.0), writes=[RRI, t_])

    def attend(n, nk, kt_t, kt_ap, vp, bi, diag, first, last):
        for h in range(8):
            c, pb = h // 2, 64 * (h % 2)
            ps = PS.next()
            mm(ps, ps.ap[0:nk, 0:n], kt_t, kt_ap[pb:pb + 64, c, 0:nk], QN, QN.ap[pb:pb + 64, c, 0:n])
            et = ET.next()
            act(et, et.ap[0:nk, 0:n], ps, ps.ap[0:nk, 0:n], AF.Exp, bias=bi.ap[0:nk, h:h + 1], extra=[bi])
            if diag:
                tt("pool", et, et.ap[0:nk, 0:n], et, et.ap[0:nk, 0:n], MUB, MUB.ap[0:nk, 0:n], ALU.mult)
            acc = PSACC[h // 4]
            av = acc.ap[0:n, 0:260].rearrange("p (h d) -> p h d", h=4)
            st_flag = first and (h % 4 == 0)
            S.op("pe", lambda e, av=av, et=et, vp=vp, h=h, st_flag=st_flag: e.matmul(
                av[:, h % 4, :], et.ap[0:nk, 0:n], vp.ap[0:nk, h, :], start=st_flag, stop=last,
                skip_group_check=True), reads=[et, vp], writes=[acc])
            if h % 2 == 1:
                yield

    def fox(n, blk, kv_src, out_k, out_v, out_lf, row0):
        for half in range(2):
            sq = tmpB.next()
            sqv = sq.ap[:, 0:4 * n].rearrange("p (j t) -> p j t", j=4)
            act(sq, sqv, QKR, QKR.ap[:, 4 * half:4 * half + 4, 0:n], AF.Square)
            ps = PS.next()
            psv = ps.ap[:, 0:512].rearrange("p (j t) -> p j t", j=4)[:, :, 0:n]
            for j in range(4):
                mm(ps, ps.ap[:, j * 128:j * 128 + n], CB, BDB, sq, sqv[:, j, :])
            r_ = tmpF.next()
            rv = r_.ap[:, 0:4 * n].rearrange("p (j t) -> p j t", j=4)
            act(r_, rv, ps, psv, AF.Sqrt, scale=1.0 / 64, bias=EPSC.ap[:, 0:1], extra=[EPSC])
            recip(r_, rv, r_, rv)
            if half == 0:
                stt(QN, QN.ap[:, :, 0:n], QKR, QKR.ap[:, 0:4, 0:n], pc(p_gq, 0), r_, rv, ALU.mult, ALU.mult,
                    extra=[PRM])
            else:
                stt(KNF, KNF.ap[:, :, 0:n], QKR, QKR.ap[:, 4:8, 0:n], pc(p_gk, 0), r_, rv, ALU.mult, ALU.mult,
                    extra=[PRM])
            yield
        cp("act", KNB, KNB.ap[:, :, 0:n], KNF, KNF.ap[:, :, 0:n])
        ps = PS.next()
        for c in range(4):
            tr(ps, ps.ap[0:n, c * 128:(c + 1) * 128], KNF, KNF.ap[:, c, 0:n], CONST, IDF)
        ko = KOUT.next()
        cp("act", ko, ko.ap[0:n], ps, ps.ap[0:n, 0:512])
        S.dma("pool", out_k.ap[row0:row0 + n], ko.ap[0:n], reads=[ko], writes=[out_k])
        vcur = VB.ts[(VB.i - 1) % len(VB.ts)]
        S.dma("pool", out_v.ap[row0:row0 + n], vcur.ap[0:n], reads=[vcur], writes=[out_v])
        yield
        lf = tmpF.next()
        tt("dve", lf, lf.ap[0:n, 0:8], LFT, LFT.ap[0:n, :], FB, FB.ap[0:n, :], ALU.add)
        act(lf, lf.ap[0:n, 0:8], lf, lf.ap[0:n, 0:8], AF.Exp, scale=-1.0)
        act(lf, lf.ap[0:n, 0:8], lf, lf.ap[0:n, 0:8], AF.Ln, bias=EPSC.ap[0:n, 3:4], extra=[EPSC])
        lo = LFO.next()
        ts("dve", lo, lo.ap[0:n], lf, lf.ap[0:n, 0:8], -1.0, None, ALU.mult)
        S.dma("pool", out_lf.ap[row0:row0 + n], lo.ap[0:n], reads=[lo], writes=[out_lf])
        ps = PS.next()
        mm(ps, ps.ap[0:n, 0:8], CONST, M_SL[0:n, 0:n], lo, lo.ap[0:n])
        mm(ps, ps.ap[:, 8:16], CONST, ONESF[0:n, :], lo, lo.ap[0:n])
        cp("dve", SUF, SUF.ap[0:n, blk, :], ps, ps.ap[0:n, 0:8])
        cp("dve", TOT, TOT.ap[:, blk, :], ps, ps.ap[:, 8:16])
        cp("dve", ACC, ACC.ap, ps, ps.ap[:, 8:16])
        yield
        vp = VP.next()
        cp("dve", vp, vp.ap[0:n, :, 0:64], vcur, vcur.ap[0:n].rearrange("p (h d) -> p h d", h=8))
        bi = BIAS.next()
        cp("dve", bi, bi.ap[0:n], SUF, SUF.ap[0:n, blk, :])
        yield from attend(n, n, KNB, KNB.ap, vp, bi, True, True, blk == 0)
        for b in range(blk - 1, -1, -1):
            k_t, k_ap, v_t, v_ap = kv_src(b)
            kb = KBLK.next()
            vb = VBLK.next()
            S.dma("sp", kb.ap, k_ap, reads=[k_t], writes=[kb])
            S.dma("sp", vb.ap, v_ap, reads=[v_t], writes=[vb])
            kbb = KBB.next()
            cp("dve", kbb, kbb.ap, kb, kb.ap)
            vp = VP.next()
            cp("dve", vp, vp.ap[:, :, 0:64], vb, vb.ap.rearrange("p (h d) -> p h d", h=8))
            ps = PS.next()
            pb_ = ps.ap.bitcast(BF16)
            for c in range(4):
                tr(ps, pb_[:, c * 128:(c + 1) * 128], kbb, kbb.ap[:, c * 128:(c + 1) * 128], CB, IDB)
            ktb = KTB.next()
            cp("act", ktb, ktb.ap.rearrange("p c t -> p (c t)"), ps, pb_[:, 0:512])
            bi = BIAS.next()
            tt("dve", bi, bi.ap, SUF, SUF.ap[:, b, :], ACC, ACC.ap, ALU.add)
            tt("dve", ACC, ACC.ap, ACC, ACC.ap, TOT, TOT.ap[:, b, :], ALU.add)
            yield
            yield from attend(n, 128, ktb, ktb.ap, vp, bi, False, False, b == 0)
        yb = tmpF.next()
        ybv = yb.ap[0:n, 0:512].rearrange("p (h d) -> p h d", h=8)
        for half in range(2):
            acc = PSACC[half]
            av = acc.ap[0:n, 0:260].rearrange("p (h d) -> p h d", h=4)
            rc = tmpF.next()
            recip(rc, rc.ap[0:n, 0:4], acc, av[:, :, 64])
            tt("dve", yb, ybv[:, 4 * half:4 * half + 4, :], acc, av[:, :, 0:64], rc,
               rc.ap[0:n, 0:4].unsqueeze(2).to_broadcast([n, 4, 64]), ALU.mult)
        ps = PS.next()
        for c in range(4):
            tr(ps, ps.ap[:, c * 128:c * 128 + n], yb, yb.ap[0:n, c * 128:(c + 1) * 128], CONST, IDF[0:n, 0:n])
        cp("act", YCAT, YCAT.ap[:, 4:8, 0:n], ps, ps.ap[:, 0:512].rearrange("p (c t) -> p c t", c=4)[:, :, 0:n])

    def in_proj(n):
        rmsnorm(n, g_mix[0])
        sls = {}
        for q in range(7):
            sl = load_piece("in", q)
            sls[q] = sl
            wv = sl.ap.rearrange("p (k c) -> p k c", k=8)
            for loc in range(4):
                tile_ = 4 * q + loc
                if tile_ >= 22:
                    continue
                ps = PS.next()
                for k in range(8):
                    mm(ps, ps.ap[:, 0:n], sl, wv[:, k, loc * 128:(loc + 1) * 128], HT, HT.ap[:, k, 0:n],
                       start=(k == 0), stop=(k == 7))
                if tile_ < 14:
                    cp("act", U, U.ap[:, tile_, 1:n + 1], ps, ps.ap[:, 0:n])
                else:
                    cp("act", QKR, QKR.ap[:, tile_ - 14, 0:n], ps, ps.ap[:, 0:n])
        w5 = sls[5].ap.rearrange("p (k c) -> p k c", k=8)
        w6 = sls[6].ap.rearrange("p (k c) -> p k c", k=8)
        ps = PS.next()
        for k in range(8):
            mm(ps, ps.ap[0:n, 0:256], HT, HT.ap[:, k, 0:n], sls[5], w5[:, k, 256:512], start=(k == 0), stop=(k == 7))
        for k in range(8):
            mm(ps, ps.ap[0:n, 256:512], HT, HT.ap[:, k, 0:n], sls[6], w6[:, k, 0:256], start=(k == 0), stop=(k == 7))
        vb = VB.next()
        cp("act", vb, vb.ap[0:n], ps, ps.ap[0:n, 0:512])
        ps = PS.next()
        for k in range(8):
            mm(ps, ps.ap[0:n, 0:8], HT, HT.ap[:, k, 0:n], sls[6], w6[:, k, 256:264], start=(k == 0), stop=(k == 7))
        cp("act", LFT, LFT.ap[0:n], ps, ps.ap[0:n, 0:8])

    def out_proj(n):
        for q in range(2):
            sl = load_piece("out", q)
            wv = sl.ap.rearrange("p (k c) -> p k c", k=8)
            for loc in range(4):
                j = 4 * q + loc
                ps = PS.next()
                for k in range(8):
                    mm(ps, ps.ap[:, 0:n], sl, wv[:, k, loc * 128:(loc + 1) * 128], YCAT, YCAT.ap[:, k, 0:n],
                       start=(k == 0), stop=(k == 7))
                tt("dve", XT, XT.ap[:, j, 0:n], XT, XT.ap[:, j, 0:n], ps, ps.ap[:, 0:n], ALU.add)

    def load_x(n, pieces):
        xin = XIN.next()
        for (r0, nr, st_, sap) in pieces:
            S.dma("sp", xin.ap[r0:r0 + nr], sap, reads=[st_], writes=[xin])
        for half in range(2):
            ps = PS.next()
            for kk_ in range(4):
                k = 4 * half + kk_
                tr(ps, ps.ap[:, kk_ * 128:kk_ * 128 + n], xin, xin.ap[0:n, k * 128:(k + 1) * 128], CONST,
                   IDF[0:n, 0:n])
            cp("act", XT, XT.ap[:, 4 * half:4 * half + 4, 0:n],
               ps, ps.ap[:, 0:512].rearrange("p (c t) -> p c t", c=4)[:, :, 0:n])

    def store_y(n, dst_t, dst_ap, skip):
        xo = XIN.next()
        for half in range(2):
            ps = PS.next()
            for kk_ in range(4):
                k = 4 * half + kk_
                tr(ps, ps.ap[0:n, kk_ * 128:(kk_ + 1) * 128], XT, XT.ap[:, k, 0:n], CONST, IDF)
            cp("act", xo, xo.ap[0:n, 512 * half:512 * half + 512], ps, ps.ap[0:n, 0:512])
        S.dma("pool", dst_ap, xo.ap[skip:n], reads=[xo], writes=[dst_t])


    def bd_transpose(dst_t, dst_ap3, src_t, src_ap3):
        for c in range(4):
            for h2 in range(2):
                pb = 64 * h2
                cp("dve", XBD, XBD.ap[pb:pb + 64, pb:pb + 64], src_t, src_ap3[pb:pb + 64, c, :])
            ps = PS.next()
            tr(ps, ps.ap[:, 0:128], XBD, XBD.ap, CONST, IDF)
            for h2 in range(2):
                pb = 64 * h2
                cp("dve", dst_t, dst_ap3[pb:pb + 64, c, :], ps, ps.ap[pb:pb + 64, pb:pb + 64])

    def run_stream(g, s, L, sample):
        if not sample:
            memset("dve", U, U.ap[:, :, 0:1], 0.0)
            memset("dve", HS, HS.ap, 0.0)
            memset("dve", CAR, CAR.ap, 0.0)
            for l in range(2):
                memset("dve", CC[l], CC[l].ap, 0.0)
            nb_past = 0
        else:
            S.dma("sp", U.ap[:, :, 0], I["state_a_shift"].ap[s].rearrange("(j p) -> p j", p=128),
                  reads=[I["state_a_shift"]], writes=[U], allow_slow_non_contiguous=True)
            S.dma("sp", WKT.ap, I["state_a_wkv"].ap[s].rearrange("(c h2) v j -> (h2 v) c j", h2=2),
                  reads=[I["state_a_wkv"]], writes=[WKT])
            bd_transpose(HS, HS.ap, WKT, WKT.ap)
            t0 = tmpF.next()
            S.dma("sp", t0.ap[:, 0:32], I["state_c_re"].ap[s].rearrange("(m gg) p -> (gg p) m", gg=2),
                  reads=[I["state_c_re"]], writes=[t0], allow_slow_non_contiguous=True)
            S.dma("sp", t0.ap[:, 32:64], I["state_c_im"].ap[s].rearrange("(m gg) p -> (gg p) m", gg=2),
                  reads=[I["state_c_im"]], writes=[t0], allow_slow_non_contiguous=True)
            tt("dve", t0, t0.ap[:, 64:96], t0, t0.ap[:, 0:32], S5P, s5[:, FIR, :], ALU.mult)
            tt("dve", t0, t0.ap[:, 96:128], t0, t0.ap[:, 32:64], S5P, s5[:, FII, :], ALU.mult)
            tt("dve", CAR, CAR.ap[:, 0, :], t0, t0.ap[:, 64:96], t0, t0.ap[:, 96:128], ALU.subtract)
            tt("dve", t0, t0.ap[:, 64:96], t0, t0.ap[:, 0:32], S5P, s5[:, FII, :], ALU.mult)
            tt("dve", t0, t0.ap[:, 96:128], t0, t0.ap[:, 32:64], S5P, s5[:, FIR, :], ALU.mult)
            tt("dve", CAR, CAR.ap[:, 1, :], t0, t0.ap[:, 64:96], t0, t0.ap[:, 96:128], ALU.add)
            for l in range(2):
                for r_ in range(2):
                    S.dma("sp", CC[l].ap[:, :, r_], I["state_ffn_conv"].ap[l, s, r_].rearrange("(j p) -> p j", p=128),
                          reads=[I["state_ffn_conv"]], writes=[CC[l]], allow_slow_non_contiguous=True)
            nb_past = PAST // 128
            lfp = tmpF.next()
            lfv = lfp.ap[:, 0:nb_past * 8].rearrange("p (b h) -> p b h", h=8)
            S.dma("sp", lfv, I["cache_b_logf"].ap[s].rearrange("(b p) h -> p b h", p=128),
                  reads=[I["cache_b_logf"]], writes=[lfp])
            for b0 in range(0, nb_past, 32):
                nb = min(32, nb_past - b0)
                ps = PS.next()
                ps2 = PS.next()
                mm(ps, ps.ap[:, 0:nb * 8], CONST, M_SL, lfp, lfp.ap[:, b0 * 8:(b0 + nb) * 8])
                mm(ps2, ps2.ap[:, 0:nb * 8], CONST, ONESF, lfp, lfp.ap[:, b0 * 8:(b0 + nb) * 8])
                cp("dve", SUF, SUF.ap[:, b0:b0 + nb, :], ps, ps.ap[:, 0:nb * 8].rearrange("p (b h) -> p b h", h=8))
                cp("dve", TOT, TOT.ap[:, b0:b0 + nb, :], ps2, ps2.ap[:, 0:nb * 8].rearrange("p (b h) -> p b h", h=8))
        cp("act", HSB, HSB.ap, HS, HS.ap)
        npass = (L + 127) // 128
        n_last = L - 128 * (npass - 1)
        for p in range(npass):
            n = min(128, L - 128 * p)
            t0_ = 128 * p
            if sample:
                load_x(n, [(0, n, I["x_sample"], I["x_sample"].ap[s, t0_:t0_ + n])])
            elif p == 0:
                xin = XIN.next()
                S.dma("sp", xin.ap[0:16], I["meta"].ap, reads=[I["meta"]], writes=[xin])
                if n > 16:
                    S.dma("sp", xin.ap[16:n], I["x_prompt"].ap[s, 0:n - 16], reads=[I["x_prompt"]], writes=[xin])
                for half in range(2):
                    ps = PS.next()
                    for kk_ in range(4):
                        k = 4 * half + kk_
                        tr(ps, ps.ap[:, kk_ * 128:kk_ * 128 + n], xin, xin.ap[0:n, k * 128:(k + 1) * 128], CONST,
                           IDF[0:n, 0:n])
                    cp("act", XT, XT.ap[:, 4 * half:4 * half + 4, 0:n],
                       ps, ps.ap[:, 0:512].rearrange("p (c t) -> p c t", c=4)[:, :, 0:n])
            else:
                load_x(n, [(0, n, I["x_prompt"], I["x_prompt"].ap[s, t0_ - 16:t0_ - 16 + n])])
            if _os.environ.get("KCNT") == "1" and p == 0 and s == 0:
                def _snap(tag, _last=[None]):
                    cur = dict(S.cnt)
                    nw = {e: sum(1 for it in S.streams[e] if it[0] == "w") for e in S.streams}
                    if _last[0] is not None:
                        print("CNT", tag, {e: cur[e] - _last[0][0][e] for e in cur}, "waits", {e: nw[e] - _last[0][1][e] for e in nw})
                    _last[0] = (cur, nw)
                build._snap = _snap
            else:
                build._snap = lambda tag: None
            build._snap("start")
            in_proj(n)
            build._snap("in_proj")
            dbg("inproj")
            blk = nb_past + p
            if sample:
                def kv_src(b):
                    return (I["cache_b_k"], I["cache_b_k"].ap[s, 128 * b:128 * b + 128],
                            I["cache_b_v"], I["cache_b_v"].ap[s, 128 * b:128 * b + 128])
            else:
                def kv_src(b):
                    return (O[g + "_b_k"], O[g + "_b_k"].ap[s, 128 * b:128 * b + 128],
                            O[g + "_b_v"], O[g + "_b_v"].ap[s, 128 * b:128 * b + 128])
            gens = [rwkv(n), fox(n, blk, kv_src, T_sub(O[g + "_b_k"], s), T_sub(O[g + "_b_v"], s),
                                 T_sub(O[g + "_b_logf"], s), t0_)]
            if _os.environ.get("KSEQ", "1") == "1":
                for g_ in gens:
                    for _ in g_:
                        pass
            else:
                while gens:
                    for g_ in list(gens):
                        try:
                            next(g_)
                        except StopIteration:
                            gens.remove(g_)
            cp("act", U, U.ap[:, :, 0:1], U, U.ap[:, :, n:n + 1])
            build._snap("rwkv+fox")
            dbg("fox")
            out_proj(n)
            build._snap("out_proj")
            dbg("outproj")
            ffn(n, 0)
            build._snap("ffn0")
            dbg("ffn0")
            s5_layer(n)
            build._snap("s5")
            dbg("s5")
            ffn(n, 1)
            build._snap("ffn1")
            if sample:
                store_y(n, O[g + "_y"], O[g + "_y"].ap[s, t0_:t0_ + n], 0)
            elif p == 0:
                if n > 16:
                    store_y(n, O[g + "_y"], O[g + "_y"].ap[s, 0:n - 16], 16)
            else:
                store_y(n, O[g + "_y"], O[g + "_y"].ap[s, t0_ - 16:t0_ - 16 + n], 0)
        S.dma("sp", O[g + "_a_shift"].ap[s].rearrange("(j p) -> p j", p=128), U.ap[:, :, 0], reads=[U],
              writes=[O[g + "_a_shift"]], allow_slow_non_contiguous=True)
        bd_transpose(WKT, WKT.ap, HS, HS.ap)
        S.dma("sp", O[g + "_a_wkv"].ap[s].rearrange("(c h2) v j -> (h2 v) c j", h2=2), WKT.ap, reads=[WKT],
              writes=[O[g + "_a_wkv"]])
        t0 = tmpF.next()
        tt("dve", t0, t0.ap[:, 64:96], CAR, CAR.ap[:, 0, :], S5P, s5[:, FRE, :], ALU.mult)
        tt("dve", t0, t0.ap[:, 96:128], CAR, CAR.ap[:, 1, :], S5P, s5[:, FIM, :], ALU.mult)
        tt("dve", t0, t0.ap[:, 0:32], t0, t0.ap[:, 64:96], t0, t0.ap[:, 96:128], ALU.subtract)
        tt("dve", t0, t0.ap[:, 64:96], CAR, CAR.ap[:, 0, :], S5P, s5[:, FIM, :], ALU.mult)
        tt("dve", t0, t0.ap[:, 96:128], CAR, CAR.ap[:, 1, :], S5P, s5[:, FRE, :], ALU.mult)
        tt("dve", t0, t0.ap[:, 32:64], t0, t0.ap[:, 64:96], t0, t0.ap[:, 96:128], ALU.add)
        S.dma("sp", O[g + "_c_re"].ap[s].rearrange("(m gg) p -> (gg p) m", gg=2), t0.ap[:, 0:32], reads=[t0],
              writes=[O[g + "_c_re"]], allow_slow_non_contiguous=True)
        S.dma("sp", O[g + "_c_im"].ap[s].rearrange("(m gg) p -> (gg p) m", gg=2), t0.ap[:, 32:64], reads=[t0],
              writes=[O[g + "_c_im"]], allow_slow_non_contiguous=True)
        for l in range(2):
            for r_ in range(2):
                S.dma("sp", O[g + "_ffn_conv"].ap[l, s, r_].rearrange("(j p) -> p j", p=128), CC[l].ap[:, :, r_],
                      reads=[CC[l]], writes=[O[g + "_ffn_conv"]], allow_slow_non_contiguous=True)

    class StopBuild(Exception):
        pass

    def dbg(stage):
        import os
        if os.environ.get("KDBG") != stage:
            return
        base = dict(XT=(XT, XT.ap), U=(U, U.ap), QKR=(QKR, QKR.ap))
        if stage in ("rwkv", "fox"):
            base.update(UM=(UM, UM.ap), GGB=(GGB, GGB.ap), YTOK=(YTOK, YTOK.ap), HS=(HS, HS.ap),
                        YCAT=(YCAT, YCAT.ap[:, 0:4] if stage == "rwkv" else YCAT.ap), KR=(KR, KR.ap), BT=(BT, BT.ap),
                        KT=(KT_, KT_.ap), BH=(BH, BH.ap), KH=(KH, KH.ap), TOK=(TOK, TOK.ap), DLC=(DLC, DLC.ap))
        if stage == "fox":
            base.update(QN=(QN, QN.ap), KNF=(KNF, KNF.ap))
        if stage == "s5":
            base.update(CAR=(CAR, CAR.ap), TCOS=(TCOS, TCOS.ap), TSIN=(TSIN, TSIN.ap), S5P=(S5P, S5P.ap[:, 0:10]),
                        YCAT=(YCAT, YCAT.ap))
        dumps = base
        for nm, (t, tap) in dumps.items():
            shp = list(tap.shape)
            d = T(nc.dram_tensor("dbg_" + nm, shp, tap.dtype, kind="ExternalOutput").ap())
            S.dma("sp", d.ap, tap, reads=[t], writes=[d])
        raise StopBuild()

    class T_sub:
        def __init__(self, parent, s):
            self.p = parent
            self.ap = parent.ap[s]

        w = property(lambda self: self.p.w, lambda self, v: setattr(self.p, "w", v))
        r = property(lambda self: self.p.r, lambda self, v: setattr(self.p, "r", v))

    try:
        for s in range(NSP):
            run_stream("p", s, LP, False)
        for s in range(NSS):
            run_stream("s", s, LS, True)
    except StopBuild:
        pass
    print('SBUF bytes remaining', nc.sbuf_bytes_remaining, 'ops', S.nops, S.cnt)
    S.emit()
    return nc, S


_CACHE = {}

WNAMES = ["meta", "norm_mix", "norm_ffn", "w_in_e", "a_mu", "a_w0", "a_w2", "a_a0", "a_a2", "a_g2", "a_kk", "a_ka",
          "a_rk", "a_ln_w", "a_ln_b", "b_fbias", "b_qnorm", "b_knorm", "w_out_e", "c_lam_re", "c_lam_im", "c_log_dt",
          "c_b_re", "c_b_im", "c_c_re", "c_c_im", "c_d", "w_glu", "w_up", "conv_w", "conv_b", "w_down"]
SQUEEZE = {"w_in_e", "a_mu", "a_w0", "a_w2", "a_a0", "a_a2", "a_g2", "a_kk", "a_ka", "a_rk", "a_ln_w", "a_ln_b",
           "b_fbias", "b_qnorm", "b_knorm", "w_out_e", "c_lam_re", "c_lam_im", "c_log_dt", "c_b_re", "c_b_im",
           "c_c_re", "c_c_im", "c_d", "w_glu"}


def kernel(**inp):
    ncores = inp.pop("_ncores", NCORES)
    B = inp["x_prompt"].shape[0]
    SEQ = inp["x_prompt"].shape[1]
    BS = inp["x_sample"].shape[0]
    LS = inp["x_sample"].shape[1]
    PAST = inp["cache_b_k"].shape[2]
    NSP, NSS = B // ncores, BS // ncores
    LP = SEQ + 16
    key = (NSP, LP, NSS, LS, PAST)
    if key not in _CACHE:
        _CACHE[key] = build(*key)
    nc, S = _CACHE[key]
    f = lambda a: np.ascontiguousarray(np.asarray(a, dtype=np.float32))
    shared = {}
    for k in WNAMES:
        a = f(inp[k])
        if k in SQUEEZE:
            a = a[0]
        if k in ("a_rk", "c_d"):
            a = a.reshape(-1)
        shared[k] = f(a)
    in_maps = []
    for c in range(ncores):
        sp = slice(c * NSP, (c + 1) * NSP)
        ss = slice(c * NSS, (c + 1) * NSS)
        m = dict(shared)
        m["x_prompt"] = f(inp["x_prompt"][sp])
        m["x_sample"] = f(inp["x_sample"][ss])
        m["state_a_shift"] = f(inp["state_a_shift"][0, ss])
        m["state_a_wkv"] = f(inp["state_a_wkv"][0, ss])
        m["cache_b_k"] = f(np.asarray(inp["cache_b_k"])[0, ss].reshape(NSS, PAST, 512))
        m["cache_b_v"] = f(np.asarray(inp["cache_b_v"])[0, ss].reshape(NSS, PAST, 512))
        m["cache_b_logf"] = f(inp["cache_b_logf"][0, ss])
        m["state_c_re"] = f(inp["state_c_re"][0, ss])
        m["state_c_im"] = f(inp["state_c_im"][0, ss])
        m["state_ffn_conv"] = f(np.asarray(inp["state_ffn_conv"])[:, ss])
        in_maps.append(m)
    res = run_bass_kernel_spmd(nc, in_maps, core_ids=list(range(ncores)))
    R = res.results
    cat = lambda name, ax=0: np.concatenate([np.asarray(r[name]) for r in R], axis=ax)
    outs = []
    outs.append(cat("p_y"))
    outs.append(cat("s_y"))
    for g, L in (("p", LP), ("s", LS)):
        n = (NSP if g == "p" else NSS) * ncores
        outs.append(cat(g + "_a_shift")[None])
        outs.append(cat(g + "_a_wkv")[None])
        outs.append(cat(g + "_b_k").reshape(1, n, L, 8, 64))
        outs.append(cat(g + "_b_v").reshape(1, n, L, 8, 64))
        outs.append(cat(g + "_b_logf")[None])
        outs.append(cat(g + "_c_re")[None])
        outs.append(cat(g + "_c_im")[None])
        outs.append(cat(g + "_ffn_conv", ax=1))
    return tuple(np.ascontiguousarray(o.astype(np.float32)) for o in outs)
```
